# Optimizing a Trainium2 kernel written in Bass

```python
import jax
import jax.numpy as jnp
from jax import lax
import numpy as np

D_MODEL = 1024
BATCH = 32
SEQ = 256
DEPTH = 4
DEC_BATCH = 2
DEC_SEQ = 2048
PAST_LEN = 256

GRID_W = 64
N_MIXERS = 3
N_MLA = (DEPTH + 2) // 3
N_HGRN = (DEPTH + 1) // 3
N_SWA = DEPTH // 3

MLA_HEADS = 8
MLA_Q_LORA = 512
MLA_KV_LORA = 256
MLA_NOPE_DIM = 128
MLA_ROPE_DIM = 64
MLA_V_DIM = 128
MLA_SCALE = (MLA_NOPE_DIM + MLA_ROPE_DIM) ** -0.5

HG_HEADS = 8
HG_DK = D_MODEL // HG_HEADS
HG_DV = D_MODEL // HG_HEADS
HG_CHUNK = 32

SWA_HEADS = 16
SWA_KV_HEADS = 4
SWA_GROUP = SWA_HEADS // SWA_KV_HEADS
SWA_HEAD_DIM = 64
SWA_WINDOW = 128
SWA_BLOCK = 128
SWA_SCALE = SWA_HEAD_DIM ** -0.5

D_FF = -(-8 * D_MODEL // (3 * 256)) * 256
Q_BLOCK = 128
ROPE_BASE = 10000.0
NORM_EPS = 1e-6
NEG_INF = -1e30

kernel_name = 'hybrid_dit_mla_hgrn2_swa_step'

F32 = jnp.float32


def _rmsnorm(x, g):
    xf = x.astype(F32)
    y = xf * lax.rsqrt(jnp.mean(xf * xf, axis=-1, keepdims=True) + NORM_EPS)
    return (y * g.astype(F32)).astype(x.dtype)


def _adaln(cond, w, b):
    m = jax.nn.silu(cond) @ w + b
    return [t[:, None, :] for t in jnp.split(m, 6, axis=-1)]


def _modulate(x, g, shift, scale):
    return _rmsnorm(x, g) * (1.0 + scale) + shift


def _axial_rope(x):
    S, R = x.shape[1], x.shape[-1]
    rows = S // GRID_W
    row = jnp.repeat(jnp.arange(rows), GRID_W).astype(F32)
    col = jnp.tile(jnp.arange(GRID_W), rows).astype(F32)
    n_freq = R // 4
    inv_freq = ROPE_BASE ** (-jnp.arange(n_freq, dtype=F32) / n_freq)
    bshape = (S,) + (1,) * (x.ndim - 3) + (n_freq,)

    def rot(xa, pos):
        ang = (pos[:, None] * inv_freq[None, :]).reshape(bshape)
        cos, sin = jnp.cos(ang), jnp.sin(ang)
        x1, x2 = xa[..., :n_freq].astype(F32), xa[..., n_freq:].astype(F32)
        return jnp.concatenate([x1 * cos - x2 * sin, x1 * sin + x2 * cos], axis=-1)

    half = R // 2
    out = jnp.concatenate([rot(x[..., :half], row), rot(x[..., half:], col)], axis=-1)
    return out.astype(x.dtype)


def _dense_attention(q, k, v, scale, sink=None):
    B, Sq, Hkv, G, Dk = q.shape
    nb = Sq // Q_BLOCK
    qb = jnp.moveaxis(q.reshape(B, nb, Q_BLOCK, Hkv, G, Dk), 1, 0)

    def block(qblk):
        s = jnp.einsum('bqhgd,bkhd->bhgqk', qblk, k).astype(F32) * scale
        if sink is not None:
            z = jnp.broadcast_to(sink.astype(F32)[None, :, :, None, None], s.shape[:-1] + (1,))
            p = jax.nn.softmax(jnp.concatenate([s, z], axis=-1), axis=-1)[..., :-1]
        else:
            p = jax.nn.softmax(s, axis=-1)
        return jnp.einsum('bhgqk,bkhd->bqhgd', p.astype(v.dtype), v)

    o = lax.map(block, qb)
    return jnp.moveaxis(o, 0, 1).reshape(B, Sq, Hkv, G, v.shape[-1])


def _band_attention_with_ctx(q, k, v, k_ctx, v_ctx, sink, scale):
    B, S, Hkv, G, D = q.shape
    nb = S // SWA_BLOCK
    qb = q.reshape(B, nb, SWA_BLOCK, Hkv, G, D)

    def band(t):
        tp = jnp.pad(t, ((0, 0), (SWA_BLOCK, SWA_BLOCK), (0, 0), (0, 0)))
        tp = tp.reshape(B, nb + 2, SWA_BLOCK, Hkv, D)
        return jnp.concatenate([tp[:, :-2], tp[:, 1:-1], tp[:, 2:]], axis=2)

    kb, vb = band(k), band(v)
    s_loc = jnp.einsum('bnqhgd,bnkhd->bnhgqk', qb, kb).astype(F32) * scale
    blk = jnp.arange(nb)[:, None, None]
    qpos = blk * SWA_BLOCK + jnp.arange(SWA_BLOCK)[None, :, None]
    kpos = (blk - 1) * SWA_BLOCK + jnp.arange(3 * SWA_BLOCK)[None, None, :]
    valid = (jnp.abs(qpos - kpos) <= SWA_WINDOW) & (kpos >= 0) & (kpos < S)
    s_loc = jnp.where(valid[None, :, None, None], s_loc, NEG_INF)
    s_ctx = jnp.einsum('bnqhgd,blhd->bnhgql', qb, k_ctx).astype(F32) * scale
    z = jnp.broadcast_to(sink.astype(F32)[None, None, :, :, None, None], s_ctx.shape[:-1] + (1,))
    p = jax.nn.softmax(jnp.concatenate([s_ctx, s_loc, z], axis=-1), axis=-1)
    L = k_ctx.shape[1]
    p_ctx = p[..., :L].astype(v.dtype)
    p_loc = p[..., L:L + 3 * SWA_BLOCK].astype(v.dtype)
    o = (jnp.einsum('bnhgql,blhd->bnqhgd', p_ctx, v_ctx)
         + jnp.einsum('bnhgqk,bnkhd->bnqhgd', p_loc, vb))
    return o.reshape(B, S, Hkv, G, D)


def _mla_q(xn, w_dq, q_norm, w_uq):
    B, S, _ = xn.shape
    q = (_rmsnorm(xn @ w_dq, q_norm) @ w_uq).reshape(B, S, MLA_HEADS, MLA_NOPE_DIM + MLA_ROPE_DIM)
    return q[..., :MLA_NOPE_DIM], q[..., MLA_NOPE_DIM:]


def _mla_compress(xn, w_dkv, kv_norm):
    kv = xn @ w_dkv
    return _rmsnorm(kv[..., :MLA_KV_LORA], kv_norm), kv[..., MLA_KV_LORA:]


def _mla_expand(c_kv, k_rope, w_uk, w_uv):
    B, S, _ = c_kv.shape
    k_nope = (c_kv @ w_uk).reshape(B, S, MLA_HEADS, MLA_NOPE_DIM)
    k_r = jnp.broadcast_to(k_rope[:, :, None, :], (B, S, MLA_HEADS, MLA_ROPE_DIM)).astype(k_nope.dtype)
    v = (c_kv @ w_uv).reshape(B, S, MLA_HEADS, MLA_V_DIM)
    return jnp.concatenate([k_nope, k_r], axis=-1), v


def _mla_context(xn, w_dq, q_norm, w_uq, w_dkv, kv_norm, w_uk, w_uv, w_o):
    B, S, _ = xn.shape
    q_nope, q_rope = _mla_q(xn, w_dq, q_norm, w_uq)
    c_kv, k_rope = _mla_compress(xn, w_dkv, kv_norm)
    k, v = _mla_expand(c_kv, k_rope, w_uk, w_uv)
    q = jnp.concatenate([q_nope, q_rope], axis=-1)[:, :, :, None, :]
    o = _dense_attention(q, k, v, MLA_SCALE)
    return o.reshape(B, S, MLA_HEADS * MLA_V_DIM) @ w_o, c_kv, k_rope


def _mla_latent(xn, ckv_ctx, krope_ctx, w_dq, q_norm, w_uq, w_dkv, kv_norm, w_uk, w_uv, w_o):
    B, S, _ = xn.shape
    q_nope, q_rope = _mla_q(xn, w_dq, q_norm, w_uq)
    q = jnp.concatenate([q_nope, _axial_rope(q_rope)], axis=-1)[:, :, :, None, :]
    c_kv, k_rope = _mla_compress(xn, w_dkv, kv_norm)
    k_lat, v_lat = _mla_expand(c_kv, _axial_rope(k_rope), w_uk, w_uv)
    k_ctx, v_ctx = _mla_expand(ckv_ctx.astype(xn.dtype), krope_ctx.astype(xn.dtype), w_uk, w_uv)
    k = jnp.concatenate([k_ctx, k_lat], axis=1)
    v = jnp.concatenate([v_ctx, v_lat], axis=1)
    o = _dense_attention(q, k, v, MLA_SCALE)
    return o.reshape(B, S, MLA_HEADS * MLA_V_DIM) @ w_o


def _hgrn_lower_bound(lb_logits, layer):
    s = jax.nn.softmax(lb_logits.astype(F32), axis=0)
    return jnp.cumsum(s, axis=0)[layer] - s[0]


def _gla_chunk_scan(q, k, v, logf, s0):
    B, H, S, K = q.shape
    V = v.shape[-1]
    n = S // HG_CHUNK

    def to_chunks(a):
        return a.reshape(B, H, n, HG_CHUNK, a.shape[-1]).transpose(2, 0, 1, 3, 4)

    tri = jnp.tril(jnp.ones((HG_CHUNK, HG_CHUNK), dtype=bool))

    def step(state, inp):
        qc, kc, vc, fc = inp
        b = jnp.cumsum(fc, axis=2)
        o_inter = jnp.einsum('bhtk,bhkv->bhtv', qc * jnp.exp(b), state)
        diff = b[:, :, :, None, :] - b[:, :, None, :, :]
        decay = jnp.exp(jnp.where(tri[None, None, :, :, None], diff, -jnp.inf))
        att = jnp.einsum('bhtk,bhsk,bhtsk->bhts', qc, kc, decay)
        o = o_inter + jnp.einsum('bhts,bhsv->bhtv', att, vc)
        b_last = b[:, :, -1:, :]
        k_dec = kc * jnp.exp(b_last - b)
        new_state = jnp.exp(b_last[:, :, 0, :])[..., None] * state + jnp.einsum('bhsk,bhsv->bhkv', k_dec, vc)
        return new_state, o

    s_fin, o = lax.scan(step, s0.astype(F32), (to_chunks(q), to_chunks(k), to_chunks(v), to_chunks(logf)))
    return o.transpose(1, 2, 0, 3, 4).reshape(B, H, S, V), s_fin


def _hgrn_mix(xn, s0_fwd, s0_bwd, lb_fwd, lb_bwd, w_q, w_f, w_i, w_g, o_norm, w_o):
    B, S, _ = xn.shape

    def heads(t, d):
        return t.reshape(B, S, HG_HEADS, d).transpose(0, 2, 1, 3).astype(F32)

    q = heads(jax.nn.silu(xn @ w_q), HG_DK)
    v = heads(xn @ w_i, HG_DV)
    f_fwd = lb_fwd + (1.0 - lb_fwd) * jax.nn.sigmoid((xn @ w_f[0]).astype(F32))
    f_bwd = lb_bwd + (1.0 - lb_bwd) * jax.nn.sigmoid((xn @ w_f[1]).astype(F32))
    flip = lambda t: jnp.flip(t, axis=2)
    o_f, s_f = _gla_chunk_scan(q, heads(1.0 - f_fwd, HG_DK), v, heads(jnp.log(f_fwd), HG_DK), s0_fwd)
    o_b, s_b = _gla_chunk_scan(flip(q), flip(heads(1.0 - f_bwd, HG_DK)), flip(v),
                               flip(heads(jnp.log(f_bwd), HG_DK)), s0_bwd)
    o = (o_f + flip(o_b)).transpose(0, 2, 1, 3)
    g = jax.nn.silu((xn @ w_g).reshape(B, S, HG_HEADS, HG_DV).astype(F32))
    o = (_rmsnorm(o, o_norm) * g).reshape(B, S, HG_HEADS * HG_DV).astype(xn.dtype)
    return o @ w_o, s_f, s_b


def _swa_qkv(xn, w_q, w_k, w_v):
    B, S, _ = xn.shape
    q = (xn @ w_q).reshape(B, S, SWA_KV_HEADS, SWA_GROUP, SWA_HEAD_DIM)
    k = (xn @ w_k).reshape(B, S, SWA_KV_HEADS, SWA_HEAD_DIM)
    v = (xn @ w_v).reshape(B, S, SWA_KV_HEADS, SWA_HEAD_DIM)
    return q, k, v


def _swa_context(xn, sink, w_q, w_k, w_v, w_o):
    B, S, _ = xn.shape
    q, k, v = _swa_qkv(xn, w_q, w_k, w_v)
    o = _dense_attention(q, k, v, SWA_SCALE, sink)
    return o.reshape(B, S, SWA_HEADS * SWA_HEAD_DIM) @ w_o, k, v


def _swa_latent(xn, k_ctx, v_ctx, sink, w_q, w_k, w_v, w_o):
    B, S, _ = xn.shape
    q, k, v = _swa_qkv(xn, w_q, w_k, w_v)
    o = _band_attention_with_ctx(_axial_rope(q), _axial_rope(k), v, k_ctx.astype(xn.dtype),
                                 v_ctx.astype(xn.dtype), sink, SWA_SCALE)
    return o.reshape(B, S, SWA_HEADS * SWA_HEAD_DIM) @ w_o


def _swiglu(h, w_gate, w_up, w_down):
    return (jax.nn.silu(h @ w_gate) * (h @ w_up)) @ w_down


def setup_inputs(seed: int = 0) -> dict:
    key = jax.random.key(seed)
    ks = iter(jax.random.split(key, 64))
    D = D_MODEL

    def nrm(shape, scale=1.0):
        return jax.random.normal(next(ks), shape, F32) * scale

    def w(shape, fan_in, scale=1.0):
        return nrm(shape, scale * fan_in ** -0.5)

    def gain(shape):
        return 1.0 + nrm(shape, 0.02)

    return {
        'x_prompt': nrm((BATCH, SEQ, D)),
        'x_sample': nrm((DEC_BATCH, DEC_SEQ, D)),
        'cache_mla_ckv': nrm((DEC_BATCH, N_MLA, PAST_LEN, MLA_KV_LORA)),
        'cache_mla_krope': nrm((DEC_BATCH, N_MLA, PAST_LEN, MLA_ROPE_DIM)),
        'state_hgrn': nrm((DEC_BATCH, N_HGRN, 2, HG_HEADS, HG_DK, HG_DV), 0.5),
        'cache_swa_k': nrm((DEC_BATCH, N_SWA, PAST_LEN, SWA_KV_HEADS, SWA_HEAD_DIM)),
        'cache_swa_v': nrm((DEC_BATCH, N_SWA, PAST_LEN, SWA_KV_HEADS, SWA_HEAD_DIM)),
        'c': nrm((DEC_BATCH, D)),
        'c_ctx': nrm((D,)),
        'ada_w': w((DEPTH, D, 6 * D), D, 0.5),
        'ada_b': nrm((DEPTH, 6 * D), 0.02),
        'norm_mix': gain((DEPTH, D)),
        'norm_ffn': gain((DEPTH, D)),
        'ffn_w_gate': w((DEPTH, D, D_FF), D),
        'ffn_w_up': w((DEPTH, D, D_FF), D),
        'ffn_w_down': w((DEPTH, D_FF, D), D_FF),
        'final_norm': gain((D,)),
        'mla_w_dq': w((N_MLA, D, MLA_Q_LORA), D),
        'mla_q_norm': gain((N_MLA, MLA_Q_LORA)),
        'mla_w_uq': w((N_MLA, MLA_Q_LORA, MLA_HEADS * (MLA_NOPE_DIM + MLA_ROPE_DIM)), MLA_Q_LORA),
        'mla_w_dkv': w((N_MLA, D, MLA_KV_LORA + MLA_ROPE_DIM), D),
        'mla_kv_norm': gain((N_MLA, MLA_KV_LORA)),
        'mla_w_uk': w((N_MLA, MLA_KV_LORA, MLA_HEADS * MLA_NOPE_DIM), MLA_KV_LORA),
        'mla_w_uv': w((N_MLA, MLA_KV_LORA, MLA_HEADS * MLA_V_DIM), MLA_KV_LORA),
        'mla_w_o': w((N_MLA, MLA_HEADS * MLA_V_DIM, D), MLA_HEADS * MLA_V_DIM),
        'hg_w_q': w((N_HGRN, D, HG_HEADS * HG_DK), D),
        'hg_w_f': w((N_HGRN, 2, D, HG_HEADS * HG_DK), D),
        'hg_w_i': w((N_HGRN, D, HG_HEADS * HG_DV), D),
        'hg_w_g': w((N_HGRN, D, HG_HEADS * HG_DV), D),
        'hg_o_norm': gain((N_HGRN, HG_DV)),
        'hg_w_o': w((N_HGRN, HG_HEADS * HG_DV, D), HG_HEADS * HG_DV),
        'hg_lb_logits': nrm((2, DEPTH, HG_HEADS * HG_DK), 0.5),
        'swa_w_q': w((N_SWA, D, SWA_HEADS * SWA_HEAD_DIM), D),
        'swa_w_k': w((N_SWA, D, SWA_KV_HEADS * SWA_HEAD_DIM), D),
        'swa_w_v': w((N_SWA, D, SWA_KV_HEADS * SWA_HEAD_DIM), D),
        'swa_w_o': w((N_SWA, SWA_HEADS * SWA_HEAD_DIM, D), SWA_HEADS * SWA_HEAD_DIM),
        'swa_sink': nrm((N_SWA, SWA_HEADS), 0.5),
    }


def reference(x_prompt, x_sample, cache_mla_ckv, cache_mla_krope, state_hgrn, cache_swa_k, cache_swa_v,
              c, c_ctx, ada_w, ada_b, norm_mix, norm_ffn, ffn_w_gate, ffn_w_up, ffn_w_down, final_norm,
              mla_w_dq, mla_q_norm, mla_w_uq, mla_w_dkv, mla_kv_norm, mla_w_uk, mla_w_uv, mla_w_o,
              hg_w_q, hg_w_f, hg_w_i, hg_w_g, hg_o_norm, hg_w_o, hg_lb_logits,
              swa_w_q, swa_w_k, swa_w_v, swa_w_o, swa_sink):
    x_p, x_s = x_prompt, x_sample
    n_prompt = x_p.shape[0]
    new_ckv, new_krope, new_hg, new_k, new_v = [], [], [], [], []
    for i in range(DEPTH):
        kind, j = i % N_MIXERS, i // N_MIXERS
        sh1p, sc1p, g1p, sh2p, sc2p, g2p = _adaln(c_ctx[None, :], ada_w[i], ada_b[i])
        sh1s, sc1s, g1s, sh2s, sc2s, g2s = _adaln(c, ada_w[i], ada_b[i])
        hp = _modulate(x_p, norm_mix[i], sh1p, sc1p)
        hs = _modulate(x_s, norm_mix[i], sh1s, sc1s)
        if kind == 0:
            p = (mla_w_dq[j], mla_q_norm[j], mla_w_uq[j], mla_w_dkv[j], mla_kv_norm[j],
                 mla_w_uk[j], mla_w_uv[j], mla_w_o[j])
            yp, ckv, krope = _mla_context(hp, *p)
            ys = _mla_latent(hs, cache_mla_ckv[:, j], cache_mla_krope[:, j], *p)
            new_ckv.append(ckv)
            new_krope.append(krope)
        elif kind == 1:
            lb_f = _hgrn_lower_bound(hg_lb_logits[0], i)
            lb_b = _hgrn_lower_bound(hg_lb_logits[1], i)
            p = (hg_w_q[j], hg_w_f[j], hg_w_i[j], hg_w_g[j], hg_o_norm[j], hg_w_o[j])
            zeros = jnp.zeros((n_prompt, HG_HEADS, HG_DK, HG_DV), F32)
            yp, s_f, s_b = _hgrn_mix(hp, zeros, zeros, lb_f, lb_b, *p)
            ys, _, _ = _hgrn_mix(hs, state_hgrn[:, j, 0], state_hgrn[:, j, 1], lb_f, lb_b, *p)
            new_hg.append(jnp.stack([s_f, s_b], axis=1))
        else:
            sink = swa_sink[j].reshape(SWA_KV_HEADS, SWA_GROUP)
            p = (swa_w_q[j], swa_w_k[j], swa_w_v[j], swa_w_o[j])
            yp, k_c, v_c = _swa_context(hp, sink, *p)
            ys = _swa_latent(hs, cache_swa_k[:, j], cache_swa_v[:, j], sink, *p)
            new_k.append(k_c)
            new_v.append(v_c)
        x_p = x_p + g1p * yp
        x_s = x_s + g1s * ys
        x_p = x_p + g2p * _swiglu(_modulate(x_p, norm_ffn[i], sh2p, sc2p), ffn_w_gate[i], ffn_w_up[i], ffn_w_down[i])
        x_s = x_s + g2s * _swiglu(_modulate(x_s, norm_ffn[i], sh2s, sc2s), ffn_w_gate[i], ffn_w_up[i], ffn_w_down[i])
    y_prompt = _rmsnorm(x_p, final_norm)
    y_sample = _rmsnorm(x_s, final_norm)
    return (y_prompt, y_sample, jnp.stack(new_ckv, axis=1), jnp.stack(new_krope, axis=1),
            jnp.stack(new_hg, axis=1), jnp.stack(new_k, axis=1), jnp.stack(new_v, axis=1))
```

```python
import numpy as np
from contextlib import ExitStack
import concourse.bass as bass
import concourse.mybir as mybir
from concourse.bass_utils import run_bass_kernel_spmd

F32 = mybir.dt.float32
BF16 = mybir.dt.bfloat16
AF = mybir.ActivationFunctionType
ALU = mybir.AluOpType
AX = mybir.AxisListType

D = 1024
DFF = 2816
DEPTH = 4
NP_TOK = 1024
NS_TOK = 512
T = NP_TOK + NS_TOK
NT = T // 512
EPS = 1e-6
NCORES = 8


class Tok:
    __slots__ = ("sem", "val", "eng")

    def __init__(self, sem, val, eng):
        self.sem, self.val, self.eng = sem, val, eng


class _Rec:
    def __init__(self):
        self.call = None

    def __getattr__(self, name):
        def f(*a, **k):
            assert self.call is None
            self.call = (name, a, k)
            return None
        return f


def _freeze(fn):
    r = _Rec()
    fn(r)
    name, a, k = r.call
    return lambda e: getattr(e, name)(*a, **k)


class Prog:
    ENG = ("pe", "act", "dve", "pool", "sp")
    EPOCH = 6000

    def __init__(self, nc, stack):
        self.nc = nc
        self.stack = stack
        self.ops = {e: [] for e in self.ENG}
        self.cur_sem = {}
        self.cur_cnt = {}
        for e in self.ENG:
            self._new_epoch(e)
        self.known = {e: {} for e in self.ENG}
        self.last_w = {}
        self.readers = {}
        self.dma_sem = {}
        self.dma_cnt = {}
        self.nsem = 0
        self.final_toks = []

    def _sem(self, name):
        self.nsem = getattr(self, "nsem", 0) + 1
        return self.stack.enter_context(self.nc.semaphore(name))

    def _new_epoch(self, e):
        self._ep = getattr(self, "_ep", 0) + 1
        self.cur_sem[e] = self._sem(f"c_{e}_{self._ep}")
        self.cur_cnt[e] = 0

    def _waits_for(self, eng, reads, writes):
        toks = []
        for k in reads:
            t = self.last_w.get(k)
            if t is not None:
                toks.append(t)
        for k in writes:
            t = self.last_w.get(k)
            if t is not None:
                toks.append(t)
            for t in self.readers.get(k, {}).values():
                toks.append(t)
        need = {}
        for t in toks:
            if t.eng == eng:
                if eng == "pe":
                    continue
                if t.sem is self.cur_sem[eng] and t.val < self.cur_cnt[eng] - 1:
                    continue
                if t.sem is not self.cur_sem[eng]:
                    continue
            kn = self.known[eng].get(id(t.sem), 0)
            if kn >= t.val:
                continue
            if need.get(id(t.sem), (None, 0))[1] < t.val:
                need[id(t.sem)] = (t.sem, t.val)
        out = []
        for sid, (s, v) in need.items():
            self.known[eng][sid] = v
            out.append((s, v))
        return out

    def _record(self, tok, reads, writes, rkey):
        for k in writes:
            self.last_w[k] = tok
            self.readers[k] = {}
        for k in reads:
            self.readers.setdefault(k, {})[rkey] = tok

    def op(self, eng, fn, reads=(), writes=()):
        if self.cur_cnt[eng] >= self.EPOCH:
            self._new_epoch(eng)
        waits = self._waits_for(eng, reads, writes)
        self.cur_cnt[eng] += 1
        tok = Tok(self.cur_sem[eng], self.cur_cnt[eng], eng)
        self.ops[eng].append((waits, _freeze(fn), (tok.sem, 1)))
        self._record(tok, reads, writes, eng)
        return tok

    def dma(self, q, fn, chan, reads=(), writes=(), final=False):
        if chan not in self.dma_sem:
            self.dma_sem[chan] = self._sem("d_" + str(len(self.dma_sem)))
            self.dma_cnt[chan] = 0
        waits = self._waits_for(q, reads, writes)
        self.dma_cnt[chan] += 16
        tok = Tok(self.dma_sem[chan], self.dma_cnt[chan], "dma")
        self.ops[q].append((waits, _freeze(fn), (tok.sem, 16)))
        self._record(tok, reads, writes, ("dma", chan))
        if final:
            self.final_toks.append(tok)
        return tok

    def coll(self, fn, chan, reads=(), writes=()):
        if chan not in self.dma_sem:
            self.dma_sem[chan] = self._sem("cc_" + str(len(self.dma_sem)))
            self.dma_cnt[chan] = 0
        waits = self._waits_for("pool", reads, writes)
        self.dma_cnt[chan] += 1
        tok = Tok(self.dma_sem[chan], self.dma_cnt[chan], "dma")
        self.ops["pool"].append((waits, _freeze(fn), (tok.sem, None)))
        self._record(tok, reads, writes, ("dma", chan))
        return tok

    def barrier(self, chans=()):
        toks = [Tok(self.cur_sem[e], self.cur_cnt[e], e) for e in self.ENG if self.cur_cnt[e] > 0]
        for c in chans:
            if c in self.dma_sem:
                toks.append(Tok(self.dma_sem[c], self.dma_cnt[c], "dma"))
        for e in self.ENG:
            waits = []
            for t in toks:
                if t.eng == e:
                    continue
                if self.known[e].get(id(t.sem), 0) >= t.val:
                    continue
                self.known[e][id(t.sem)] = t.val
                waits.append((t.sem, t.val))
            if waits:
                self.ops[e].append((waits, None, None))

    def emit(self):
        nc = self.nc
        fin = {}
        for t in self.final_toks:
            if fin.get(id(t.sem), (None, 0))[1] < t.val:
                fin[id(t.sem)] = (t.sem, t.val)
        self.ops["sp"].append((list(fin.values()), None, None))
        with nc.Block() as block:
            def run(eng_obj, lst):
                for waits, fn, inc in lst:
                    for s, v in waits:
                        eng_obj.wait_ge(s, v)
                    if fn is not None:
                        ins = fn(eng_obj)
                        if inc is not None:
                            if inc[1] is None:
                                ins.then_inc(inc[0])
                            else:
                                ins.then_inc(inc[0], inc[1])

            @block.tensor
            def _(e):
                run(e, self.ops["pe"])

            @block.scalar
            def _(e):
                run(e, self.ops["act"])

            @block.vector
            def _(e):
                run(e, self.ops["dve"])

            @block.gpsimd
            def _(e):
                run(e, self.ops["pool"])

            @block.sync
            def _(e):
                run(e, self.ops["sp"])


W_SPECS = {
    "ada_w": [DEPTH, D, 6 * D], "ffn_w_gate": [DEPTH, D, DFF], "ffn_w_up": [DEPTH, D, DFF],
    "ffn_w_down": [DEPTH, DFF, D],
    "mla_w_dq": [2, D, 512], "mla_w_uq": [2, 512, 1536], "mla_w_dkv": [2, D, 320],
    "mla_w_uk": [2, 256, 1024], "mla_w_uv": [2, 256, 1024], "mla_w_o": [2, 1024, D],
    "swa_w_q": [1, D, D], "swa_w_k": [1, D, 256], "swa_w_v": [1, D, 256], "swa_w_o": [1, D, D],
    "hg_w_q": [1, D, D], "hg_w_f": [1, 2, D, D], "hg_w_i": [1, D, D], "hg_w_g": [1, D, D], "hg_w_o": [1, D, D],
}
NCONST = 2048
AR = 40960


class K:
    IDENT = 0
    ONESN = 128
    MASKF = 256
    MASKB = 384
    CVEC = 512
    ADAB = 528
    NMIX = 720
    NFFN = 752
    NFIN = 784
    QNORM = 792
    KVNORM = 800
    PERM = 804
    COS = 868
    SIN = 1380
    LBL = 1892
    ONORM = 1956
    RMASK = 1957
    SINK = 1973
    SEL = 1989
    END = 1997


def build(flags):
    nc = bass.Bass("TRN2", target_bir_lowering=False)
    stack = ExitStack()
    depth = flags.get("depth", DEPTH)
    mixers = flags.get("mixers", 1)
    with stack:
        P = Prog(nc, stack)
        dr = {}
        dr["xin"] = nc.dram_tensor("xin", [T, D], F32, kind="ExternalInput").ap()
        dr["consts"] = nc.dram_tensor("consts", [128, NCONST], F32, kind="ExternalInput").ap()
        dr["ckv_ctxT"] = nc.dram_tensor("ckv_ctxT", [2, 128, 2, 256], F32, kind="ExternalInput").ap()
        dr["krope_ctxT"] = nc.dram_tensor("krope_ctxT", [2, 64, 256], F32, kind="ExternalInput").ap()
        for k, shp in W_SPECS.items():
            dr[k] = nc.dram_tensor(k, shp, F32, kind="ExternalInput").ap()
        dr["y"] = nc.dram_tensor("y", [T, D], F32, kind="ExternalOutput").ap()
        dr["o_ckv"] = nc.dram_tensor("o_ckv", [4, 2, 256, 256], F32, kind="ExternalOutput").ap()
        dr["o_krope"] = nc.dram_tensor("o_krope", [4, 2, 256, 64], F32, kind="ExternalOutput").ap()
        dr["swa_kctxT"] = nc.dram_tensor("swa_kctxT", [64, 4, 256], F32, kind="ExternalInput").ap()
        dr["swa_vctx"] = nc.dram_tensor("swa_vctx", [256, 256], F32, kind="ExternalInput").ap()
        dr["bandmask"] = nc.dram_tensor("bandmask", [128, 1024], F32, kind="ExternalInput").ap()
        dr["o_k"] = nc.dram_tensor("o_k", [4, 1, 256, 256], F32, kind="ExternalOutput").ap()
        dr["o_v"] = nc.dram_tensor("o_v", [4, 1, 256, 256], F32, kind="ExternalOutput").ap()
        swin = nc.dram_tensor("swin", [128, 1536], BF16)
        swout = nc.dram_tensor("swout", [4 * 128, 1536], BF16)
        dr["hg_s0"] = nc.dram_tensor("hg_s0", [2, 8, 128, 128], F32, kind="ExternalInput").ap()
        dr["o_hg"] = nc.dram_tensor("o_hg", [4, 1, 2, 8, 128, 128], F32, kind="ExternalOutput").ap()
        hgin = [nc.dram_tensor(f"hgin{d_}", [128, 1032], F32) for d_ in range(2)]
        hgout = [nc.dram_tensor(f"hgout{d_}", [4 * 128, 1032], F32) for d_ in range(2)]
        agin = [nc.dram_tensor(f"agin{j}", [128, 1536], BF16) for j in range(2)]
        agout = [nc.dram_tensor(f"agout{j}", [4 * 128, 1536], BF16) for j in range(2)]

        def sb(name, shape, dt):
            return stack.enter_context(nc.sbuf_tensor(name, shape, dt))

        xT = sb("xT", [128, 8, T], F32)
        hT = sb("hT", [128, 8, T], BF16)
        cst = sb("cst", [128, NCONST], F32)
        cstb = sb("cstb", [128, 512], BF16)
        NSLOT = 4
        wring = sb("wring", [128, NSLOT, 4096], BF16)
        mod = sb("mod", [128, 2, 48], F32)
        gm = sb("gm", [128, 2, 2, 8], F32)
        silc = sb("silc", [128, 16], BF16)
        rstd = sb("rstd", [128, 512], F32)
        stage = sb("stage", [128, 2, 1024], F32)
        hgs = sb("hgs", [128, 128], F32)
        hgs2 = sb("hgs2", [128, 32], F32)
        lbw = sb("lbw", [128, 144], F32)
        ones64 = sb("ones64", [128, 64], F32)
        A = sb("arena", [128, AR], BF16)
        ps = stack.enter_context(nc.psum_tensor("ps", [128, 8, 512], F32))

        def carve(off, shape, dt=BF16):
            n = 1
            for s_ in shape[1:]:
                n *= s_
            if dt == F32:
                v = A[:, off:off + 2 * n].bitcast(F32)
            else:
                v = A[:, off:off + n]
            if len(shape) == 3:
                v = v.rearrange("p (a b) -> p a b", a=shape[1])
            elif len(shape) == 4:
                v = v.rearrange("p (a b c) -> p a b c", a=shape[1], b=shape[2])
            return v

        sq_default = carve(0, [128, 8, 512])
        ytmp = carve(4096, [128, 8, 512], F32)
        hid = carve(12288, [128, 2, 8, T])
        sgt = carve(36864, [128, 2, 512], F32)

        ident = cst[:, K.IDENT:K.IDENT + 128]
        identb = cstb[:, 0:128]
        onesb = cstb[:, 128:256]

        state = {"bank": 0, "slot": 0, "ev": 0, "banks": list(range(8))}

        def bank():
            bl = state["banks"]
            b = bl[state["bank"] % len(bl)]
            state["bank"] += 1
            return b

        def evac(dst, src, reads, writes, eng=None):
            if eng is None:
                eng = "act" if state["ev"] % 2 == 0 else "dve"
                state["ev"] += 1
            if eng == "act":
                return P.op("act", lambda e: e.copy(dst, src), reads=reads, writes=writes)
            return P.op("dve", lambda e: e.tensor_copy(dst, src), reads=reads, writes=writes)

        P.dma("sp", lambda e: e.dma_start(out=cst[:], in_=dr["consts"]), "cst", writes=["cst"])
        P.dma("pool", lambda e: e.dma_start(out=cstb[:], in_=dr["consts"][:, 0:512]), "cstb", writes=["cstb"])

        def wload(src, a, b):
            s = state["slot"]
            state["slot"] = (s + 1) % NSLOT
            assert a * b <= 4096
            dst = wring[:, s, 0:a * b].rearrange("p (a b) -> p a b", a=a)
            P.dma("pool", lambda e: e.dma_start(out=dst, in_=src), ("w", s), writes=[("w", s)])
            return dst, ("w", s)

        def kmajor(w2d):
            return w2d.rearrange("(kc p) n -> p kc n", p=128)

        for tt in range(T // 128):
            sl = tt % 2
            P.dma("sp", lambda e, tt=tt, sl=sl: e.dma_start(out=stage[:, sl, :], in_=dr["xin"][tt * 128:(tt + 1) * 128, :]),
                  ("stage", sl), writes=[("stage", sl)])
            for half in range(2):
                b = bank()
                for j in range(4):
                    c = half * 4 + j
                    P.op("pe", lambda e, b=b, j=j, c=c, sl=sl: e.transpose(ps[:, b, j * 128:(j + 1) * 128], stage[:, sl, c * 128:(c + 1) * 128], ident),
                         reads=[("stage", sl), "cst"], writes=[("ps", b)])
                evac(xT[:, half * 4:half * 4 + 4, tt * 128:(tt + 1) * 128], ps[:, b, :].rearrange("p (j n) -> p j n", j=4),
                     [("ps", b)], [("x", tt // 4)])

        def cond_of(t):
            return 0 if t < 2 else 1

        ADA_BANK = 7

        def adaln_piece(layer, pc):
            if layer == 0 and pc == 0:
                P.op("act", lambda e: e.activation(silc[:], cst[:, K.CVEC:K.CVEC + 16], AF.Silu), reads=["cst"], writes=["silc"])
            b = ADA_BANK
            wv = kmajor(dr["ada_w"][layer])
            wt, wk = wload(wv[:, :, pc * 512:(pc + 1) * 512], 8, 512)
            for jc in range(4):
                j = pc * 4 + jc
                for kc in range(8):
                    P.op("pe", lambda e, wt=wt, jc=jc, kc=kc, j=j, b=b: e.matmul(ps[:, b, 2 * j:2 * j + 2], wt[:, kc, jc * 128:(jc + 1) * 128], silc[:, 2 * kc:2 * kc + 2], start=(kc == 0), stop=(kc == 7)),
                         reads=[wk, "silc"], writes=[("ps", b)])

        def adaln_finish(layer):
            b = ADA_BANK
            for c in range(2):
                P.op("dve", lambda e, c=c, b=b: e.tensor_tensor(mod[:, c, :], ps[:, b, 0:96].rearrange("p (j c) -> p j c", c=2)[:, :, c], cst[:, K.ADAB + layer * 48:K.ADAB + (layer + 1) * 48], ALU.add),
                     reads=[("ps", b), "cst"], writes=["mod"])
            for n, (goff, so) in enumerate(((K.NMIX, 8), (K.NFFN, 32))):
                for c in range(2):
                    P.op("dve", lambda e, n=n, c=c, goff=goff, so=so: e.scalar_tensor_tensor(gm[:, n, c, :], mod[:, c, so:so + 8], 1.0, cst[:, goff + layer * 8:goff + layer * 8 + 8], ALU.add, ALU.mult),
                         reads=["mod", "cst"], writes=["gm"])

        def rms_rstd(src_fn, nch, srckeys, mscale, sq=None):
            if sq is None:
                sq = sq_default
            P.op("act", lambda e: e.activation(sq[:, 0:nch, :], src_fn(), AF.Square), reads=srckeys, writes=["sq"])
            b = bank()
            for c in range(nch):
                P.op("pe", lambda e, c=c, b=b: e.matmul(ps[:, b, :], onesb, sq[:, c, :], start=(c == 0), stop=(c == nch - 1)),
                     reads=["sq", "cstb"], writes=[("ps", b)])
            P.op("act", lambda e, b=b: e.activation(rstd[:], ps[:, b, :], AF.Sqrt, bias=EPS, scale=mscale), reads=[("ps", b)], writes=["rstd"])
            P.op("dve", lambda e: e.reciprocal(rstd[:], rstd[:]), reads=["rstd"], writes=["rstd"])

        def norm_tile(t, gain_fn, shift_fn, out_fn, out_keys):
            tok = slice(t * 512, (t + 1) * 512)
            rms_rstd(lambda: xT[:, :, tok], 8, [("x", t)], 1.0)
            for c in range(8):
                P.op("dve", lambda e, c=c: e.tensor_tensor(ytmp[:, c, :], xT[:, c, tok], rstd[:], ALU.mult),
                     reads=[("x", t), "rstd"], writes=[("ytmp", c)])
                P.op("act", lambda e, c=c: e.activation(out_fn(c), ytmp[:, c, :], AF.Identity, bias=shift_fn(c), scale=gain_fn(c)),
                     reads=[("ytmp", c), "gm", "mod", "cst"], writes=out_keys(c))

        def modnorm(n):
            so = 0 if n == 0 else 24
            for t in range(NT):
                cd = cond_of(t)
                norm_tile(t,
                          lambda c, cd=cd: gm[:, n, cd, c:c + 1],
                          lambda c, cd=cd: mod[:, cd, so + c:so + c + 1],
                          lambda c, t=t: hT[:, c, t * 512:(t + 1) * 512],
                          lambda c, t=t: [("h", t)])

        def resid_proj(wsrc, in_fn, in_keys, goff):
            wv = kmajor(wsrc)
            for pc in range(2):
                wt, wk = wload(wv[:, :, pc * 512:(pc + 1) * 512], 8, 512)
                for mc in range(4):
                    m = pc * 4 + mc
                    for t in range(NT):
                        tok = slice(t * 512, (t + 1) * 512)
                        cd = cond_of(t)
                        b = bank()
                        for kc in range(8):
                            P.op("pe", lambda e, kc=kc, b=b, wt=wt, mc=mc, t=t: e.matmul(ps[:, b, :], wt[:, kc, mc * 128:(mc + 1) * 128], in_fn(kc, t), start=(kc == 0), stop=(kc == 7)),
                                 reads=[wk] + in_keys(t), writes=[("ps", b)])
                        P.op("dve", lambda e, b=b, m=m, tok=tok, cd=cd: e.scalar_tensor_tensor(xT[:, m, tok], ps[:, b, :], mod[:, cd, goff + m:goff + m + 1], xT[:, m, tok], ALU.mult, ALU.add),
                             reads=[("ps", b), "mod", ("x", t)], writes=[("x", t)])

        def ffn(layer, ada_next=None):
            groups = [(0, 8), (8, 8), (16, 6)]
            state["banks"] = [0, 1, 2, 3, 4, 5, 6]
            ada_todo = list(range(12)) if ada_next is not None else []

            def ada_step():
                if ada_todo:
                    adaln_piece(ada_next, ada_todo.pop(0))

            wg = kmajor(dr["ffn_w_gate"][layer])
            wu = kmajor(dr["ffn_w_up"][layer])
            wd = dr["ffn_w_down"][layer].rearrange("(f p) n -> p f n", p=128)
            go = 40
            for gi, (f0, nf) in enumerate(groups):
                hb = gi % 2
                npc = (nf + 3) // 4
                for pc in range(npc):
                    nfc = min(4, nf - pc * 4)
                    c0 = (f0 + pc * 4) * 128
                    wgt, wgk = wload(wg[:, :, c0:c0 + nfc * 128], 8, nfc * 128)
                    wut, wuk = wload(wu[:, :, c0:c0 + nfc * 128], 8, nfc * 128)
                    for fc in range(nfc):
                        fl = pc * 4 + fc
                        for t in range(NT):
                            tok = slice(t * 512, (t + 1) * 512)
                            bg, bu = bank(), bank()
                            for kc in range(8):
                                P.op("pe", lambda e, kc=kc, bg=bg, wgt=wgt, fc=fc, tok=tok: e.matmul(ps[:, bg, :], wgt[:, kc, fc * 128:(fc + 1) * 128], hT[:, kc, tok], start=(kc == 0), stop=(kc == 7)),
                                     reads=[wgk, ("h", t)], writes=[("ps", bg)])
                            for kc in range(8):
                                P.op("pe", lambda e, kc=kc, bu=bu, wut=wut, fc=fc, tok=tok: e.matmul(ps[:, bu, :], wut[:, kc, fc * 128:(fc + 1) * 128], hT[:, kc, tok], start=(kc == 0), stop=(kc == 7)),
                                     reads=[wuk, ("h", t)], writes=[("ps", bu)])
                            sl = (fl * NT + t) % 2
                            P.op("act", lambda e, bg=bg, sl=sl: e.activation(sgt[:, sl, :], ps[:, bg, :], AF.Silu),
                                 reads=[("ps", bg)], writes=[("sgt", sl)])
                            P.op("dve", lambda e, bu=bu, sl=sl, hb=hb, fl=fl, tok=tok: e.tensor_tensor(hid[:, hb, fl, tok], sgt[:, sl, :], ps[:, bu, :], ALU.mult),
                                 reads=[("ps", bu), ("sgt", sl)], writes=[("hid", hb, t)])
                    ada_step()
                for pc in range(2):
                    wdt, wdk = wload(wd[:, f0:f0 + nf, pc * 512:(pc + 1) * 512], nf, 512)
                    for mc in range(4):
                        m = pc * 4 + mc
                        for t in range(NT):
                            tok = slice(t * 512, (t + 1) * 512)
                            cd = cond_of(t)
                            b = bank()
                            for fl in range(nf):
                                P.op("pe", lambda e, fl=fl, b=b, wdt=wdt, mc=mc, hb=hb, tok=tok: e.matmul(ps[:, b, :], wdt[:, fl, mc * 128:(mc + 1) * 128], hid[:, hb, fl, tok], start=(fl == 0), stop=(fl == nf - 1)),
                                     reads=[wdk, ("hid", hb, t)], writes=[("ps", b)])
                            P.op("dve", lambda e, b=b, m=m, tok=tok, cd=cd: e.scalar_tensor_tensor(xT[:, m, tok], ps[:, b, :], mod[:, cd, go + m:go + m + 1], xT[:, m, tok], ALU.mult, ALU.add),
                                 reads=[("ps", b), "mod", ("x", t)], writes=[("x", t)])
                    ada_step()
            while ada_todo:
                ada_step()
            state["banks"] = list(range(8))

        def attn_core(G, Lk, hoffs, blocks, score_ops, v_ops, dv, scale, out_fn, out_keys, cfg, Pb, PT, st, sinkv=None, tag=0):
            Sb = cfg["S"]
            nkc = (Lk + 127) // 128

            if isinstance(hoffs, int):
                hoffs = [g * hoffs for g in range(G)]

            def scol(g, off):
                col = hoffs[g] + off
                return Sb[col // 512], col % 512

            def K_(n):
                return (n, tag)

            skeys = [("S", b) for b in Sb]
            mx, negm, rs, rinv = st[:, 0:G], st[:, G:2 * G], st[:, 2 * G:3 * G], st[:, 3 * G:4 * G]
            tmpv = st[:, 4 * G:5 * G]
            bm = st[:, 5 * G:5 * G + 8]
            blockwise = (G == 1 and len(blocks) > 1)
            for g in range(G):
                for bi, (off, n) in enumerate(blocks):
                    b, c0 = scol(g, off)
                    assert c0 + n <= 512
                    ops_ = score_ops(g, off, n)
                    for i, (lt, rh, rk) in enumerate(ops_):
                        P.op("pe", lambda e, b=b, c0=c0, n=n, lt=lt, rh=rh, i=i, last=len(ops_) - 1: e.matmul(ps[:, b, c0:c0 + n], lt, rh, start=(i == 0), stop=(i == last)),
                             reads=rk, writes=[("S", b)])
                    if blockwise:
                        P.op("dve", lambda e, b=b, c0=c0, n=n, bi=bi: e.tensor_reduce(bm[:, bi:bi + 1], ps[:, b, c0:c0 + n], AX.X, ALU.max), reads=[("S", b)], writes=[K_("st_bm")])
            if blockwise:
                P.op("dve", lambda e: e.tensor_reduce(mx, bm[:, 0:len(blocks)], AX.X, ALU.max), reads=[K_("st_bm")], writes=[K_("st_mx")])
            elif all(hoffs[g] == g * Lk for g in range(G)):
                sview = ps[:, Sb[0]:Sb[0] + len(Sb), :].rearrange("p b n -> p (b n)")[:, 0:G * Lk].rearrange("p (g k) -> p g k", g=G)
                P.op("dve", lambda e: e.tensor_reduce(mx, sview, AX.X, ALU.max), reads=skeys, writes=[K_("st_mx")])
            else:
                for g in range(G):
                    col = hoffs[g]
                    sv = ps[:, Sb[0]:Sb[0] + len(Sb), :].rearrange("p b n -> p (b n)")[:, col:col + Lk]
                    P.op("dve", lambda e, g=g, sv=sv: e.tensor_reduce(mx[:, g:g + 1], sv, AX.X, ALU.max), reads=skeys, writes=[K_("st_mx")])
            if sinkv is not None:
                P.op("dve", lambda e: e.scalar_tensor_tensor(mx, mx, scale, sinkv, ALU.mult, ALU.max), reads=[K_("st_mx"), "cst"], writes=[K_("st_mx")])
                P.op("dve", lambda e: e.tensor_scalar(negm, mx, -1.0, None, ALU.mult), reads=[K_("st_mx")], writes=[K_("st_negm")])
            else:
                P.op("dve", lambda e: e.tensor_scalar(negm, mx, -scale, None, ALU.mult), reads=[K_("st_mx")], writes=[K_("st_negm")])
            for g in range(G):
                col = hoffs[g]
                sv = ps[:, Sb[0]:Sb[0] + len(Sb), :].rearrange("p b n -> p (b n)")[:, col:col + Lk]
                P.op("act", lambda e, g=g, sv=sv: e.activation(Pb(g), sv, AF.Exp, bias=negm[:, g:g + 1], scale=scale, accum_out=rs[:, g:g + 1]),
                     reads=skeys + [K_("st_negm")], writes=[("Pb", g, tag), K_("st_rs")])
            if sinkv is not None:
                P.op("dve", lambda e: e.tensor_tensor(tmpv, sinkv, negm, ALU.add), reads=[K_("st_negm"), "cst"], writes=[K_("st_tmp")])
                P.op("act", lambda e: e.activation(tmpv, tmpv, AF.Exp), reads=[K_("st_tmp")], writes=[K_("st_tmp")])
                P.op("dve", lambda e: e.tensor_tensor(rs, rs, tmpv, ALU.add), reads=[K_("st_tmp"), K_("st_rs")], writes=[K_("st_rs")])
            P.op("dve", lambda e: e.reciprocal(rinv, rs), reads=[K_("st_rs")], writes=[K_("st_rinv")])

            def phase2():
                ptb = cfg["PT"]
                idx = 0
                pend = []
                total = G * nkc
                for g in range(G):
                    for kc in range(nkc):
                        nk = min(128, Lk - kc * 128)
                        slot = idx % 8
                        pb_ = ptb[(idx // 8) % len(ptb)]
                        pv = ps[:, pb_, :].bitcast(BF16)
                        P.op("pe", lambda e, g=g, kc=kc, nk=nk, slot=slot, pv=pv: e.transpose(pv[0:nk, slot * 128:(slot + 1) * 128], Pb(g)[:, kc * 128:kc * 128 + nk], identb),
                             reads=[("Pb", g, tag), "cstb"], writes=[("ps", pb_)])
                        pend.append(idx)
                        idx += 1
                        if len(pend) == 8 or idx == total:
                            i0 = pend[0]
                            n_ = len(pend)
                            evac(PT(i0, n_), pv[:, 0:n_ * 128].rearrange("p (a b) -> p a b", a=n_), [("ps", pb_)], ["PT"])
                            pend = []
                ob, oc0 = cfg["O"]
                for g in range(G):
                    col = oc0 + g * dv
                    b = ob + col // 512
                    c0 = col % 512
                    for kc in range(nkc):
                        nk = min(128, Lk - kc * 128)
                        rh, rk = v_ops(g, kc, nk)
                        ii = g * nkc + kc
                        P.op("pe", lambda e, b=b, c0=c0, ii=ii, nk=nk, rh=rh, kc=kc: e.matmul(ps[:, b, c0:c0 + dv], PT(ii, 1)[0:nk, 0, :], rh, start=(kc == 0), stop=(kc == nkc - 1)),
                             reads=["PT"] + rk, writes=["Oacc"])
                    P.op("act", lambda e, b=b, c0=c0, g=g: e.activation(out_fn(g), ps[:, b, c0:c0 + dv], AF.Identity, scale=rinv[:, g:g + 1]),
                         reads=["Oacc", K_("st_rinv")], writes=out_keys)

            return phase2

        def otok_to_oT(otok_v, okeys, tokcol, cfg):
            pb_ = cfg["PT"][0]
            pv = ps[:, pb_, :].bitcast(BF16)
            for c in range(8):
                P.op("pe", lambda e, c=c, pv=pv: e.transpose(pv[:, c * 128:(c + 1) * 128], otok_v[:, c * 128:(c + 1) * 128], identb),
                     reads=okeys + ["cstb"], writes=[("ps", pb_)])
            evac(hT[:, :, tokcol:tokcol + 128], pv.rearrange("p (a b) -> p a b", a=8), [("ps", pb_)], [("h", tokcol // 512)])

        def mla(layer):
            j = layer // 3
            SC = 192 ** -0.5
            P.barrier()
            qn = carve(0, [128, 4, T])
            ckb = carve(6144, [128, 2, 3328])
            krb = carve(12800, [128, 3328])
            B0 = 16128
            qlf = carve(B0, [128, 4, 512], F32)
            sqm = carve(B0 + 4096, [128, 4, 512])
            ckf = carve(B0 + 6144, [128, 2, 512], F32)
            krf = carve(B0 + 8192, [128, 512], F32)
            krt = carve(B0 + 9216, [128, 512], F32)
            agst = carve(B0 + 10240, [128, 1536])
            P.dma("pool", lambda e: e.dma_start(out=ckb[:, :, 1024:1280], in_=dr["ckv_ctxT"][j]), "ckctx", writes=["ckb_ctx"])
            P.dma("pool", lambda e: e.dma_start(out=krb[0:64, 1024:1280], in_=dr["krope_ctxT"][j]), "krctx", writes=["krb_ctx"])
            wdq, wdqk = wload(kmajor(dr["mla_w_dq"][j]), 8, 512)
            wdkv, wdkvk = wload(kmajor(dr["mla_w_dkv"][j]), 8, 320)
            state["banks"] = list(range(8))
            for t in range(NT):
                tok = slice(t * 512, (t + 1) * 512)
                for oc in range(4):
                    b = bank()
                    for kc in range(8):
                        P.op("pe", lambda e, b=b, kc=kc, oc=oc, tok=tok: e.matmul(ps[:, b, :], wdq[:, kc, oc * 128:(oc + 1) * 128], hT[:, kc, tok], start=(kc == 0), stop=(kc == 7)),
                             reads=[wdqk, ("h", t)], writes=[("ps", b)])
                    evac(qlf[:, oc, :], ps[:, b, :], [("ps", b)], ["qlf"])
                rms_rstd(lambda: qlf[:], 4, ["qlf"], 2.0, sqm)
                for oc in range(4):
                    P.op("dve", lambda e, oc=oc: e.tensor_tensor(qlf[:, oc, :], qlf[:, oc, :], rstd[:], ALU.mult), reads=["qlf", "rstd"], writes=["qlf"])
                    P.op("act", lambda e, oc=oc, tok=tok: e.activation(qn[:, oc, tok], qlf[:, oc, :], AF.Identity, scale=cst[:, K.QNORM + j * 4 + oc:K.QNORM + j * 4 + oc + 1]),
                         reads=["qlf", "cst"], writes=[("qn", t)])
                for oc in range(2):
                    b = bank()
                    for kc in range(8):
                        P.op("pe", lambda e, b=b, kc=kc, oc=oc, tok=tok: e.matmul(ps[:, b, :], wdkv[:, kc, oc * 128:(oc + 1) * 128], hT[:, kc, tok], start=(kc == 0), stop=(kc == 7)),
                             reads=[wdkvk, ("h", t)], writes=[("ps", b)])
                    evac(ckf[:, oc, :], ps[:, b, :], [("ps", b)], ["ckf"])
                b = bank()
                for kc in range(8):
                    P.op("pe", lambda e, b=b, kc=kc, tok=tok: e.matmul(ps[0:64, b, :], wdkv[:, kc, 256:320], hT[:, kc, tok], start=(kc == 0), stop=(kc == 7)),
                         reads=[wdkvk, ("h", t)], writes=[("ps", b)])
                evac(krf[0:64, :], ps[0:64, b, :], [("ps", b)], ["krf"])
                rms_rstd(lambda: ckf[:], 2, ["ckf"], 4.0, sqm)
                kcol = t * 512 if t < 2 else None
                for oc in range(2):
                    P.op("dve", lambda e, oc=oc: e.tensor_tensor(ckf[:, oc, :], ckf[:, oc, :], rstd[:], ALU.mult), reads=["ckf", "rstd"], writes=["ckf"])
                    P.op("act", lambda e, oc=oc: e.activation(ckf[:, oc, :], ckf[:, oc, :], AF.Identity, scale=cst[:, K.KVNORM + j * 2 + oc:K.KVNORM + j * 2 + oc + 1]),
                         reads=["ckf", "cst"], writes=["ckf"])
                    if t < 2:
                        evac(ckb[:, oc, kcol:kcol + 512], ckf[:, oc, :], ["ckf"], [("ckb", t)])
                    else:
                        evac(agst[:, oc * 512:(oc + 1) * 512], ckf[:, oc, :], ["ckf"], ["agst"])
                if t < 2:
                    evac(krb[0:64, kcol:kcol + 512], krf[0:64, :], ["krf"], [("krb", t)])
                    for q in range(4):
                        b = bank()
                        for oc in range(2):
                            P.op("pe", lambda e, b=b, oc=oc, q=q: e.transpose(ps[:, b, oc * 128:(oc + 1) * 128], ckf[:, oc, q * 128:(q + 1) * 128], ident),
                                 reads=["ckf", "cst"], writes=[("ps", b)])
                        P.op("pe", lambda e, b=b, q=q: e.transpose(ps[:, b, 256:320], krf[0:64, q * 128:(q + 1) * 128], ident[0:64, 0:64]),
                             reads=["krf", "cst"], writes=[("ps", b)])
                        sl = q % 2
                        evac(stage[:, sl, 0:320], ps[:, b, 0:320], [("ps", b)], [("stage", sl)])
                        gtok = t * 512 + q * 128
                        sq_, r0 = gtok // 256, gtok % 256
                        P.dma("sp", lambda e, sl=sl, sq_=sq_, r0=r0: e.dma_start(out=dr["o_ckv"][sq_, j, r0:r0 + 128, :], in_=stage[:, sl, 0:256]),
                              ("ost", sl), reads=[("stage", sl)], final=True)
                        P.dma("sp", lambda e, sl=sl, sq_=sq_, r0=r0: e.dma_start(out=dr["o_krope"][sq_, j, r0:r0 + 128, :], in_=stage[:, sl, 256:320]),
                              ("ost2", sl), reads=[("stage", sl)], final=True)
                else:
                    b = bank()
                    P.op("pe", lambda e, b=b: e.matmul(ps[0:64, b, :], cst[0:64, K.PERM:K.PERM + 64], krf[0:64, :], start=True, stop=True),
                         reads=["krf", "cst"], writes=[("ps", b)])
                    P.op("dve", lambda e, b=b: e.tensor_tensor(krt[0:64, :], ps[0:64, b, :], cst[0:64, K.SIN:K.SIN + 512], ALU.mult), reads=[("ps", b), "cst"], writes=["krt"])
                    P.op("dve", lambda e: e.tensor_tensor(krf[0:64, :], krf[0:64, :], cst[0:64, K.COS:K.COS + 512], ALU.mult), reads=["krf", "cst"], writes=["krf"])
                    P.op("dve", lambda e: e.tensor_tensor(agst[0:64, 1024:1536], krf[0:64, :], krt[0:64, :], ALU.add), reads=["krf", "krt"], writes=["agst"])
                    P.op("dve", lambda e: e.memset(agst[64:128, 1024:1536], 0.0), reads=[], writes=["agst"])
                    P.dma("sp", lambda e: e.dma_start(out=agin[j].ap(), in_=agst[:]), ("agin", j), reads=["agst"], writes=[("agin", j)])
                    P.coll(lambda e: e.collective_compute("AllGather", ALU.bypass, replica_groups=[[0, 1, 2, 3], [4, 5, 6, 7]],
                                                          ins=[agin[j].ap().opt()], outs=[agout[j].ap().opt()]),
                           ("agc", j), reads=[("agin", j)], writes=[("agout", j)])
                    agv = agout[j].ap().rearrange("(r p) n -> p r n", p=128)
                    for oc in range(2):
                        P.dma("sp", lambda e, oc=oc: e.dma_start(out=ckb[:, oc, 1280:3328].rearrange("p (r n) -> p r n", r=4), in_=agv[:, :, oc * 512:(oc + 1) * 512]),
                              ("agld", oc), reads=[("agout", j)], writes=["ckb_lat"])
                    P.dma("sp", lambda e: e.dma_start(out=krb[0:64, 1280:3328].rearrange("p (r n) -> p r n", r=4), in_=agv[0:64, :, 1024:1536]),
                          ("agld", 2), reads=[("agout", j)], writes=["krb_lat"])
            P.barrier(chans=[("agin", j)])
            wuq0, wuq0k = wload(kmajor(dr["mla_w_uq"][j])[:, :, 0:768], 4, 768)
            wuq1, wuq1k = wload(kmajor(dr["mla_w_uq"][j])[:, :, 768:1536], 4, 768)
            wuk, wukk = wload(kmajor(dr["mla_w_uk"][j]), 2, 1024)
            wuv, wuvk = wload(kmajor(dr["mla_w_uv"][j]), 2, 1024)

            def uq(h):
                w_, k_ = (wuq0, wuq0k) if h < 4 else (wuq1, wuq1k)
                return w_, k_, (h % 4) * 192

            qno = carve(B0, [128, 8, 512])
            qro = carve(B0 + 4096, [128, 8, 512])
            kn = carve(B0 + 8192, [128, 8, 512])
            V = carve(B0 + 12288, [128, 4, 1024])
            Pbp2 = [carve(B0 + 16384, [128, 8, 256]), carve(B0 + 22784, [128, 8, 256])]
            PTp = carve(B0 + 18432, [128, 16, 128])
            otok = carve(B0 + 20480, [128, 2, 1024])
            st = carve(B0 + 22528, [128, 128], F32)
            cfgp = {"S": [0, 1, 2, 3], "O": (4, 0), "PT": [6]}
            state["banks"] = [7]
            for t in range(2):
                tok = slice(t * 512, (t + 1) * 512)
                for h in range(8):
                    w_, k_, c0 = uq(h)
                    b = bank()
                    for kc in range(4):
                        P.op("pe", lambda e, b=b, kc=kc, w_=w_, c0=c0, tok=tok: e.matmul(ps[:, b, :], w_[:, kc, c0:c0 + 128], qn[:, kc, tok], start=(kc == 0), stop=(kc == 3)),
                             reads=[k_, ("qn", t)], writes=[("ps", b)])
                    evac(qno[:, h, :], ps[:, b, :], [("ps", b)], ["qno"])
                    b = bank()
                    for kc in range(4):
                        P.op("pe", lambda e, b=b, kc=kc, w_=w_, c0=c0, tok=tok: e.matmul(ps[0:64, b, :], w_[:, kc, c0 + 128:c0 + 192], qn[:, kc, tok], start=(kc == 0), stop=(kc == 3)),
                             reads=[k_, ("qn", t)], writes=[("ps", b)])
                    evac(qro[0:64, h, :], ps[0:64, b, :], [("ps", b)], ["qro"])
                    b = bank()
                    for oc in range(2):
                        P.op("pe", lambda e, b=b, oc=oc, h=h, tok=tok: e.matmul(ps[:, b, :], wuk[:, oc, h * 128:(h + 1) * 128], ckb[:, oc, tok], start=(oc == 0), stop=(oc == 1)),
                             reads=[wukk, ("ckb", t)], writes=[("ps", b)])
                    evac(kn[:, h, :], ps[:, b, :], [("ps", b)], ["kn"])
                for c in range(4):
                    for hf in range(2):
                        b = bank()
                        for oc in range(2):
                            P.op("pe", lambda e, b=b, oc=oc, c=c, hf=hf, t=t: e.matmul(ps[:, b, :], ckb[:, oc, t * 512 + c * 128:t * 512 + (c + 1) * 128], wuv[:, oc, hf * 512:(hf + 1) * 512], start=(oc == 0), stop=(oc == 1)),
                                 reads=[wuvk, ("ckb", t)], writes=[("ps", b)])
                        evac(V[:, c, hf * 512:(hf + 1) * 512], ps[:, b, :], [("ps", b)], ["V"])
                pend2 = None
                for s_ in range(2):
                    for qt in range(2):
                        q0 = s_ * 256 + qt * 128
                        ob = (s_ * 2 + qt) % 2

                        def score_ops(g, off, n, q0=q0, s_=s_):
                            return [(qno[:, g, q0:q0 + 128], kn[:, g, s_ * 256 + off:s_ * 256 + off + n], ["qno", "kn"]),
                                    (qro[0:64, g, q0:q0 + 128], krb[0:64, t * 512 + s_ * 256 + off:t * 512 + s_ * 256 + off + n], ["qro", ("krb", t)])]

                        def v_ops(g, kc, nk, s_=s_):
                            return V[0:nk, s_ * 2 + kc, g * 128:(g + 1) * 128], ["V"]

                        p2 = attn_core(8, 256, 256, [(0, 256)], score_ops, v_ops, 128, SC,
                                       lambda g, ob=ob: otok[:, ob, g * 128:(g + 1) * 128], [("otok", ob)], cfgp,
                                       lambda g, ob=ob: Pbp2[ob][:, g, :], lambda i0, n_: PTp[:, i0:i0 + n_, :], st[:, ob * 64:(ob + 1) * 64], tag=ob)

                        def fin(p2=p2, ob=ob, tc=t * 512 + q0):
                            p2()
                            otok_to_oT(otok[:, ob, :], [("otok", ob)], tc, cfgp)

                        if pend2 is not None:
                            pend2()
                        pend2 = fin
                pend2()
            P.barrier()
            qh = carve(B0, [128, 2, 512])
            qr = carve(B0 + 1024, [128, 2, 512])
            qraw = carve(B0 + 2048, [128, 512], F32)
            qt1 = carve(B0 + 3072, [128, 512], F32)
            kns = carve(B0 + 4096, [128, 2, 2304])
            vh = carve(B0 + 8704, [128, 2, 18, 128])
            Pbs = carve(B0 + 13312, [128, 2, 2304])
            PTs = carve(B0 + 17920, [128, 18, 128])
            otoks = carve(B0 + 20224, [128, 4, 1024])
            sts = carve(B0 + 24320, [128, 128], F32)
            pend_s = [None]
            cfgs = {"S": [0, 1, 2, 3, 4], "O": (5, 0), "PT": [6]}
            t = 2
            tok = slice(1024, 1536)
            kblocks = [(0, 512), (512, 512), (1024, 512), (1536, 512), (2048, 256)]
            it = 0
            for h in range(8):
                hb = h % 2
                w_, k_, c0 = uq(h)
                b = bank()
                for kc in range(4):
                    P.op("pe", lambda e, b=b, kc=kc, w_=w_, c0=c0, tok=tok: e.matmul(ps[:, b, :], w_[:, kc, c0:c0 + 128], qn[:, kc, tok], start=(kc == 0), stop=(kc == 3)),
                         reads=[k_, ("qn", t)], writes=[("ps", b)])
                evac(qh[:, hb, :], ps[:, b, :], [("ps", b)], [("qh", hb)])
                b = bank()
                for kc in range(4):
                    P.op("pe", lambda e, b=b, kc=kc, w_=w_, c0=c0, tok=tok: e.matmul(ps[0:64, b, :], w_[:, kc, c0 + 128:c0 + 192], qn[:, kc, tok], start=(kc == 0), stop=(kc == 3)),
                         reads=[k_, ("qn", t)], writes=[("ps", b)])
                evac(qraw[0:64, :], ps[0:64, b, :], [("ps", b)], ["qraw"], eng="act")
                b = bank()
                P.op("pe", lambda e, b=b: e.matmul(ps[0:64, b, :], cst[0:64, K.PERM:K.PERM + 64], qraw[0:64, :], start=True, stop=True),
                     reads=["qraw", "cst"], writes=[("ps", b)])
                P.op("dve", lambda e, b=b: e.tensor_tensor(qt1[0:64, :], ps[0:64, b, :], cst[0:64, K.SIN:K.SIN + 512], ALU.mult), reads=[("ps", b), "cst"], writes=["qt1"])
                P.op("dve", lambda e: e.tensor_tensor(qraw[0:64, :], qraw[0:64, :], cst[0:64, K.COS:K.COS + 512], ALU.mult), reads=["qraw", "cst"], writes=["qraw"])
                P.op("dve", lambda e, hb=hb: e.tensor_tensor(qr[0:64, hb, :], qraw[0:64, :], qt1[0:64, :], ALU.add), reads=["qraw", "qt1"], writes=[("qr", hb)])
                for (off, n) in kblocks:
                    b = bank()
                    for oc in range(2):
                        P.op("pe", lambda e, b=b, oc=oc, h=h, off=off, n=n: e.matmul(ps[:, b, 0:n], wuk[:, oc, h * 128:(h + 1) * 128], ckb[:, oc, 1024 + off:1024 + off + n], start=(oc == 0), stop=(oc == 1)),
                             reads=[wukk, "ckb_ctx", "ckb_lat"], writes=[("ps", b)])
                    evac(kns[:, hb, off:off + n], ps[:, b, 0:n], [("ps", b)], [("kns", hb)])
                for k4 in range(5):
                    nk4 = min(4, 18 - k4 * 4)
                    b = bank()
                    for kk in range(nk4):
                        kc = k4 * 4 + kk
                        for oc in range(2):
                            P.op("pe", lambda e, b=b, oc=oc, kk=kk, kc=kc, h=h: e.matmul(ps[:, b, kk * 128:(kk + 1) * 128], ckb[:, oc, 1024 + kc * 128:1024 + (kc + 1) * 128], wuv[:, oc, h * 128:(h + 1) * 128], start=(oc == 0), stop=(oc == 1)),
                                 reads=[wuvk, "ckb_ctx", "ckb_lat"], writes=[("ps", b)])
                    evac(vh[:, hb, k4 * 4:k4 * 4 + nk4, :], ps[:, b, 0:nk4 * 128].rearrange("p (a b) -> p a b", a=nk4), [("ps", b)], [("vh", hb)])
                for qt in range(4):
                    q0 = qt * 128
                    pbuf = it % 2
                    it += 1

                    def score_ops(g, off, n, q0=q0, hb=hb):
                        return [(qh[:, hb, q0:q0 + 128], kns[:, hb, off:off + n], [("qh", hb), ("kns", hb)]),
                                (qr[0:64, hb, q0:q0 + 128], krb[0:64, 1024 + off:1024 + off + n], [("qr", hb), "krb_ctx", "krb_lat"])]

                    def v_ops(g, kc, nk, hb=hb):
                        return vh[0:nk, hb, kc, :], [("vh", hb)]

                    p2 = attn_core(1, 2304, 2304, kblocks, score_ops, v_ops, 128, SC,
                                   lambda g, qt=qt, h=h: otoks[:, qt, h * 128:(h + 1) * 128], [("otoks", qt)], cfgs,
                                   lambda g, pbuf=pbuf: Pbs[:, pbuf, :], lambda i0, n_: PTs[:, i0:i0 + n_, :], sts[:, pbuf * 64:(pbuf + 1) * 64], tag=pbuf)
                    if pend_s[0] is not None:
                        pend_s[0]()
                    pend_s[0] = p2
            pend_s[0]()
            for qt in range(4):
                otok_to_oT(otoks[:, qt, :], [("otoks", qt)], 1024 + qt * 128, cfgs)
            state["banks"] = list(range(8))
            P.barrier()
            resid_proj(dr["mla_w_o"][j], lambda kc, t: hT[:, kc, t * 512:(t + 1) * 512], lambda t: [("h", t)], 16)

        def hgrn(layer):
            j = layer // 3
            P.barrier()
            Vt = carve(0, [128, 4, 1024])
            qT = carve(4096, [128, 8, 512])
            gT = carve(8192, [128, 8, 512])
            Qt = carve(12288, [128, 8, 512])
            Kt = carve(16384, [128, 8, 512])
            Ktok = carve(20480, [128, 4, 1024])
            oacc = carve(24576, [128, 4, 1024], F32)
            tA = carve(32768, [128, 512], F32)
            tB = carve(32768 + 1024, [128, 512], F32)
            tC = carve(32768 + 2048, [128, 512], F32)
            tE = carve(32768 + 3072, [128, 512], F32)
            osq = carve(32768, [128, 1024], F32)
            onb = carve(32768 + 2048, [128, 1024])
            Sf = carve(36864, [128, 8, 128], F32)
            Sb = carve(38912, [128, 8, 128])
            attb = carve(39936, [128, 8, 128])
            Dd = hgs[:, 0:64].rearrange("p (h c) -> p h c", h=8)
            Fl = hgs[:, 64:72]
            ssq = hgs[:, 72:80]
            rsd = hgs[:, 80:88]
            Fm = hgs[:, 88:120].rearrange("p (r h) -> p r h", r=4)
            lg = cst[:, K.LBL:K.LBL + 64].rearrange("p (a l) -> p a l", l=4)
            le = lbw[:, 0:64].rearrange("p (a l) -> p a l", l=4)
            lmx, lsum, lnum = lbw[:, 64:80], lbw[:, 80:96], lbw[:, 96:112]
            lb, oml = lbw[:, 112:128], lbw[:, 128:144]
            P.op("dve", lambda e: e.tensor_reduce(lmx, lg, AX.X, ALU.max), reads=["cst"], writes=["lmx"])
            P.op("dve", lambda e: e.tensor_tensor(le, lg, lmx.unsqueeze(2).to_broadcast([128, 16, 4]), ALU.subtract), reads=["cst", "lmx"], writes=["le"])
            P.op("act", lambda e: e.activation(le, le, AF.Exp), reads=["le"], writes=["le"])
            P.op("dve", lambda e: e.tensor_reduce(lsum, le, AX.X, ALU.add), reads=["le"], writes=["lsum"])
            P.op("dve", lambda e: e.tensor_reduce(lnum, le[:, :, 1:layer + 1], AX.X, ALU.add), reads=["le"], writes=["lnum"])
            P.op("dve", lambda e: e.reciprocal(lsum, lsum), reads=["lsum"], writes=["lsum"])
            P.op("dve", lambda e: e.tensor_tensor(lb, lnum, lsum, ALU.mult), reads=["lsum", "lnum"], writes=["lb"])
            P.op("dve", lambda e: e.tensor_scalar(oml, lb, -1.0, 1.0, ALU.mult, ALU.add), reads=["lb"], writes=["lb"])
            P.op("dve", lambda e: e.memset(ones64[:], 1.0), writes=["ones64"])

            wq_d, wi_d, wg_d = kmajor(dr["hg_w_q"][j]), kmajor(dr["hg_w_i"][j]), kmajor(dr["hg_w_g"][j])
            wf_d = [kmajor(dr["hg_w_f"][j, 0]), kmajor(dr["hg_w_f"][j, 1])]
            maskb = [cstb[:, 256:384], cstb[:, 384:512]]
            AB, R1, R2, UB = [0, 1], [2, 3], [4, 5], [6, 7]

            def fm_proj(wd, out_fn, tsl, t, keyw):
                for pc in range(2):
                    wt, wk = wload(wd[:, :, pc * 512:(pc + 1) * 512], 8, 512)
                    for cc in range(4):
                        c = pc * 4 + cc
                        b = bank()
                        for kc in range(8):
                            P.op("pe", lambda e, b=b, kc=kc, wt=wt, cc=cc, tsl=tsl: e.matmul(ps[:, b, :], wt[:, kc, cc * 128:(cc + 1) * 128], hT[:, kc, tsl], start=(kc == 0), stop=(kc == 7)),
                                 reads=[wk, ("h", t)], writes=[("ps", b)])
                        out_fn(c, ps[:, b, :], ("ps", b))

            def scan_dir(t, d, seqs, sample):
                for (soff, slen) in seqs:
                    subs = list(range(soff // 128, (soff + slen) // 128))
                    if d == 1:
                        subs = subs[::-1]
                    chs = [0, 1] if d == 0 else [1, 0]
                    for outputs in ([False, True] if sample else [True]):
                        if not sample:
                            P.op("dve", lambda e: e.memset(Sf[:], 0.0), writes=["Sf"])
                            P.op("dve", lambda e: e.memset(Sb[:], 0.0), writes=["Sb"])
                        elif not outputs:
                            P.op("dve", lambda e: e.memset(Sf[:], 0.0), writes=["Sf"])
                        for s in subs:
                            cols = slice(s * 128, (s + 1) * 128)
                            if outputs:
                                for h in range(8):
                                    b = AB[h // 4]
                                    P.op("pe", lambda e, b=b, h=h, cols=cols: e.matmul(ps[:, b, (h % 4) * 128:(h % 4 + 1) * 128], Kt[:, h, cols], Qt[:, h, cols], start=True, stop=True),
                                         reads=["Kt", "Qt"], writes=[("ps", b)])
                                for hf in range(2):
                                    b = AB[hf]
                                    P.op("dve", lambda e, b=b, hf=hf: e.tensor_tensor(attb[:, hf * 4:hf * 4 + 4, :], ps[:, b, :].rearrange("p (a n) -> p a n", a=4), maskb[d].unsqueeze(1).to_broadcast([128, 4, 128]), ALU.mult),
                                         reads=[("ps", b), "cstb"], writes=["attb"])
                            for ch in chs:
                                r0 = ch * 64
                                ci = (s * 2 + ch)
                                tcols = slice(s * 128 + r0, s * 128 + r0 + 64)
                                if outputs:
                                    for h in range(8):
                                        b = R1[h // 4]
                                        P.op("pe", lambda e, b=b, h=h, r0=r0, tcols=tcols: e.matmul(ps[r0:r0 + 64, b, (h % 4) * 128:(h % 4 + 1) * 128], Qt[:, h, tcols], Sb[:, h, :], start=True, stop=True),
                                             reads=["Qt", "Sb"], writes=[("ps", b)])
                                for h in range(8):
                                    b = UB[h // 4]
                                    P.op("pe", lambda e, b=b, h=h, r0=r0, s=s: e.matmul(ps[:, b, (h % 4) * 128:(h % 4 + 1) * 128], Ktok[r0:r0 + 64, s, h * 128:(h + 1) * 128], Vt[r0:r0 + 64, s, h * 128:(h + 1) * 128], start=True, stop=True),
                                         reads=["Ktok", "Vt"], writes=[("ps", b)])
                                for hf in range(2):
                                    b = UB[hf]
                                    P.op("dve", lambda e, b=b, hf=hf: e.tensor_tensor(Sf[:, hf * 4:hf * 4 + 4, :], Sf[:, hf * 4:hf * 4 + 4, :], ps[:, b, :].rearrange("p (a n) -> p a n", a=4), ALU.add),
                                         reads=[("ps", b), "Sf"], writes=["Sf"])
                                P.op("dve", lambda e, ci=ci: e.tensor_tensor(Sf[:], Sf[:], Dd[:, :, ci % 8].unsqueeze(2).to_broadcast([128, 8, 128]), ALU.mult),
                                     reads=["Sf", "Dd"], writes=["Sf"])
                                if outputs:
                                    P.op("act", lambda e: e.copy(Sb[:], Sf[:]), reads=["Sf"], writes=["Sb"])
                            if outputs:
                                for h in range(8):
                                    b = R2[h // 4]
                                    P.op("pe", lambda e, b=b, h=h, s=s: e.matmul(ps[:, b, (h % 4) * 128:(h % 4 + 1) * 128], attb[:, h, :], Vt[:, s, h * 128:(h + 1) * 128], start=True, stop=True),
                                         reads=["attb", "Vt"], writes=[("ps", b)])
                                for hf in range(2):
                                    osl = oacc[:, s, hf * 512:(hf + 1) * 512]
                                    if d == 0:
                                        P.op("act", lambda e, hf=hf, osl=osl: e.copy(osl, ps[:, R1[hf], :]), reads=[("ps", R1[hf])], writes=[("oacc", s)])
                                    else:
                                        P.op("dve", lambda e, hf=hf, osl=osl: e.tensor_tensor(osl, osl, ps[:, R1[hf], :], ALU.add), reads=[("ps", R1[hf]), ("oacc", s)], writes=[("oacc", s)])
                                    P.op("dve", lambda e, hf=hf, osl=osl: e.tensor_tensor(osl, osl, ps[:, R2[hf], :], ALU.add), reads=[("ps", R2[hf]), ("oacc", s)], writes=[("oacc", s)])
                        if sample and not outputs:
                            P.op("dve", lambda e: e.tensor_reduce(Fl, Dd, AX.X, ALU.mult), reads=["Dd"], writes=["Fl"])
                            P.dma("sp", lambda e: e.dma_start(out=hgin[d].ap()[:, 0:1024], in_=Sf[:].rearrange("p h v -> p (h v)")), ("hgin", d), reads=["Sf"], writes=[("hgin", d)])
                            P.dma("sp", lambda e: e.dma_start(out=hgin[d].ap()[:, 1024:1032], in_=Fl), ("hginF", d), reads=["Fl"], writes=[("hgin", d)])
                            P.coll(lambda e: e.collective_compute("AllGather", ALU.bypass, replica_groups=[[0, 1, 2, 3], [4, 5, 6, 7]],
                                                                  ins=[hgin[d].ap().opt()], outs=[hgout[d].ap().opt()]),
                                   ("hgc", d), reads=[("hgin", d)], writes=[("hgout", d)])
                            gv = hgout[d].ap().rearrange("(r p) n -> p r n", p=128)
                            Fg = hgs2[:, 0:32].rearrange("p (r h) -> p r h", r=4)
                            P.dma("sp", lambda e: e.dma_start(out=Fg, in_=gv[:, :, 1024:1032]), ("hgF", d), reads=[("hgout", d)], writes=["Fg"])
                            P.dma("sp", lambda e: e.dma_start(out=Sf[:], in_=dr["hg_s0"][d].rearrange("h k v -> k h v")), ("hgs0", d), writes=["Sf"])
                            mo = K.RMASK + (0 if d == 0 else 8)
                            ranks = [0, 1, 2, 3] if d == 0 else [3, 2, 1, 0]
                            for r in ranks:
                                P.op("dve", lambda e, r=r: e.tensor_scalar(Fm[:, r, :], Fg[:, r, :], cst[:, mo + r:mo + r + 1], cst[:, mo + 4 + r:mo + 5 + r], ALU.mult, ALU.add),
                                     reads=["Fg", "cst"], writes=["Fm"])
                            for r in ranks:
                                sl = r % 2
                                P.dma("sp", lambda e, r=r, sl=sl: e.dma_start(out=stage[:, sl, :], in_=gv[:, r, 0:1024]), ("stage", sl), reads=[("hgout", d)], writes=[("stage", sl)])
                                P.op("dve", lambda e, r=r: e.tensor_tensor(Sf[:], Sf[:], Fm[:, r, :].unsqueeze(2).to_broadcast([128, 8, 128]), ALU.mult), reads=["Sf", "Fm"], writes=["Sf"])
                                P.op("dve", lambda e, r=r, sl=sl: e.scalar_tensor_tensor(Sf[:].rearrange("p h v -> p (h v)"), stage[:, sl, :], cst[:, mo + r:mo + r + 1], Sf[:].rearrange("p h v -> p (h v)"), ALU.mult, ALU.add),
                                     reads=["Sf", ("stage", sl), "cst"], writes=["Sf"])
                            P.op("act", lambda e: e.copy(Sb[:], Sf[:]), reads=["Sf"], writes=["Sb"])
                    if not sample:
                        sidx = (t * 512 + soff) // 256
                        P.dma("sp", lambda e, sidx=sidx: e.dma_start(out=dr["o_hg"][sidx, j, d].rearrange("h k v -> k h v"), in_=Sf[:]), ("ohg", d), reads=["Sf"], final=True)

            for t in [2, 0, 1]:
                tsl = slice(t * 512, (t + 1) * 512)
                sample = (t == 2)
                seqs = [(0, 512)] if sample else [(0, 256), (256, 256)]
                state["banks"] = list(range(8))
                wts = [wload(wi_d[:, :, pc * 512:(pc + 1) * 512], 8, 512) for pc in range(2)]
                for s in range(4):
                    for pc in range(2):
                        wt, wk = wts[pc]
                        b = bank()
                        for kc in range(8):
                            P.op("pe", lambda e, b=b, kc=kc, wt=wt, s=s, t=t: e.matmul(ps[:, b, :], hT[:, kc, t * 512 + s * 128:t * 512 + (s + 1) * 128], wt[:, kc, :], start=(kc == 0), stop=(kc == 7)),
                                 reads=[wk, ("h", t)], writes=[("ps", b)])
                        evac(Vt[:, s, pc * 512:(pc + 1) * 512], ps[:, b, :], [("ps", b)], ["Vt"])
                fm_proj(wq_d, lambda c, pa, pk: P.op("act", lambda e: e.activation(qT[:, c, :], pa, AF.Silu), reads=[pk], writes=["qT"]), tsl, t, "q")
                fm_proj(wg_d, lambda c, pa, pk: P.op("act", lambda e: e.activation(gT[:, c, :], pa, AF.Silu), reads=[pk], writes=["gT"]), tsl, t, "g")
                for d in range(2):
                    def prep(c, pa, pk, d=d):
                        lbc = lb[:, d * 8 + c:d * 8 + c + 1]
                        omc = oml[:, d * 8 + c:d * 8 + c + 1]
                        if c % 2 == 0:
                            tA_, tB_, tC_, tE_ = tA, tB, tC, tE
                            kA, kB, kC, kE = "tA", "tB", "tC", "tE"
                        else:
                            tA_, tB_, tC_, tE_ = stage[:, 0, 0:512], stage[:, 0, 512:1024], stage[:, 1, 0:512], stage[:, 1, 512:1024]
                            kA, kB, kC, kE = ("stage", 0), ("stage", 0), ("stage", 1), ("stage", 1)
                        P.op("act", lambda e: e.activation(tA_, pa, AF.Exp, scale=-1.0), reads=[pk], writes=[kA])
                        P.op("dve", lambda e: e.tensor_scalar(tA_, tA_, 1.0, None, ALU.add), reads=[kA], writes=[kA])
                        P.op("dve", lambda e: e.reciprocal(tA_, tA_), reads=[kA], writes=[kA])
                        P.op("dve", lambda e: e.tensor_scalar(tA_, tA_, omc, lbc, ALU.mult, ALU.add), reads=[kA, "lb"], writes=[kA])
                        P.op("pool", lambda e: e.tensor_scalar(tB_, tA_, -1.0, 1.0, ALU.mult, ALU.add), reads=[kA], writes=[kB])
                        P.op("act", lambda e: e.activation(tA_, tA_, AF.Ln), reads=[kA, kB], writes=[kA])
                        for ck in range(8):
                            if d == 0:
                                o_, i_ = tC_[:, ck * 64:(ck + 1) * 64], tA_[:, ck * 64:(ck + 1) * 64]
                            else:
                                lo = ck * 64 - 1 if ck > 0 else None
                                o_, i_ = tC_[:, ck * 64 + 63:lo:-1], tA_[:, ck * 64 + 63:lo:-1]
                            P.op("dve", lambda e, o_=o_, i_=i_: e.tensor_tensor_scan(o_, ones64[:], i_, 0.0, ALU.mult, ALU.add), reads=[kA, "ones64"], writes=[kC])
                        P.op("act", lambda e: e.activation(tE_, tC_, AF.Exp), reads=[kC], writes=[kE])
                        P.op("pool", lambda e: e.tensor_tensor(Qt[:, c, :], qT[:, c, :], tE_, ALU.mult), reads=[kE, "qT"], writes=["Qt"])
                        dcol = tE_[:, 63::64] if d == 0 else tE_[:, 0::64]
                        P.op("dve", lambda e: e.tensor_copy(Dd[:, c, :], dcol), reads=[kE], writes=["Dd"])
                        P.op("act", lambda e: e.activation(tE_, tC_, AF.Exp, scale=-1.0), reads=[kC, kE, "Qt", "Dd"], writes=[kE])
                        P.op("pool", lambda e: e.tensor_tensor(Kt[:, c, :], tB_, tE_, ALU.mult), reads=[kE, kB], writes=["Kt"])

                    fm_proj(wf_d[d], prep, tsl, t, "f")
                    for s in range(4):
                        b = bank()
                        pv = ps[:, b, :].bitcast(BF16)
                        for h in range(8):
                            P.op("pe", lambda e, pv=pv, h=h, s=s: e.transpose(pv[:, h * 128:(h + 1) * 128], Kt[:, h, s * 128:(s + 1) * 128], identb),
                                 reads=["Kt", "cstb"], writes=[("ps", b)])
                        evac(Ktok[:, s, :], pv, [("ps", b)], ["Ktok"])
                    scan_dir(t, d, seqs, sample)
                state["banks"] = list(range(8))
                for s in range(4):
                    P.op("act", lambda e, s=s: e.activation(osq[:], oacc[:, s, :], AF.Square), reads=[("oacc", s)], writes=["osq"])
                    P.op("dve", lambda e: e.tensor_reduce(ssq, osq[:].rearrange("p (h v) -> p h v", h=8), AX.X, ALU.add), reads=["osq"], writes=["ssq"])
                    P.op("act", lambda e: e.activation(rsd, ssq, AF.Sqrt, bias=EPS, scale=1.0 / 128.0), reads=["ssq"], writes=["rsd"])
                    P.op("dve", lambda e: e.reciprocal(rsd, rsd), reads=["rsd"], writes=["rsd"])
                    P.op("dve", lambda e, s=s: e.tensor_tensor(onb[:].rearrange("p (h v) -> p h v", h=8), oacc[:, s, :].rearrange("p (h v) -> p h v", h=8), rsd.unsqueeze(2).to_broadcast([128, 8, 128]), ALU.mult),
                         reads=[("oacc", s), "rsd"], writes=["onb"])
                    b = bank()
                    pv = ps[:, b, :].bitcast(BF16)
                    for h in range(8):
                        P.op("pe", lambda e, pv=pv, h=h: e.transpose(pv[:, h * 128:(h + 1) * 128], onb[:, h * 128:(h + 1) * 128], identb),
                             reads=["onb", "cstb"], writes=[("ps", b)])
                    tc0 = t * 512 + s * 128
                    P.op("dve", lambda e, pv=pv, tc0=tc0, s=s: e.scalar_tensor_tensor(hT[:, :, tc0:tc0 + 128], pv.rearrange("p (a n) -> p a n", a=8), cst[:, K.ONORM:K.ONORM + 1], gT[:, :, s * 128:(s + 1) * 128], ALU.mult, ALU.mult),
                         reads=[("ps", b), "gT", "cst"], writes=[("h", t)])
                P.barrier(chans=[("ohg", 0), ("ohg", 1), ("hgin", 0), ("hgin", 1), ("hginF", 0), ("hginF", 1)])
            resid_proj(dr["hg_w_o"][j], lambda kc, t: hT[:, kc, t * 512:(t + 1) * 512], lambda t: [("h", t)], 16)

        def swa(layer):
            j = layer // 3
            SC = 64 ** -0.5
            P.barrier()
            QsT = carve(0, [128, 16, 512])
            KsT = carve(8192, [128, 4, 512])
            Vs = carve(10240, [128, 4, 256])
            Gt = carve(11264, [128, 4, 1536])
            HK = carve(17408, [128, 2, 4, 128])
            HV = carve(18432, [128, 2, 256])
            KcT = carve(18944, [128, 4, 256])
            Vc = carve(19968, [128, 2, 256])
            bmask = carve(20480, [128, 4, 2, 128])
            agst = carve(21504, [128, 1536])
            B1 = 23040
            QT = carve(B1, [128, 16, 512])
            qraw = carve(B1, [128, 512], F32)
            qt1 = carve(B1 + 1024, [128, 512], F32)
            KT = carve(B1 + 8192, [128, 4, 512])
            Vb = carve(B1 + 10240, [128, 4, 256])
            Pbp = carve(B1 + 11264, [128, 8, 256])
            Pbs = carve(B1 + 11264, [128, 4, 640])
            PTb = carve(B1 + 13824, [128, 20, 128])
            otok = carve(B1 + 16384, [128, 1024])
            st = carve(B1 + 17408, [128, 64], F32)

            wq_d, wk_d, wv_d = kmajor(dr["swa_w_q"][j]), kmajor(dr["swa_w_k"][j]), kmajor(dr["swa_w_v"][j])
            P.dma("pool", lambda e: e.dma_start(out=KcT[0:64, :, :], in_=dr["swa_kctxT"]), "swkc", writes=["KcT"])
            P.dma("pool", lambda e: e.dma_start(out=Vc[:], in_=dr["swa_vctx"].rearrange("(c p) f -> p c f", p=128)), "swvc", writes=["Vc"])
            P.dma("pool", lambda e: e.dma_start(out=bmask[:].rearrange("p a b n -> p (a b n)"), in_=dr["bandmask"]), "swbm", writes=["bmask"])

            def load_qkv_w():
                wq = [wload(wq_d[:, :, pc * 512:(pc + 1) * 512], 8, 512) for pc in range(2)]
                wkv = wload(wk_d, 8, 256)
                wvv = wload(wv_d, 8, 256)
                return wq, wkv, wvv

            def rope_evac(pb_, dst):
                evac(qraw[0:64, :], ps[0:64, pb_, :], [("ps", pb_)], ["qraw"], eng="act")
                b2 = bank()
                P.op("pe", lambda e, b2=b2: e.matmul(ps[0:64, b2, :], cst[0:64, K.PERM:K.PERM + 64], qraw[0:64, :], start=True, stop=True),
                     reads=["qraw", "cst"], writes=[("ps", b2)])
                P.op("dve", lambda e, b2=b2: e.tensor_tensor(qt1[0:64, :], ps[0:64, b2, :], cst[0:64, K.SIN:K.SIN + 512], ALU.mult), reads=[("ps", b2), "cst"], writes=["qt1"])
                P.op("dve", lambda e: e.tensor_tensor(qraw[0:64, :], qraw[0:64, :], cst[0:64, K.COS:K.COS + 512], ALU.mult), reads=["qraw", "cst"], writes=["qraw"])
                P.op("dve", lambda e: e.tensor_tensor(dst, qraw[0:64, :], qt1[0:64, :], ALU.add), reads=["qraw", "qt1"], writes=["ropeout"])

            t = 2
            tsl = slice(1024, 1536)
            state["banks"] = list(range(8))
            wq, (wkt, wkk), (wvt, wvk) = load_qkv_w()
            for hd in range(16):
                wt, wk = wq[hd // 8]
                c0 = (hd % 8) * 64
                b = bank()
                for kc in range(8):
                    P.op("pe", lambda e, b=b, kc=kc, wt=wt, c0=c0, tsl=tsl: e.matmul(ps[0:64, b, :], wt[:, kc, c0:c0 + 64], hT[:, kc, tsl], start=(kc == 0), stop=(kc == 7)),
                         reads=[wk, ("h", 2)], writes=[("ps", b)])
                rope_evac(b, QsT[0:64, hd, :])
            for kvh in range(4):
                b = bank()
                for kc in range(8):
                    P.op("pe", lambda e, b=b, kc=kc, kvh=kvh, tsl=tsl, wkt=wkt: e.matmul(ps[0:64, b, :], wkt[:, kc, kvh * 64:(kvh + 1) * 64], hT[:, kc, tsl], start=(kc == 0), stop=(kc == 7)),
                         reads=[wkk, ("h", 2)], writes=[("ps", b)])
                rope_evac(b, KsT[0:64, kvh, :])
            for s in range(4):
                b = bank()
                for kc in range(8):
                    P.op("pe", lambda e, b=b, kc=kc, s=s, wvt=wvt: e.matmul(ps[:, b, 0:256], hT[:, kc, 1024 + s * 128:1024 + (s + 1) * 128], wvt[:, kc, :], start=(kc == 0), stop=(kc == 7)),
                         reads=[wvk, ("h", 2)], writes=[("ps", b)])
                evac(Vs[:, s, :], ps[:, b, 0:256], [("ps", b)], ["Vs"])
            P.op("dve", lambda e: e.memset(agst[:], 0.0), writes=["agst"])
            P.op("dve", lambda e: e.tensor_copy(agst[0:64, 0:512].rearrange("p (a n) -> p a n", a=4), KsT[0:64, :, 0:128]), reads=["ropeout"], writes=["agst"])
            P.op("dve", lambda e: e.tensor_copy(agst[0:64, 512:1024].rearrange("p (a n) -> p a n", a=4), KsT[0:64, :, 384:512]), reads=["ropeout"], writes=["agst"])
            P.op("dve", lambda e: e.tensor_copy(agst[:, 1024:1280], Vs[:, 0, :]), reads=["Vs"], writes=["agst"])
            P.op("dve", lambda e: e.tensor_copy(agst[:, 1280:1536], Vs[:, 3, :]), reads=["Vs"], writes=["agst"])
            P.dma("sp", lambda e: e.dma_start(out=swin.ap(), in_=agst[:]), "swin", reads=["agst"], writes=["swin"])
            P.coll(lambda e: e.collective_compute("AllGather", ALU.bypass, replica_groups=[[0, 1, 2, 3], [4, 5, 6, 7]],
                                                  ins=[swin.ap().opt()], outs=[swout.ap().opt()]),
                   "swc", reads=["swin"], writes=["swout"])
            P.dma("sp", lambda e: e.dma_start(out=Gt[:], in_=swout.ap().rearrange("(r p) n -> p r n", p=128)), "swg", reads=["swout"], writes=["Gt"])
            P.barrier()

            cfgp = {"S": [0, 1, 2, 3], "O": (4, 0), "PT": [5]}
            for t in (range(2) if flags.get("swa_parts", 7) & 2 else []):
                tsl = slice(t * 512, (t + 1) * 512)
                state["banks"] = [6, 7]
                wq, (wkt, wkk), (wvt, wvk) = load_qkv_w()
                for hd in range(16):
                    wt, wk = wq[hd // 8]
                    c0 = (hd % 8) * 64
                    b = bank()
                    for kc in range(8):
                        P.op("pe", lambda e, b=b, kc=kc, wt=wt, c0=c0, tsl=tsl: e.matmul(ps[0:64, b, :], wt[:, kc, c0:c0 + 64], hT[:, kc, tsl], start=(kc == 0), stop=(kc == 7)),
                             reads=[wk, ("h", t)], writes=[("ps", b)])
                    evac(QT[0:64, hd, :], ps[0:64, b, :], [("ps", b)], ["QT"])
                for kvh in range(4):
                    b = bank()
                    for kc in range(8):
                        P.op("pe", lambda e, b=b, kc=kc, kvh=kvh, tsl=tsl, wkt=wkt: e.matmul(ps[0:64, b, :], wkt[:, kc, kvh * 64:(kvh + 1) * 64], hT[:, kc, tsl], start=(kc == 0), stop=(kc == 7)),
                             reads=[wkk, ("h", t)], writes=[("ps", b)])
                    evac(KT[0:64, kvh, :], ps[0:64, b, :], [("ps", b)], ["KT"])
                for s in (range(4) if not flags.get("nokvtok") else []):
                    b = bank()
                    b2 = bank()
                    tcs = slice(t * 512 + s * 128, t * 512 + (s + 1) * 128)
                    for kc in range(8):
                        P.op("pe", lambda e, b=b, kc=kc, tcs=tcs, wkt=wkt: e.matmul(ps[:, b, 0:256], hT[:, kc, tcs], wkt[:, kc, :], start=(kc == 0), stop=(kc == 7)),
                             reads=[wkk, ("h", t)], writes=[("ps", b)])
                    for kc in range(8):
                        P.op("pe", lambda e, b2=b2, kc=kc, tcs=tcs, wvt=wvt: e.matmul(ps[:, b2, 0:256], hT[:, kc, tcs], wvt[:, kc, :], start=(kc == 0), stop=(kc == 7)),
                             reads=[wvk, ("h", t)], writes=[("ps", b2)])
                    sl = s % 2
                    kvl = flags.get("kvlevel", 3)
                    if kvl >= 2:
                        evac(stage[:, sl, 0:256], ps[:, b, 0:256], [("ps", b)], [("stage", sl)], eng="act")
                        evac(stage[:, sl, 256:512], ps[:, b2, 0:256], [("ps", b2)], [("stage", sl)], eng="act")
                    if kvl >= 3:
                        evac(Vb[:, s, :], stage[:, sl, 256:512], [("stage", sl)], ["Vb"], eng="dve")
                    gtok = t * 512 + s * 128
                    sq_, r0 = gtok // 256, gtok % 256
                    if not flags.get("nokvout"):
                        P.dma("sp", lambda e, sl=sl, sq_=sq_, r0=r0: e.dma_start(out=dr["o_k"][sq_, j, r0:r0 + 128, :], in_=stage[:, sl, 0:256]),
                              ("ost", sl), reads=[("stage", sl)], final=True)
                        P.dma("sp", lambda e, sl=sl, sq_=sq_, r0=r0: e.dma_start(out=dr["o_v"][sq_, j, r0:r0 + 128, :], in_=stage[:, sl, 256:512]),
                              ("ost2", sl), reads=[("stage", sl)], final=True)
                for s_ in (range(2) if not flags.get("noattn") else []):
                    for qt in range(2):
                        q0 = s_ * 256 + qt * 128
                        for hg_ in range(2):
                            def score_ops(g, off, n, q0=q0, s_=s_, hg_=hg_):
                                hd = hg_ * 8 + g
                                return [(QT[0:64, hd, q0:q0 + 128], KT[0:64, hd // 4, s_ * 256 + off:s_ * 256 + off + n], ["QT", "KT"])]

                            def v_ops(g, kc, nk, s_=s_, hg_=hg_):
                                hd = hg_ * 8 + g
                                return Vb[0:nk, s_ * 2 + kc, (hd // 4) * 64:(hd // 4 + 1) * 64], ["Vb"]

                            p2_ = attn_core(8, 256, 256, [(0, 256)], score_ops, v_ops, 64, SC,
                                      lambda g, hg_=hg_: otok[:, (hg_ * 8 + g) * 64:(hg_ * 8 + g + 1) * 64], ["otok"], cfgp,
                                      lambda g: Pbp[:, g, :], lambda i0, n_: PTb[:, i0:i0 + n_, :], st,
                                      sinkv=(cst[:, K.SINK + hg_ * 8:K.SINK + hg_ * 8 + 8] if not flags.get("nosink") else None))
                            p2_()
                        otok_to_oT(otok, ["otok"], t * 512 + q0, cfgp)
            P.barrier()

            so = K.SEL
            for side, (koff, voff) in enumerate(((512, 1280), (0, 1024))):
                for r in range(4):
                    sc_ = cst[:, so + side * 4 + r:so + side * 4 + r + 1]
                    kin = Gt[0:64, r, koff:koff + 512].rearrange("p (a n) -> p a n", a=4)
                    vin = Gt[:, r, voff:voff + 256]
                    if r == 0:
                        P.op("dve", lambda e, sc_=sc_, kin=kin, side=side: e.tensor_scalar(HK[0:64, side, :, :], kin, sc_[0:64, :], None, ALU.mult), reads=["Gt", "cst"], writes=["HK"])
                        P.op("dve", lambda e, sc_=sc_, vin=vin, side=side: e.tensor_scalar(HV[:, side, :], vin, sc_, None, ALU.mult), reads=["Gt", "cst"], writes=["HV"])
                    else:
                        P.op("dve", lambda e, sc_=sc_, kin=kin, side=side: e.scalar_tensor_tensor(HK[0:64, side, :, :], kin, sc_[0:64, :], HK[0:64, side, :, :], ALU.mult, ALU.add), reads=["Gt", "cst", "HK"], writes=["HK"])
                        P.op("dve", lambda e, sc_=sc_, vin=vin, side=side: e.scalar_tensor_tensor(HV[:, side, :], vin, sc_, HV[:, side, :], ALU.mult, ALU.add), reads=["Gt", "cst", "HV"], writes=["HV"])
            cfgs = {"S": [0, 1, 2, 3, 4, 5], "O": (6, 0), "PT": [7]}
            hoffs = [0, 640, 1280, 2048]
            blocks = [(0, 256), (256, 128), (384, 128), (512, 128)]
            for jj in (range(4) if flags.get("swa_parts", 7) & 4 else []):
                q0 = jj * 128
                for kvh in range(4):
                    def kband(which, jj=jj, kvh=kvh):
                        bi = jj - 1 + which
                        if bi < 0:
                            return HK[0:64, 0, kvh, :], HV[:, 0, kvh * 64:(kvh + 1) * 64], ["HK", "HV"]
                        if bi > 3:
                            return HK[0:64, 1, kvh, :], HV[:, 1, kvh * 64:(kvh + 1) * 64], ["HK", "HV"]
                        return KsT[0:64, kvh, bi * 128:(bi + 1) * 128], Vs[:, bi, kvh * 64:(kvh + 1) * 64], ["ropeout", "Vs"]

                    def score_ops(g, off, n, jj=jj, kvh=kvh, q0=q0):
                        hd = kvh * 4 + g
                        qa = QsT[0:64, hd, q0:q0 + 128]
                        if off == 0:
                            return [(qa, KcT[0:64, kvh, :], ["ropeout", "KcT"])]
                        which = (off - 256) // 128
                        ka, _, rk = kband(which)
                        ops_ = [(qa, ka, ["ropeout"] + rk)]
                        if which != 1:
                            ops_.append((identb, bmask[:, jj, 0 if which == 0 else 1, :], ["cstb", "bmask"]))
                        return ops_

                    def v_ops(g, kc, nk, kvh=kvh):
                        if kc < 2:
                            return Vc[:, kc, kvh * 64:(kvh + 1) * 64], ["Vc"]
                        _, va, rk = kband(kc - 2)
                        return va, rk

                    p2_ = attn_core(4, 640, hoffs, blocks, score_ops, v_ops, 64, SC,
                              lambda g, kvh=kvh: otok[:, (kvh * 4 + g) * 64:(kvh * 4 + g + 1) * 64], ["otok"], cfgs,
                              lambda g: Pbs[:, g, :], lambda i0, n_: PTb[:, i0:i0 + n_, :], st,
                              sinkv=cst[:, K.SINK + kvh * 4:K.SINK + kvh * 4 + 4])
                    p2_()
                otok_to_oT(otok, ["otok"], 1024 + q0, cfgs)
            state["banks"] = list(range(8))
            P.barrier()
            resid_proj(dr["swa_w_o"][j], lambda kc, t: hT[:, kc, t * 512:(t + 1) * 512], lambda t: [("h", t)], 16)

        def final():
            for t in range(NT):
                yT = ytmp
                norm_tile(t,
                          lambda c: cst[:, K.NFIN + c:K.NFIN + c + 1],
                          lambda c: 0.0,
                          lambda c: yT[:, c, :],
                          lambda c: [("ytmp", c)])
                for q in range(4):
                    tt = t * 4 + q
                    sl = tt % 2
                    for half in range(2):
                        b = bank()
                        for j in range(4):
                            c = half * 4 + j
                            P.op("pe", lambda e, b=b, j=j, c=c, q=q: e.transpose(ps[:, b, j * 128:(j + 1) * 128], yT[:, c, q * 128:(q + 1) * 128], ident),
                                 reads=[("ytmp", c), "cst"], writes=[("ps", b)])
                        evac(stage[:, sl, half * 512:(half + 1) * 512], ps[:, b, :], [("ps", b)], [("stage", sl)])
                    P.dma("sp", lambda e, tt=tt, sl=sl: e.dma_start(out=dr["y"][tt * 128:(tt + 1) * 128, :], in_=stage[:, sl, :]),
                          ("ost", sl), reads=[("stage", sl)], final=True)

        for layer in range(depth):
            if layer == 0:
                for pc in range(12):
                    adaln_piece(0, pc)
            adaln_finish(layer)
            if mixers:
                kind = layer % 3
                if kind == 0:
                    modnorm(0)
                    mla(layer)
                elif kind == 1:
                    modnorm(0)
                    hgrn(layer)
                else:
                    modnorm(0)
                    swa(layer)
            modnorm(1)
            ffn(layer, ada_next=(layer + 1 if layer + 1 < depth else None))
        final()
        import os as _os
        if _os.environ.get("KDEBUG"):
            print("OPS", {e: len(v) for e, v in P.ops.items()}, "waits", {e: sum(len(w) for w, _, _ in v) for e, v in P.ops.items()}, "nsem", P.nsem)
        P.emit()
    return nc


def fm(v):
    v = np.asarray(v, np.float32)
    lead = v.shape[:-1]
    a = v.reshape(*lead, v.shape[-1] // 128, 128)
    a = np.moveaxis(a, -1, 0)
    return np.ascontiguousarray(a).reshape(128, -1)


def rope_tables(pos0):
    pos = np.arange(pos0, pos0 + NS_TOK)
    row = (pos // 64).astype(np.float32)
    col = (pos % 64).astype(np.float32)
    inv = (10000.0 ** (-np.arange(16, dtype=np.float32) / 16)).astype(np.float32)
    cos = np.zeros((64, NS_TOK), np.float32)
    sin = np.zeros((64, NS_TOK), np.float32)
    perm = np.zeros((64, 64), np.float32)
    for i in range(64):
        p = row if i < 32 else col
        ii = i % 32
        f = ii % 16
        first = ii < 16
        ang = (p * inv[f]).astype(np.float32)
        cos[i] = np.cos(ang)
        sin[i] = -np.sin(ang) if first else np.sin(ang)
        sw = i + 16 if first else i - 16
        perm[sw, i] = 1.0
    return cos, sin, perm


def make_consts(inp, core):
    g, q = core // 4, core % 4
    c = np.zeros((128, NCONST), np.float32)
    c[:, K.IDENT:K.IDENT + 128] = np.eye(128, dtype=np.float32)
    c[:, K.ONESN:K.ONESN + 128] = 1.0 / 1024.0
    cv = np.stack([inp["c_ctx"], inp["c"][g]], 0)
    c[:, K.CVEC:K.CVEC + 16] = np.ascontiguousarray(cv.reshape(2, 8, 128).transpose(2, 1, 0)).reshape(128, 16)
    ab = inp["ada_b"].reshape(DEPTH, 48, 128).transpose(2, 0, 1).reshape(128, DEPTH * 48)
    c[:, K.ADAB:K.ADAB + DEPTH * 48] = ab
    c[:, K.NMIX:K.NMIX + 32] = fm(inp["norm_mix"])
    c[:, K.NFFN:K.NFFN + 32] = fm(inp["norm_ffn"])
    c[:, K.NFIN:K.NFIN + 8] = fm(inp["final_norm"])
    c[:, K.QNORM:K.QNORM + 8] = fm(inp["mla_q_norm"])
    c[:, K.KVNORM:K.KVNORM + 4] = fm(inp["mla_kv_norm"])
    cos, sin, perm = rope_tables(q * NS_TOK)
    c[0:64, K.PERM:K.PERM + 64] = perm
    c[0:64, K.COS:K.COS + 512] = cos
    c[0:64, K.SIN:K.SIN + 512] = sin
    s_ = np.arange(128)[:, None]
    t_ = np.arange(128)[None, :]
    same = (s_ // 64) == (t_ // 64)
    c[:, K.MASKF:K.MASKF + 128] = (same & (s_ <= t_)).astype(np.float32)
    c[:, K.MASKB:K.MASKB + 128] = (same & (s_ >= t_)).astype(np.float32)
    lbl = inp["hg_lb_logits"].reshape(2, DEPTH, 8, 128).transpose(3, 0, 2, 1)
    c[:, K.LBL:K.LBL + 64] = np.ascontiguousarray(lbl).reshape(128, 64)
    c[:, K.ONORM] = inp["hg_o_norm"][0]
    r = np.arange(4)
    c[:, K.RMASK + 0:K.RMASK + 4] = (r < q).astype(np.float32)[None, :]
    c[:, K.RMASK + 4:K.RMASK + 8] = 1.0 - (r < q).astype(np.float32)[None, :]
    c[:, K.RMASK + 8:K.RMASK + 12] = (r > q).astype(np.float32)[None, :]
    c[:, K.RMASK + 12:K.RMASK + 16] = 1.0 - (r > q).astype(np.float32)[None, :]
    c[:, K.SINK:K.SINK + 16] = inp["swa_sink"][0][None, :]
    c[:, K.SEL + 0:K.SEL + 4] = (r == q - 1).astype(np.float32)[None, :]
    c[:, K.SEL + 4:K.SEL + 8] = (r == q + 1).astype(np.float32)[None, :]
    return c


def make_bandmask(core):
    q = core % 4
    qq = np.arange(128)[:, None]
    kk = np.arange(128)[None, :]
    m = np.zeros((128, 4, 2, 128), np.float32)
    for jj in range(4):
        bq = 4 * q + jj
        prev_ok = (kk >= qq) & (bq >= 1)
        next_ok = (kk <= qq) & (bq <= 14)
        m[:, jj, 0, :] = np.where(prev_ok, 0.0, -1e30)
        m[:, jj, 1, :] = np.where(next_ok, 0.0, -1e30)
    return m.reshape(128, 1024)


_CACHE = {}


def run(inputs, flags):
    inp = {k: np.asarray(v) for k, v in inputs.items()}
    key = tuple(sorted(flags.items()))
    if key not in _CACHE:
        _CACHE[key] = build(flags)
    nc = _CACHE[key]
    in_maps = []
    for core in range(NCORES):
        g, q = core // 4, core % 4
        xin = np.concatenate([inp["x_prompt"][4 * core:4 * core + 4].reshape(NP_TOK, D),
                              inp["x_sample"][g, q * NS_TOK:(q + 1) * NS_TOK]], 0)
        m = {"xin": np.ascontiguousarray(xin, dtype=np.float32), "consts": make_consts(inp, core)}
        ck = inp["cache_mla_ckv"][g]
        m["ckv_ctxT"] = np.ascontiguousarray(ck.reshape(2, 256, 2, 128).transpose(0, 3, 2, 1), dtype=np.float32)
        m["krope_ctxT"] = np.ascontiguousarray(inp["cache_mla_krope"][g].transpose(0, 2, 1), dtype=np.float32)
        m["hg_s0"] = np.ascontiguousarray(inp["state_hgrn"][g, 0], dtype=np.float32)
        m["swa_kctxT"] = np.ascontiguousarray(inp["cache_swa_k"][g, 0].transpose(2, 1, 0), dtype=np.float32)
        m["swa_vctx"] = np.ascontiguousarray(inp["cache_swa_v"][g, 0].reshape(256, 256), dtype=np.float32)
        m["bandmask"] = make_bandmask(core)
        for k in W_SPECS:
            m[k] = np.ascontiguousarray(inp[k], dtype=np.float32)
        in_maps.append(m)
    res = run_bass_kernel_spmd(nc, in_maps, core_ids=list(range(NCORES)))
    return res.results


def assemble(res):
    y_p = np.zeros((32, 256, D), np.float32)
    y_s = np.zeros((2, 2048, D), np.float32)
    ckv = np.zeros((32, 2, 256, 256), np.float32)
    krope = np.zeros((32, 2, 256, 64), np.float32)
    hg = np.zeros((32, 1, 2, 8, 128, 128), np.float32)
    ok = np.zeros((32, 1, 256, 4, 64), np.float32)
    ov = np.zeros((32, 1, 256, 4, 64), np.float32)
    for core in range(NCORES):
        g, q = core // 4, core % 4
        y = res[core]["y"]
        y_p[4 * core:4 * core + 4] = y[:NP_TOK].reshape(4, 256, D)
        y_s[g, q * NS_TOK:(q + 1) * NS_TOK] = y[NP_TOK:]
        ckv[4 * core:4 * core + 4] = res[core]["o_ckv"]
        krope[4 * core:4 * core + 4] = res[core]["o_krope"]
        hg[4 * core:4 * core + 4] = res[core]["o_hg"]
        ok[4 * core:4 * core + 4] = res[core]["o_k"].reshape(4, 1, 256, 4, 64)
        ov[4 * core:4 * core + 4] = res[core]["o_v"].reshape(4, 1, 256, 4, 64)
    return y_p, y_s, ckv, krope, hg, ok, ov


def kernel(**inputs):
    res = run(inputs, {})
    return assemble(res)
```

```python
import numpy as np
from contextlib import ExitStack
import concourse.bass as bass
import concourse.mybir as mybir
from concourse.bass_utils import run_bass_kernel_spmd

F32 = mybir.dt.float32
BF16 = mybir.dt.bfloat16
AF = mybir.ActivationFunctionType
ALU = mybir.AluOpType
AX = mybir.AxisListType

D = 1024
DFF = 2816
DEPTH = 4
NP_TOK = 1024
NS_TOK = 512
T = NP_TOK + NS_TOK
NT = T // 512
EPS = 1e-6
NCORES = 8


class Tok:
    __slots__ = ("sem", "val", "eng")

    def __init__(self, sem, val, eng):
        self.sem, self.val, self.eng = sem, val, eng


class _Rec:
    def __init__(self):
        self.call = None

    def __getattr__(self, name):
        def f(*a, **k):
            assert self.call is None
            self.call = (name, a, k)
            return None
        return f


def _freeze(fn):
    r = _Rec()
    fn(r)
    name, a, k = r.call
    return lambda e: getattr(e, name)(*a, **k)


class Prog:
    ENG = ("pe", "act", "dve", "pool", "sp")
    EPOCH = 6000

    def __init__(self, nc, stack):
        self.nc = nc
        self.stack = stack
        self.ops = {e: [] for e in self.ENG}
        self.cur_sem = {}
        self.cur_cnt = {}
        for e in self.ENG:
            self._new_epoch(e)
        self.known = {e: {} for e in self.ENG}
        self.last_w = {}
        self.readers = {}
        self.dma_sem = {}
        self.dma_cnt = {}
        self.nsem = 0
        self.final_toks = []

    def _sem(self, name):
        self.nsem = getattr(self, "nsem", 0) + 1
        return self.stack.enter_context(self.nc.semaphore(name))

    def _new_epoch(self, e):
        self._ep = getattr(self, "_ep", 0) + 1
        self.cur_sem[e] = self._sem(f"c_{e}_{self._ep}")
        self.cur_cnt[e] = 0

    def _waits_for(self, eng, reads, writes):
        toks = []
        for k in reads:
            t = self.last_w.get(k)
            if t is not None:
                toks.append(t)
        for k in writes:
            t = self.last_w.get(k)
            if t is not None:
                toks.append(t)
            for t in self.readers.get(k, {}).values():
                toks.append(t)
        need = {}
        for t in toks:
            if t.eng == eng:
                if eng == "pe":
                    continue
                if t.sem is self.cur_sem[eng] and t.val < self.cur_cnt[eng] - 1:
                    continue
                if t.sem is not self.cur_sem[eng]:
                    continue
            kn = self.known[eng].get(id(t.sem), 0)
            if kn >= t.val:
                continue
            if need.get(id(t.sem), (None, 0))[1] < t.val:
                need[id(t.sem)] = (t.sem, t.val)
        out = []
        for sid, (s, v) in need.items():
            self.known[eng][sid] = v
            out.append((s, v))
        return out

    def _record(self, tok, reads, writes, rkey):
        for k in writes:
            self.last_w[k] = tok
            self.readers[k] = {}
        for k in reads:
            self.readers.setdefault(k, {})[rkey] = tok

    def op(self, eng, fn, reads=(), writes=()):
        if self.cur_cnt[eng] >= self.EPOCH:
            self._new_epoch(eng)
        waits = self._waits_for(eng, reads, writes)
        self.cur_cnt[eng] += 1
        tok = Tok(self.cur_sem[eng], self.cur_cnt[eng], eng)
        self.ops[eng].append((waits, _freeze(fn), (tok.sem, 1)))
        self._record(tok, reads, writes, eng)
        return tok

    def dma(self, q, fn, chan, reads=(), writes=(), final=False):
        if chan not in self.dma_sem:
            self.dma_sem[chan] = self._sem("d_" + str(len(self.dma_sem)))
            self.dma_cnt[chan] = 0
        waits = self._waits_for(q, reads, writes)
        self.dma_cnt[chan] += 16
        tok = Tok(self.dma_sem[chan], self.dma_cnt[chan], "dma")
        self.ops[q].append((waits, _freeze(fn), (tok.sem, 16)))
        self._record(tok, reads, writes, ("dma", chan))
        if final:
            self.final_toks.append(tok)
        return tok

    def coll(self, fn, chan, reads=(), writes=()):
        if chan not in self.dma_sem:
            self.dma_sem[chan] = self._sem("cc_" + str(len(self.dma_sem)))
            self.dma_cnt[chan] = 0
        waits = self._waits_for("pool", reads, writes)
        self.dma_cnt[chan] += 1
        tok = Tok(self.dma_sem[chan], self.dma_cnt[chan], "dma")
        self.ops["pool"].append((waits, _freeze(fn), (tok.sem, None)))
        self._record(tok, reads, writes, ("dma", chan))
        return tok

    def barrier(self, chans=()):
        toks = [Tok(self.cur_sem[e], self.cur_cnt[e], e) for e in self.ENG if self.cur_cnt[e] > 0]
        for c in chans:
            if c in self.dma_sem:
                toks.append(Tok(self.dma_sem[c], self.dma_cnt[c], "dma"))
        for e in self.ENG:
            waits = []
            for t in toks:
                if t.eng == e:
                    continue
                if self.known[e].get(id(t.sem), 0) >= t.val:
                    continue
                self.known[e][id(t.sem)] = t.val
                waits.append((t.sem, t.val))
            if waits:
                self.ops[e].append((waits, None, None))

    def emit(self):
        nc = self.nc
        fin = {}
        for t in self.final_toks:
            if fin.get(id(t.sem), (None, 0))[1] < t.val:
                fin[id(t.sem)] = (t.sem, t.val)
        self.ops["sp"].append((list(fin.values()), None, None))
        with nc.Block() as block:
            def run(eng_obj, lst):
                for waits, fn, inc in lst:
                    for s, v in waits:
                        eng_obj.wait_ge(s, v)
                    if fn is not None:
                        ins = fn(eng_obj)
                        if inc is not None:
                            if inc[1] is None:
                                ins.then_inc(inc[0])
                            else:
                                ins.then_inc(inc[0], inc[1])

            @block.tensor
            def _(e):
                run(e, self.ops["pe"])

            @block.scalar
            def _(e):
                run(e, self.ops["act"])

            @block.vector
            def _(e):
                run(e, self.ops["dve"])

            @block.gpsimd
            def _(e):
                run(e, self.ops["pool"])

            @block.sync
            def _(e):
                run(e, self.ops["sp"])


W_SPECS = {
    "ada_w": [DEPTH, D, 6 * D], "ffn_w_gate": [DEPTH, D, DFF], "ffn_w_up": [DEPTH, D, DFF],
    "ffn_w_down": [DEPTH, DFF, D],
    "mla_w_dq": [2, D, 512], "mla_w_uq": [2, 512, 1536], "mla_w_dkv": [2, D, 320],
    "mla_w_uk": [2, 256, 1024], "mla_w_uv": [2, 256, 1024], "mla_w_o": [2, 1024, D],
    "swa_w_q": [1, D, D], "swa_w_k": [1, D, 256], "swa_w_v": [1, D, 256], "swa_w_o": [1, D, D],
    "hg_w_q": [1, D, D], "hg_w_f": [1, 2, D, D], "hg_w_i": [1, D, D], "hg_w_g": [1, D, D], "hg_w_o": [1, D, D],
}
NCONST = 2048
AR = 40960


class K:
    IDENT = 0
    ONESN = 128
    MASKF = 256
    MASKB = 384
    CVEC = 512
    ADAB = 528
    NMIX = 720
    NFFN = 752
    NFIN = 784
    QNORM = 792
    KVNORM = 800
    PERM = 804
    COS = 868
    SIN = 1380
    LBL = 1892
    ONORM = 1956
    RMASK = 1957
    SINK = 1973
    SEL = 1989
    END = 1997


def build(flags):
    nc = bass.Bass("TRN2", target_bir_lowering=False)
    stack = ExitStack()
    depth = flags.get("depth", DEPTH)
    mixers = flags.get("mixers", 1)
    with stack:
        P = Prog(nc, stack)
        dr = {}
        dr["xin"] = nc.dram_tensor("xin", [T, D], F32, kind="ExternalInput").ap()
        dr["consts"] = nc.dram_tensor("consts", [128, NCONST], F32, kind="ExternalInput").ap()
        dr["ckv_ctxT"] = nc.dram_tensor("ckv_ctxT", [2, 128, 2, 256], F32, kind="ExternalInput").ap()
        dr["krope_ctxT"] = nc.dram_tensor("krope_ctxT", [2, 64, 256], F32, kind="ExternalInput").ap()
        for k, shp in W_SPECS.items():
            dr[k] = nc.dram_tensor(k, shp, F32, kind="ExternalInput").ap()
        dr["y"] = nc.dram_tensor("y", [T, D], F32, kind="ExternalOutput").ap()
        dr["o_ckv"] = nc.dram_tensor("o_ckv", [4, 2, 256, 256], F32, kind="ExternalOutput").ap()
        dr["o_krope"] = nc.dram_tensor("o_krope", [4, 2, 256, 64], F32, kind="ExternalOutput").ap()
        dr["swa_kctxT"] = nc.dram_tensor("swa_kctxT", [64, 4, 256], F32, kind="ExternalInput").ap()
        dr["swa_vctx"] = nc.dram_tensor("swa_vctx", [256, 256], F32, kind="ExternalInput").ap()
        dr["bandmask"] = nc.dram_tensor("bandmask", [128, 1024], F32, kind="ExternalInput").ap()
        dr["o_k"] = nc.dram_tensor("o_k", [4, 1, 256, 256], F32, kind="ExternalOutput").ap()
        dr["o_v"] = nc.dram_tensor("o_v", [4, 1, 256, 256], F32, kind="ExternalOutput").ap()
        swin = nc.dram_tensor("swin", [128, 1536], BF16)
        swout = nc.dram_tensor("swout", [4 * 128, 1536], BF16)
        dr["hg_s0"] = nc.dram_tensor("hg_s0", [2, 8, 128, 128], F32, kind="ExternalInput").ap()
        dr["o_hg"] = nc.dram_tensor("o_hg", [4, 1, 2, 8, 128, 128], F32, kind="ExternalOutput").ap()
        hgin = [nc.dram_tensor(f"hgin{d_}", [128, 1032], F32) for d_ in range(2)]
        hgout = [nc.dram_tensor(f"hgout{d_}", [4 * 128, 1032], F32) for d_ in range(2)]
        agin = [nc.dram_tensor(f"agin{j}", [128, 1536], BF16) for j in range(2)]
        agout = [nc.dram_tensor(f"agout{j}", [4 * 128, 1536], BF16) for j in range(2)]

        def sb(name, shape, dt):
            return stack.enter_context(nc.sbuf_tensor(name, shape, dt))

        xT = sb("xT", [128, 8, T], F32)
        hT = sb("hT", [128, 8, T], BF16)
        cst = sb("cst", [128, NCONST], F32)
        cstb = sb("cstb", [128, 512], BF16)
        NSLOT = 4
        wring = sb("wring", [128, NSLOT, 4096], BF16)
        mod = sb("mod", [128, 2, 48], F32)
        gm = sb("gm", [128, 2, 2, 8], F32)
        silc = sb("silc", [128, 16], BF16)
        rstd = sb("rstd", [128, 512], F32)
        stage = sb("stage", [128, 2, 1024], F32)
        hgs = sb("hgs", [128, 128], F32)
        hgs2 = sb("hgs2", [128, 32], F32)
        lbw = sb("lbw", [128, 144], F32)
        ones64 = sb("ones64", [128, 64], F32)
        A = sb("arena", [128, AR], BF16)
        ps = stack.enter_context(nc.psum_tensor("ps", [128, 8, 512], F32))

        def carve(off, shape, dt=BF16):
            n = 1
            for s_ in shape[1:]:
                n *= s_
            if dt == F32:
                v = A[:, off:off + 2 * n].bitcast(F32)
            else:
                v = A[:, off:off + n]
            if len(shape) == 3:
                v = v.rearrange("p (a b) -> p a b", a=shape[1])
            elif len(shape) == 4:
                v = v.rearrange("p (a b c) -> p a b c", a=shape[1], b=shape[2])
            return v

        sq_default = carve(0, [128, 8, 512])
        ytmp = carve(4096, [128, 8, 512], F32)
        hid = carve(12288, [128, 2, 8, T])
        sgt = carve(36864, [128, 2, 512], F32)

        ident = cst[:, K.IDENT:K.IDENT + 128]
        identb = cstb[:, 0:128]
        onesb = cstb[:, 128:256]

        state = {"bank": 0, "slot": 0, "ev": 0, "banks": list(range(8))}

        def bank():
            bl = state["banks"]
            b = bl[state["bank"] % len(bl)]
            state["bank"] += 1
            return b

        def evac(dst, src, reads, writes, eng=None):
            if eng is None:
                eng = "act" if state["ev"] % 2 == 0 else "dve"
                state["ev"] += 1
            if eng == "act":
                return P.op("act", lambda e: e.copy(dst, src), reads=reads, writes=writes)
            return P.op("dve", lambda e: e.tensor_copy(dst, src), reads=reads, writes=writes)

        P.dma("sp", lambda e: e.dma_start(out=cst[:], in_=dr["consts"]), "cst", writes=["cst"])
        P.dma("pool", lambda e: e.dma_start(out=cstb[:], in_=dr["consts"][:, 0:512]), "cstb", writes=["cstb"])

        def wload(src, a, b):
            s = state["slot"]
            state["slot"] = (s + 1) % NSLOT
            assert a * b <= 4096
            dst = wring[:, s, 0:a * b].rearrange("p (a b) -> p a b", a=a)
            P.dma("pool", lambda e: e.dma_start(out=dst, in_=src), ("w", s), writes=[("w", s)])
            return dst, ("w", s)

        def kmajor(w2d):
            return w2d.rearrange("(kc p) n -> p kc n", p=128)

        for tt in range(T // 128):
            sl = tt % 2
            P.dma("sp", lambda e, tt=tt, sl=sl: e.dma_start(out=stage[:, sl, :], in_=dr["xin"][tt * 128:(tt + 1) * 128, :]),
                  ("stage", sl), writes=[("stage", sl)])
            for half in range(2):
                b = bank()
                for j in range(4):
                    c = half * 4 + j
                    P.op("pe", lambda e, b=b, j=j, c=c, sl=sl: e.transpose(ps[:, b, j * 128:(j + 1) * 128], stage[:, sl, c * 128:(c + 1) * 128], ident),
                         reads=[("stage", sl), "cst"], writes=[("ps", b)])
                evac(xT[:, half * 4:half * 4 + 4, tt * 128:(tt + 1) * 128], ps[:, b, :].rearrange("p (j n) -> p j n", j=4),
                     [("ps", b)], [("x", tt // 4)])

        def cond_of(t):
            return 0 if t < 2 else 1

        ADA_BANK = 7

        def adaln_piece(layer, pc):
            if layer == 0 and pc == 0:
                P.op("act", lambda e: e.activation(silc[:], cst[:, K.CVEC:K.CVEC + 16], AF.Silu), reads=["cst"], writes=["silc"])
            b = ADA_BANK
            wv = kmajor(dr["ada_w"][layer])
            wt, wk = wload(wv[:, :, pc * 512:(pc + 1) * 512], 8, 512)
            for jc in range(4):
                j = pc * 4 + jc
                for kc in range(8):
                    P.op("pe", lambda e, wt=wt, jc=jc, kc=kc, j=j, b=b: e.matmul(ps[:, b, 2 * j:2 * j + 2], wt[:, kc, jc * 128:(jc + 1) * 128], silc[:, 2 * kc:2 * kc + 2], start=(kc == 0), stop=(kc == 7)),
                         reads=[wk, "silc"], writes=[("ps", b)])

        def adaln_finish(layer):
            b = ADA_BANK
            for c in range(2):
                P.op("dve", lambda e, c=c, b=b: e.tensor_tensor(mod[:, c, :], ps[:, b, 0:96].rearrange("p (j c) -> p j c", c=2)[:, :, c], cst[:, K.ADAB + layer * 48:K.ADAB + (layer + 1) * 48], ALU.add),
                     reads=[("ps", b), "cst"], writes=["mod"])
            for n, (goff, so) in enumerate(((K.NMIX, 8), (K.NFFN, 32))):
                for c in range(2):
                    P.op("dve", lambda e, n=n, c=c, goff=goff, so=so: e.scalar_tensor_tensor(gm[:, n, c, :], mod[:, c, so:so + 8], 1.0, cst[:, goff + layer * 8:goff + layer * 8 + 8], ALU.add, ALU.mult),
                         reads=["mod", "cst"], writes=["gm"])

        def rms_rstd(src_fn, nch, srckeys, mscale, sq=None):
            if sq is None:
                sq = sq_default
            P.op("act", lambda e: e.activation(sq[:, 0:nch, :], src_fn(), AF.Square), reads=srckeys, writes=["sq"])
            b = bank()
            for c in range(nch):
                P.op("pe", lambda e, c=c, b=b: e.matmul(ps[:, b, :], onesb, sq[:, c, :], start=(c == 0), stop=(c == nch - 1)),
                     reads=["sq", "cstb"], writes=[("ps", b)])
            P.op("act", lambda e, b=b: e.activation(rstd[:], ps[:, b, :], AF.Sqrt, bias=EPS, scale=mscale), reads=[("ps", b)], writes=["rstd"])
            P.op("dve", lambda e: e.reciprocal(rstd[:], rstd[:]), reads=["rstd"], writes=["rstd"])

        def norm_tile(t, gain_fn, shift_fn, out_fn, out_keys):
            tok = slice(t * 512, (t + 1) * 512)
            rms_rstd(lambda: xT[:, :, tok], 8, [("x", t)], 1.0)
            for c in range(8):
                P.op("dve", lambda e, c=c: e.tensor_tensor(ytmp[:, c, :], xT[:, c, tok], rstd[:], ALU.mult),
                     reads=[("x", t), "rstd"], writes=[("ytmp", c)])
                P.op("act", lambda e, c=c: e.activation(out_fn(c), ytmp[:, c, :], AF.Identity, bias=shift_fn(c), scale=gain_fn(c)),
                     reads=[("ytmp", c), "gm", "mod", "cst"], writes=out_keys(c))

        def modnorm(n):
            so = 0 if n == 0 else 24
            for t in range(NT):
                cd = cond_of(t)
                norm_tile(t,
                          lambda c, cd=cd: gm[:, n, cd, c:c + 1],
                          lambda c, cd=cd: mod[:, cd, so + c:so + c + 1],
                          lambda c, t=t: hT[:, c, t * 512:(t + 1) * 512],
                          lambda c, t=t: [("h", t)])

        def resid_proj(wsrc, in_fn, in_keys, goff):
            wv = kmajor(wsrc)
            for pc in range(2):
                wt, wk = wload(wv[:, :, pc * 512:(pc + 1) * 512], 8, 512)
                for mc in range(4):
                    m = pc * 4 + mc
                    for t in range(NT):
                        tok = slice(t * 512, (t + 1) * 512)
                        cd = cond_of(t)
                        b = bank()
                        for kc in range(8):
                            P.op("pe", lambda e, kc=kc, b=b, wt=wt, mc=mc, t=t: e.matmul(ps[:, b, :], wt[:, kc, mc * 128:(mc + 1) * 128], in_fn(kc, t), start=(kc == 0), stop=(kc == 7)),
                                 reads=[wk] + in_keys(t), writes=[("ps", b)])
                        P.op("dve", lambda e, b=b, m=m, tok=tok, cd=cd: e.scalar_tensor_tensor(xT[:, m, tok], ps[:, b, :], mod[:, cd, goff + m:goff + m + 1], xT[:, m, tok], ALU.mult, ALU.add),
                             reads=[("ps", b), "mod", ("x", t)], writes=[("x", t)])

        def ffn(layer, ada_next=None):
            groups = [(0, 8), (8, 8), (16, 6)]
            state["banks"] = [0, 1, 2, 3, 4, 5, 6]
            ada_todo = list(range(12)) if ada_next is not None else []

            def ada_step():
                if ada_todo:
                    adaln_piece(ada_next, ada_todo.pop(0))

            wg = kmajor(dr["ffn_w_gate"][layer])
            wu = kmajor(dr["ffn_w_up"][layer])
            wd = dr["ffn_w_down"][layer].rearrange("(f p) n -> p f n", p=128)
            go = 40
            for gi, (f0, nf) in enumerate(groups):
                hb = gi % 2
                npc = (nf + 3) // 4
                for pc in range(npc):
                    nfc = min(4, nf - pc * 4)
                    c0 = (f0 + pc * 4) * 128
                    wgt, wgk = wload(wg[:, :, c0:c0 + nfc * 128], 8, nfc * 128)
                    wut, wuk = wload(wu[:, :, c0:c0 + nfc * 128], 8, nfc * 128)
                    for fc in range(nfc):
                        fl = pc * 4 + fc
                        for t in range(NT):
                            tok = slice(t * 512, (t + 1) * 512)
                            bg, bu = bank(), bank()
                            for kc in range(8):
                                P.op("pe", lambda e, kc=kc, bg=bg, wgt=wgt, fc=fc, tok=tok: e.matmul(ps[:, bg, :], wgt[:, kc, fc * 128:(fc + 1) * 128], hT[:, kc, tok], start=(kc == 0), stop=(kc == 7)),
                                     reads=[wgk, ("h", t)], writes=[("ps", bg)])
                            for kc in range(8):
                                P.op("pe", lambda e, kc=kc, bu=bu, wut=wut, fc=fc, tok=tok: e.matmul(ps[:, bu, :], wut[:, kc, fc * 128:(fc + 1) * 128], hT[:, kc, tok], start=(kc == 0), stop=(kc == 7)),
                                     reads=[wuk, ("h", t)], writes=[("ps", bu)])
                            sl = (fl * NT + t) % 2
                            P.op("act", lambda e, bg=bg, sl=sl: e.activation(sgt[:, sl, :], ps[:, bg, :], AF.Silu),
                                 reads=[("ps", bg)], writes=[("sgt", sl)])
                            P.op("dve", lambda e, bu=bu, sl=sl, hb=hb, fl=fl, tok=tok: e.tensor_tensor(hid[:, hb, fl, tok], sgt[:, sl, :], ps[:, bu, :], ALU.mult),
                                 reads=[("ps", bu), ("sgt", sl)], writes=[("hid", hb, t)])
                    ada_step()
                for pc in range(2):
                    wdt, wdk = wload(wd[:, f0:f0 + nf, pc * 512:(pc + 1) * 512], nf, 512)
                    for mc in range(4):
                        m = pc * 4 + mc
                        for t in range(NT):
                            tok = slice(t * 512, (t + 1) * 512)
                            cd = cond_of(t)
                            b = bank()
                            for fl in range(nf):
                                P.op("pe", lambda e, fl=fl, b=b, wdt=wdt, mc=mc, hb=hb, tok=tok: e.matmul(ps[:, b, :], wdt[:, fl, mc * 128:(mc + 1) * 128], hid[:, hb, fl, tok], start=(fl == 0), stop=(fl == nf - 1)),
                                     reads=[wdk, ("hid", hb, t)], writes=[("ps", b)])
                            P.op("dve", lambda e, b=b, m=m, tok=tok, cd=cd: e.scalar_tensor_tensor(xT[:, m, tok], ps[:, b, :], mod[:, cd, go + m:go + m + 1], xT[:, m, tok], ALU.mult, ALU.add),
                                 reads=[("ps", b), "mod", ("x", t)], writes=[("x", t)])
                    ada_step()
            while ada_todo:
                ada_step()
            state["banks"] = list(range(8))

        def attn_core(G, Lk, hoffs, blocks, score_ops, v_ops, dv, scale, out_fn, out_keys, cfg, Pb, PT, st, sinkv=None, tag=0):
            Sb = cfg["S"]
            nkc = (Lk + 127) // 128

            if isinstance(hoffs, int):
                hoffs = [g * hoffs for g in range(G)]

            def scol(g, off):
                col = hoffs[g] + off
                return Sb[col // 512], col % 512

            def K_(n):
                return (n, tag)

            skeys = [("S", b) for b in Sb]
            mx, negm, rs, rinv = st[:, 0:G], st[:, G:2 * G], st[:, 2 * G:3 * G], st[:, 3 * G:4 * G]
            tmpv = st[:, 4 * G:5 * G]
            bm = st[:, 5 * G:5 * G + 8]
            blockwise = (G == 1 and len(blocks) > 1)
            for g in range(G):
                for bi, (off, n) in enumerate(blocks):
                    b, c0 = scol(g, off)
                    assert c0 + n <= 512
                    ops_ = score_ops(g, off, n)
                    for i, (lt, rh, rk) in enumerate(ops_):
                        P.op("pe", lambda e, b=b, c0=c0, n=n, lt=lt, rh=rh, i=i, last=len(ops_) - 1: e.matmul(ps[:, b, c0:c0 + n], lt, rh, start=(i == 0), stop=(i == last)),
                             reads=rk, writes=[("S", b)])
                    if blockwise:
                        P.op("dve", lambda e, b=b, c0=c0, n=n, bi=bi: e.tensor_reduce(bm[:, bi:bi + 1], ps[:, b, c0:c0 + n], AX.X, ALU.max), reads=[("S", b)], writes=[K_("st_bm")])
            if blockwise:
                P.op("dve", lambda e: e.tensor_reduce(mx, bm[:, 0:len(blocks)], AX.X, ALU.max), reads=[K_("st_bm")], writes=[K_("st_mx")])
            elif all(hoffs[g] == g * Lk for g in range(G)):
                sview = ps[:, Sb[0]:Sb[0] + len(Sb), :].rearrange("p b n -> p (b n)")[:, 0:G * Lk].rearrange("p (g k) -> p g k", g=G)
                P.op("dve", lambda e: e.tensor_reduce(mx, sview, AX.X, ALU.max), reads=skeys, writes=[K_("st_mx")])
            else:
                for g in range(G):
                    col = hoffs[g]
                    sv = ps[:, Sb[0]:Sb[0] + len(Sb), :].rearrange("p b n -> p (b n)")[:, col:col + Lk]
                    P.op("dve", lambda e, g=g, sv=sv: e.tensor_reduce(mx[:, g:g + 1], sv, AX.X, ALU.max), reads=skeys, writes=[K_("st_mx")])
            if sinkv is not None:
                P.op("dve", lambda e: e.scalar_tensor_tensor(mx, mx, scale, sinkv, ALU.mult, ALU.max), reads=[K_("st_mx"), "cst"], writes=[K_("st_mx")])
                P.op("dve", lambda e: e.tensor_scalar(negm, mx, -1.0, None, ALU.mult), reads=[K_("st_mx")], writes=[K_("st_negm")])
            else:
                P.op("dve", lambda e: e.tensor_scalar(negm, mx, -scale, None, ALU.mult), reads=[K_("st_mx")], writes=[K_("st_negm")])
            for g in range(G):
                col = hoffs[g]
                sv = ps[:, Sb[0]:Sb[0] + len(Sb), :].rearrange("p b n -> p (b n)")[:, col:col + Lk]
                P.op("act", lambda e, g=g, sv=sv: e.activation(Pb(g), sv, AF.Exp, bias=negm[:, g:g + 1], scale=scale, accum_out=rs[:, g:g + 1]),
                     reads=skeys + [K_("st_negm")], writes=[("Pb", g, tag), K_("st_rs")])
            if sinkv is not None:
                P.op("dve", lambda e: e.tensor_tensor(tmpv, sinkv, negm, ALU.add), reads=[K_("st_negm"), "cst"], writes=[K_("st_tmp")])
                P.op("act", lambda e: e.activation(tmpv, tmpv, AF.Exp), reads=[K_("st_tmp")], writes=[K_("st_tmp")])
                P.op("dve", lambda e: e.tensor_tensor(rs, rs, tmpv, ALU.add), reads=[K_("st_tmp"), K_("st_rs")], writes=[K_("st_rs")])
            P.op("dve", lambda e: e.reciprocal(rinv, rs), reads=[K_("st_rs")], writes=[K_("st_rinv")])

            def phase2():
                ptb = cfg["PT"]
                idx = 0
                pend = []
                total = G * nkc
                for g in range(G):
                    for kc in range(nkc):
                        nk = min(128, Lk - kc * 128)
                        slot = idx % 8
                        pb_ = ptb[(idx // 8) % len(ptb)]
                        pv = ps[:, pb_, :].bitcast(BF16)
                        P.op("pe", lambda e, g=g, kc=kc, nk=nk, slot=slot, pv=pv: e.transpose(pv[0:nk, slot * 128:(slot + 1) * 128], Pb(g)[:, kc * 128:kc * 128 + nk], identb),
                             reads=[("Pb", g, tag), "cstb"], writes=[("ps", pb_)])
                        pend.append(idx)
                        idx += 1
                        if len(pend) == 8 or idx == total:
                            i0 = pend[0]
                            n_ = len(pend)
                            evac(PT(i0, n_), pv[:, 0:n_ * 128].rearrange("p (a b) -> p a b", a=n_), [("ps", pb_)], ["PT"])
                            pend = []
                ob, oc0 = cfg["O"]
                for g in range(G):
                    col = oc0 + g * dv
                    b = ob + col // 512
                    c0 = col % 512
                    for kc in range(nkc):
                        nk = min(128, Lk - kc * 128)
                        rh, rk = v_ops(g, kc, nk)
                        ii = g * nkc + kc
                        P.op("pe", lambda e, b=b, c0=c0, ii=ii, nk=nk, rh=rh, kc=kc: e.matmul(ps[:, b, c0:c0 + dv], PT(ii, 1)[0:nk, 0, :], rh, start=(kc == 0), stop=(kc == nkc - 1)),
                             reads=["PT"] + rk, writes=["Oacc"])
                    P.op("act", lambda e, b=b, c0=c0, g=g: e.activation(out_fn(g), ps[:, b, c0:c0 + dv], AF.Identity, scale=rinv[:, g:g + 1]),
                         reads=["Oacc", K_("st_rinv")], writes=out_keys)

            return phase2

        def otok_to_oT(otok_v, okeys, tokcol, cfg):
            pb_ = cfg["PT"][0]
            pv = ps[:, pb_, :].bitcast(BF16)
            for c in range(8):
                P.op("pe", lambda e, c=c, pv=pv: e.transpose(pv[:, c * 128:(c + 1) * 128], otok_v[:, c * 128:(c + 1) * 128], identb),
                     reads=okeys + ["cstb"], writes=[("ps", pb_)])
            evac(hT[:, :, tokcol:tokcol + 128], pv.rearrange("p (a b) -> p a b", a=8), [("ps", pb_)], [("h", tokcol // 512)])

        def mla(layer):
            j = layer // 3
            SC = 192 ** -0.5
            P.barrier()
            qn = carve(0, [128, 4, T])
            ckb = carve(6144, [128, 2, 3328])
            krb = carve(12800, [128, 3328])
            B0 = 16128
            qlf = carve(B0, [128, 4, 512], F32)
            sqm = carve(B0 + 4096, [128, 4, 512])
            ckf = carve(B0 + 6144, [128, 2, 512], F32)
            krf = carve(B0 + 8192, [128, 512], F32)
            krt = carve(B0 + 9216, [128, 512], F32)
            agst = carve(B0 + 10240, [128, 1536])
            P.dma("pool", lambda e: e.dma_start(out=ckb[:, :, 1024:1280], in_=dr["ckv_ctxT"][j]), "ckctx", writes=["ckb_ctx"])
            P.dma("pool", lambda e: e.dma_start(out=krb[0:64, 1024:1280], in_=dr["krope_ctxT"][j]), "krctx", writes=["krb_ctx"])
            wdq, wdqk = wload(kmajor(dr["mla_w_dq"][j]), 8, 512)
            wdkv, wdkvk = wload(kmajor(dr["mla_w_dkv"][j]), 8, 320)
            state["banks"] = list(range(8))
            for t in range(NT):
                tok = slice(t * 512, (t + 1) * 512)
                for oc in range(4):
                    b = bank()
                    for kc in range(8):
                        P.op("pe", lambda e, b=b, kc=kc, oc=oc, tok=tok: e.matmul(ps[:, b, :], wdq[:, kc, oc * 128:(oc + 1) * 128], hT[:, kc, tok], start=(kc == 0), stop=(kc == 7)),
                             reads=[wdqk, ("h", t)], writes=[("ps", b)])
                    evac(qlf[:, oc, :], ps[:, b, :], [("ps", b)], ["qlf"])
                rms_rstd(lambda: qlf[:], 4, ["qlf"], 2.0, sqm)
                for oc in range(4):
                    P.op("dve", lambda e, oc=oc: e.tensor_tensor(qlf[:, oc, :], qlf[:, oc, :], rstd[:], ALU.mult), reads=["qlf", "rstd"], writes=["qlf"])
                    P.op("act", lambda e, oc=oc, tok=tok: e.activation(qn[:, oc, tok], qlf[:, oc, :], AF.Identity, scale=cst[:, K.QNORM + j * 4 + oc:K.QNORM + j * 4 + oc + 1]),
                         reads=["qlf", "cst"], writes=[("qn", t)])
                for oc in range(2):
                    b = bank()
                    for kc in range(8):
                        P.op("pe", lambda e, b=b, kc=kc, oc=oc, tok=tok: e.matmul(ps[:, b, :], wdkv[:, kc, oc * 128:(oc + 1) * 128], hT[:, kc, tok], start=(kc == 0), stop=(kc == 7)),
                             reads=[wdkvk, ("h", t)], writes=[("ps", b)])
                    evac(ckf[:, oc, :], ps[:, b, :], [("ps", b)], ["ckf"])
                b = bank()
                for kc in range(8):
                    P.op("pe", lambda e, b=b, kc=kc, tok=tok: e.matmul(ps[0:64, b, :], wdkv[:, kc, 256:320], hT[:, kc, tok], start=(kc == 0), stop=(kc == 7)),
                         reads=[wdkvk, ("h", t)], writes=[("ps", b)])
                evac(krf[0:64, :], ps[0:64, b, :], [("ps", b)], ["krf"])
                rms_rstd(lambda: ckf[:], 2, ["ckf"], 4.0, sqm)
                kcol = t * 512 if t < 2 else None
                for oc in range(2):
                    P.op("dve", lambda e, oc=oc: e.tensor_tensor(ckf[:, oc, :], ckf[:, oc, :], rstd[:], ALU.mult), reads=["ckf", "rstd"], writes=["ckf"])
                    P.op("act", lambda e, oc=oc: e.activation(ckf[:, oc, :], ckf[:, oc, :], AF.Identity, scale=cst[:, K.KVNORM + j * 2 + oc:K.KVNORM + j * 2 + oc + 1]),
                         reads=["ckf", "cst"], writes=["ckf"])
                    if t < 2:
                        evac(ckb[:, oc, kcol:kcol + 512], ckf[:, oc, :], ["ckf"], [("ckb", t)])
                    else:
                        evac(agst[:, oc * 512:(oc + 1) * 512], ckf[:, oc, :], ["ckf"], ["agst"])
                if t < 2:
                    evac(krb[0:64, kcol:kcol + 512], krf[0:64, :], ["krf"], [("krb", t)])
                    for q in range(4):
                        b = bank()
                        for oc in range(2):
                            P.op("pe", lambda e, b=b, oc=oc, q=q: e.transpose(ps[:, b, oc * 128:(oc + 1) * 128], ckf[:, oc, q * 128:(q + 1) * 128], ident),
                                 reads=["ckf", "cst"], writes=[("ps", b)])
                        P.op("pe", lambda e, b=b, q=q: e.transpose(ps[:, b, 256:320], krf[0:64, q * 128:(q + 1) * 128], ident[0:64, 0:64]),
                             reads=["krf", "cst"], writes=[("ps", b)])
                        sl = q % 2
                        evac(stage[:, sl, 0:320], ps[:, b, 0:320], [("ps", b)], [("stage", sl)])
                        gtok = t * 512 + q * 128
                        sq_, r0 = gtok // 256, gtok % 256
                        P.dma("sp", lambda e, sl=sl, sq_=sq_, r0=r0: e.dma_start(out=dr["o_ckv"][sq_, j, r0:r0 + 128, :], in_=stage[:, sl, 0:256]),
                              ("ost", sl), reads=[("stage", sl)], final=True)
                        P.dma("sp", lambda e, sl=sl, sq_=sq_, r0=r0: e.dma_start(out=dr["o_krope"][sq_, j, r0:r0 + 128, :], in_=stage[:, sl, 256:320]),
                              ("ost2", sl), reads=[("stage", sl)], final=True)
                else:
                    b = bank()
                    P.op("pe", lambda e, b=b: e.matmul(ps[0:64, b, :], cst[0:64, K.PERM:K.PERM + 64], krf[0:64, :], start=True, stop=True),
                         reads=["krf", "cst"], writes=[("ps", b)])
                    P.op("dve", lambda e, b=b: e.tensor_tensor(krt[0:64, :], ps[0:64, b, :], cst[0:64, K.SIN:K.SIN + 512], ALU.mult), reads=[("ps", b), "cst"], writes=["krt"])
                    P.op("dve", lambda e: e.tensor_tensor(krf[0:64, :], krf[0:64, :], cst[0:64, K.COS:K.COS + 512], ALU.mult), reads=["krf", "cst"], writes=["krf"])
                    P.op("dve", lambda e: e.tensor_tensor(agst[0:64, 1024:1536], krf[0:64, :], krt[0:64, :], ALU.add), reads=["krf", "krt"], writes=["agst"])
                    P.op("dve", lambda e: e.memset(agst[64:128, 1024:1536], 0.0), reads=[], writes=["agst"])
                    P.dma("sp", lambda e: e.dma_start(out=agin[j].ap(), in_=agst[:]), ("agin", j), reads=["agst"], writes=[("agin", j)])
                    P.coll(lambda e: e.collective_compute("AllGather", ALU.bypass, replica_groups=[[0, 1, 2, 3], [4, 5, 6, 7]],
                                                          ins=[agin[j].ap().opt()], outs=[agout[j].ap().opt()]),
                           ("agc", j), reads=[("agin", j)], writes=[("agout", j)])
                    agv = agout[j].ap().rearrange("(r p) n -> p r n", p=128)
                    for oc in range(2):
                        P.dma("sp", lambda e, oc=oc: e.dma_start(out=ckb[:, oc, 1280:3328].rearrange("p (r n) -> p r n", r=4), in_=agv[:, :, oc * 512:(oc + 1) * 512]),
                              ("agld", oc), reads=[("agout", j)], writes=["ckb_lat"])
                    P.dma("sp", lambda e: e.dma_start(out=krb[0:64, 1280:3328].rearrange("p (r n) -> p r n", r=4), in_=agv[0:64, :, 1024:1536]),
                          ("agld", 2), reads=[("agout", j)], writes=["krb_lat"])
            P.barrier(chans=[("agin", j)])
            wuq0, wuq0k = wload(kmajor(dr["mla_w_uq"][j])[:, :, 0:768], 4, 768)
            wuq1, wuq1k = wload(kmajor(dr["mla_w_uq"][j])[:, :, 768:1536], 4, 768)
            wuk, wukk = wload(kmajor(dr["mla_w_uk"][j]), 2, 1024)
            wuv, wuvk = wload(kmajor(dr["mla_w_uv"][j]), 2, 1024)

            def uq(h):
                w_, k_ = (wuq0, wuq0k) if h < 4 else (wuq1, wuq1k)
                return w_, k_, (h % 4) * 192

            qno = carve(B0, [128, 8, 512])
            qro = carve(B0 + 4096, [128, 8, 512])
            kn = carve(B0 + 8192, [128, 8, 512])
            V = carve(B0 + 12288, [128, 4, 1024])
            Pbp2 = [carve(B0 + 16384, [128, 8, 256]), carve(B0 + 22784, [128, 8, 256])]
            PTp = carve(B0 + 18432, [128, 16, 128])
            otok = carve(B0 + 20480, [128, 2, 1024])
            st = carve(B0 + 22528, [128, 128], F32)
            cfgp = {"S": [0, 1, 2, 3], "O": (4, 0), "PT": [6]}
            state["banks"] = [7]
            for t in range(2):
                tok = slice(t * 512, (t + 1) * 512)
                for h in range(8):
                    w_, k_, c0 = uq(h)
                    b = bank()
                    for kc in range(4):
                        P.op("pe", lambda e, b=b, kc=kc, w_=w_, c0=c0, tok=tok: e.matmul(ps[:, b, :], w_[:, kc, c0:c0 + 128], qn[:, kc, tok], start=(kc == 0), stop=(kc == 3)),
                             reads=[k_, ("qn", t)], writes=[("ps", b)])
                    evac(qno[:, h, :], ps[:, b, :], [("ps", b)], ["qno"])
                    b = bank()
                    for kc in range(4):
                        P.op("pe", lambda e, b=b, kc=kc, w_=w_, c0=c0, tok=tok: e.matmul(ps[0:64, b, :], w_[:, kc, c0 + 128:c0 + 192], qn[:, kc, tok], start=(kc == 0), stop=(kc == 3)),
                             reads=[k_, ("qn", t)], writes=[("ps", b)])
                    evac(qro[0:64, h, :], ps[0:64, b, :], [("ps", b)], ["qro"])
                    b = bank()
                    for oc in range(2):
                        P.op("pe", lambda e, b=b, oc=oc, h=h, tok=tok: e.matmul(ps[:, b, :], wuk[:, oc, h * 128:(h + 1) * 128], ckb[:, oc, tok], start=(oc == 0), stop=(oc == 1)),
                             reads=[wukk, ("ckb", t)], writes=[("ps", b)])
                    evac(kn[:, h, :], ps[:, b, :], [("ps", b)], ["kn"])
                for c in range(4):
                    for hf in range(2):
                        b = bank()
                        for oc in range(2):
                            P.op("pe", lambda e, b=b, oc=oc, c=c, hf=hf, t=t: e.matmul(ps[:, b, :], ckb[:, oc, t * 512 + c * 128:t * 512 + (c + 1) * 128], wuv[:, oc, hf * 512:(hf + 1) * 512], start=(oc == 0), stop=(oc == 1)),
                                 reads=[wuvk, ("ckb", t)], writes=[("ps", b)])
                        evac(V[:, c, hf * 512:(hf + 1) * 512], ps[:, b, :], [("ps", b)], ["V"])
                pend2 = None
                for s_ in range(2):
                    for qt in range(2):
                        q0 = s_ * 256 + qt * 128
                        ob = (s_ * 2 + qt) % 2

                        def score_ops(g, off, n, q0=q0, s_=s_):
                            return [(qno[:, g, q0:q0 + 128], kn[:, g, s_ * 256 + off:s_ * 256 + off + n], ["qno", "kn"]),
                                    (qro[0:64, g, q0:q0 + 128], krb[0:64, t * 512 + s_ * 256 + off:t * 512 + s_ * 256 + off + n], ["qro", ("krb", t)])]

                        def v_ops(g, kc, nk, s_=s_):
                            return V[0:nk, s_ * 2 + kc, g * 128:(g + 1) * 128], ["V"]

                        p2 = attn_core(8, 256, 256, [(0, 256)], score_ops, v_ops, 128, SC,
                                       lambda g, ob=ob: otok[:, ob, g * 128:(g + 1) * 128], [("otok", ob)], cfgp,
                                       lambda g, ob=ob: Pbp2[ob][:, g, :], lambda i0, n_: PTp[:, i0:i0 + n_, :], st[:, ob * 64:(ob + 1) * 64], tag=ob)

                        def fin(p2=p2, ob=ob, tc=t * 512 + q0):
                            p2()
                            otok_to_oT(otok[:, ob, :], [("otok", ob)], tc, cfgp)

                        if pend2 is not None:
                            pend2()
                        pend2 = fin
                pend2()
            P.barrier()
            qh = carve(B0, [128, 2, 512])
            qr = carve(B0 + 1024, [128, 2, 512])
            qraw = carve(B0 + 2048, [128, 512], F32)
            qt1 = carve(B0 + 3072, [128, 512], F32)
            kns = carve(B0 + 4096, [128, 2, 2304])
            vh = carve(B0 + 8704, [128, 2, 18, 128])
            Pbs = carve(B0 + 13312, [128, 2, 2304])
            PTs = carve(B0 + 17920, [128, 18, 128])
            otoks = carve(B0 + 20224, [128, 4, 1024])
            sts = carve(B0 + 24320, [128, 128], F32)
            pend_s = [None]
            cfgs = {"S": [0, 1, 2, 3, 4], "O": (5, 0), "PT": [6]}
            t = 2
            tok = slice(1024, 1536)
            kblocks = [(0, 512), (512, 512), (1024, 512), (1536, 512), (2048, 256)]
            it = 0
            for h in range(8):
                hb = h % 2
                w_, k_, c0 = uq(h)
                b = bank()
                for kc in range(4):
                    P.op("pe", lambda e, b=b, kc=kc, w_=w_, c0=c0, tok=tok: e.matmul(ps[:, b, :], w_[:, kc, c0:c0 + 128], qn[:, kc, tok], start=(kc == 0), stop=(kc == 3)),
                         reads=[k_, ("qn", t)], writes=[("ps", b)])
                evac(qh[:, hb, :], ps[:, b, :], [("ps", b)], [("qh", hb)])
                b = bank()
                for kc in range(4):
                    P.op("pe", lambda e, b=b, kc=kc, w_=w_, c0=c0, tok=tok: e.matmul(ps[0:64, b, :], w_[:, kc, c0 + 128:c0 + 192], qn[:, kc, tok], start=(kc == 0), stop=(kc == 3)),
                         reads=[k_, ("qn", t)], writes=[("ps", b)])
                evac(qraw[0:64, :], ps[0:64, b, :], [("ps", b)], ["qraw"], eng="act")
                b = bank()
                P.op("pe", lambda e, b=b: e.matmul(ps[0:64, b, :], cst[0:64, K.PERM:K.PERM + 64], qraw[0:64, :], start=True, stop=True),
                     reads=["qraw", "cst"], writes=[("ps", b)])
                P.op("dve", lambda e, b=b: e.tensor_tensor(qt1[0:64, :], ps[0:64, b, :], cst[0:64, K.SIN:K.SIN + 512], ALU.mult), reads=[("ps", b), "cst"], writes=["qt1"])
                P.op("dve", lambda e: e.tensor_tensor(qraw[0:64, :], qraw[0:64, :], cst[0:64, K.COS:K.COS + 512], ALU.mult), reads=["qraw", "cst"], writes=["qraw"])
                P.op("dve", lambda e, hb=hb: e.tensor_tensor(qr[0:64, hb, :], qraw[0:64, :], qt1[0:64, :], ALU.add), reads=["qraw", "qt1"], writes=[("qr", hb)])
                for (off, n) in kblocks:
                    b = bank()
                    for oc in range(2):
                        P.op("pe", lambda e, b=b, oc=oc, h=h, off=off, n=n: e.matmul(ps[:, b, 0:n], wuk[:, oc, h * 128:(h + 1) * 128], ckb[:, oc, 1024 + off:1024 + off + n], start=(oc == 0), stop=(oc == 1)),
                             reads=[wukk, "ckb_ctx", "ckb_lat"], writes=[("ps", b)])
                    evac(kns[:, hb, off:off + n], ps[:, b, 0:n], [("ps", b)], [("kns", hb)])
                for k4 in range(5):
                    nk4 = min(4, 18 - k4 * 4)
                    b = bank()
                    for kk in range(nk4):
                        kc = k4 * 4 + kk
                        for oc in range(2):
                            P.op("pe", lambda e, b=b, oc=oc, kk=kk, kc=kc, h=h: e.matmul(ps[:, b, kk * 128:(kk + 1) * 128], ckb[:, oc, 1024 + kc * 128:1024 + (kc + 1) * 128], wuv[:, oc, h * 128:(h + 1) * 128], start=(oc == 0), stop=(oc == 1)),
                                 reads=[wuvk, "ckb_ctx", "ckb_lat"], writes=[("ps", b)])
                    evac(vh[:, hb, k4 * 4:k4 * 4 + nk4, :], ps[:, b, 0:nk4 * 128].rearrange("p (a b) -> p a b", a=nk4), [("ps", b)], [("vh", hb)])
                for qt in range(4):
                    q0 = qt * 128
                    pbuf = it % 2
                    it += 1

                    def score_ops(g, off, n, q0=q0, hb=hb):
                        return [(qh[:, hb, q0:q0 + 128], kns[:, hb, off:off + n], [("qh", hb), ("kns", hb)]),
                                (qr[0:64, hb, q0:q0 + 128], krb[0:64, 1024 + off:1024 + off + n], [("qr", hb), "krb_ctx", "krb_lat"])]

                    def v_ops(g, kc, nk, hb=hb):
                        return vh[0:nk, hb, kc, :], [("vh", hb)]

                    p2 = attn_core(1, 2304, 2304, kblocks, score_ops, v_ops, 128, SC,
                                   lambda g, qt=qt, h=h: otoks[:, qt, h * 128:(h + 1) * 128], [("otoks", qt)], cfgs,
                                   lambda g, pbuf=pbuf: Pbs[:, pbuf, :], lambda i0, n_: PTs[:, i0:i0 + n_, :], sts[:, pbuf * 64:(pbuf + 1) * 64], tag=pbuf)
                    if pend_s[0] is not None:
                        pend_s[0]()
                    pend_s[0] = p2
            pend_s[0]()
            for qt in range(4):
                otok_to_oT(otoks[:, qt, :], [("otoks", qt)], 1024 + qt * 128, cfgs)
            state["banks"] = list(range(8))
            P.barrier()
            resid_proj(dr["mla_w_o"][j], lambda kc, t: hT[:, kc, t * 512:(t + 1) * 512], lambda t: [("h", t)], 16)

        def hgrn(layer):
            j = layer // 3
            P.barrier()
            Vt = carve(0, [128, 4, 1024])
            qT = carve(4096, [128, 8, 512])
            gT = carve(8192, [128, 8, 512])
            Qt = carve(12288, [128, 8, 512])
            Kt = carve(16384, [128, 8, 512])
            Ktok = carve(20480, [128, 4, 1024])
            oacc = carve(24576, [128, 4, 1024], F32)
            tA = carve(32768, [128, 512], F32)
            tB = carve(32768 + 1024, [128, 512], F32)
            tC = carve(32768 + 2048, [128, 512], F32)
            tE = carve(32768 + 3072, [128, 512], F32)
            osq = carve(32768, [128, 1024], F32)
            onb = carve(32768 + 2048, [128, 1024])
            Sf = carve(36864, [128, 8, 128], F32)
            Sb = carve(38912, [128, 8, 128])
            attb = carve(39936, [128, 8, 128])
            Dd = hgs[:, 0:64].rearrange("p (h c) -> p h c", h=8)
            Fl = hgs[:, 64:72]
            ssq = hgs[:, 72:80]
            rsd = hgs[:, 80:88]
            Fm = hgs[:, 88:120].rearrange("p (r h) -> p r h", r=4)
            lg = cst[:, K.LBL:K.LBL + 64].rearrange("p (a l) -> p a l", l=4)
            le = lbw[:, 0:64].rearrange("p (a l) -> p a l", l=4)
            lmx, lsum, lnum = lbw[:, 64:80], lbw[:, 80:96], lbw[:, 96:112]
            lb, oml = lbw[:, 112:128], lbw[:, 128:144]
            P.op("dve", lambda e: e.tensor_reduce(lmx, lg, AX.X, ALU.max), reads=["cst"], writes=["lmx"])
            P.op("dve", lambda e: e.tensor_tensor(le, lg, lmx.unsqueeze(2).to_broadcast([128, 16, 4]), ALU.subtract), reads=["cst", "lmx"], writes=["le"])
            P.op("act", lambda e: e.activation(le, le, AF.Exp), reads=["le"], writes=["le"])
            P.op("dve", lambda e: e.tensor_reduce(lsum, le, AX.X, ALU.add), reads=["le"], writes=["lsum"])
            P.op("dve", lambda e: e.tensor_reduce(lnum, le[:, :, 1:layer + 1], AX.X, ALU.add), reads=["le"], writes=["lnum"])
            P.op("dve", lambda e: e.reciprocal(lsum, lsum), reads=["lsum"], writes=["lsum"])
            P.op("dve", lambda e: e.tensor_tensor(lb, lnum, lsum, ALU.mult), reads=["lsum", "lnum"], writes=["lb"])
            P.op("dve", lambda e: e.tensor_scalar(oml, lb, -1.0, 1.0, ALU.mult, ALU.add), reads=["lb"], writes=["lb"])
            P.op("dve", lambda e: e.memset(ones64[:], 1.0), writes=["ones64"])

            wq_d, wi_d, wg_d = kmajor(dr["hg_w_q"][j]), kmajor(dr["hg_w_i"][j]), kmajor(dr["hg_w_g"][j])
            wf_d = [kmajor(dr["hg_w_f"][j, 0]), kmajor(dr["hg_w_f"][j, 1])]
            maskb = [cstb[:, 256:384], cstb[:, 384:512]]
            AB, R1, R2, UB = [0, 1], [2, 3], [4, 5], [6, 7]

            def fm_proj(wd, out_fn, tsl, t, keyw):
                for pc in range(2):
                    wt, wk = wload(wd[:, :, pc * 512:(pc + 1) * 512], 8, 512)
                    for cc in range(4):
                        c = pc * 4 + cc
                        b = bank()
                        for kc in range(8):
                            P.op("pe", lambda e, b=b, kc=kc, wt=wt, cc=cc, tsl=tsl: e.matmul(ps[:, b, :], wt[:, kc, cc * 128:(cc + 1) * 128], hT[:, kc, tsl], start=(kc == 0), stop=(kc == 7)),
                                 reads=[wk, ("h", t)], writes=[("ps", b)])
                        out_fn(c, ps[:, b, :], ("ps", b))

            def scan_dir(t, d, seqs, sample):
                for (soff, slen) in seqs:
                    subs = list(range(soff // 128, (soff + slen) // 128))
                    if d == 1:
                        subs = subs[::-1]
                    chs = [0, 1] if d == 0 else [1, 0]
                    for outputs in ([False, True] if sample else [True]):
                        if not sample:
                            P.op("dve", lambda e: e.memset(Sf[:], 0.0), writes=["Sf"])
                            P.op("dve", lambda e: e.memset(Sb[:], 0.0), writes=["Sb"])
                        elif not outputs:
                            P.op("dve", lambda e: e.memset(Sf[:], 0.0), writes=["Sf"])
                        for s in subs:
                            cols = slice(s * 128, (s + 1) * 128)
                            if outputs:
                                for h in range(8):
                                    b = AB[h // 4]
                                    P.op("pe", lambda e, b=b, h=h, cols=cols: e.matmul(ps[:, b, (h % 4) * 128:(h % 4 + 1) * 128], Kt[:, h, cols], Qt[:, h, cols], start=True, stop=True),
                                         reads=["Kt", "Qt"], writes=[("ps", b)])
                                for hf in range(2):
                                    b = AB[hf]
                                    P.op("dve", lambda e, b=b, hf=hf: e.tensor_tensor(attb[:, hf * 4:hf * 4 + 4, :], ps[:, b, :].rearrange("p (a n) -> p a n", a=4), maskb[d].unsqueeze(1).to_broadcast([128, 4, 128]), ALU.mult),
                                         reads=[("ps", b), "cstb"], writes=["attb"])
                            for ch in chs:
                                r0 = ch * 64
                                ci = (s * 2 + ch)
                                tcols = slice(s * 128 + r0, s * 128 + r0 + 64)
                                if outputs:
                                    for h in range(8):
                                        b = R1[h // 4]
                                        P.op("pe", lambda e, b=b, h=h, r0=r0, tcols=tcols: e.matmul(ps[r0:r0 + 64, b, (h % 4) * 128:(h % 4 + 1) * 128], Qt[:, h, tcols], Sb[:, h, :], start=True, stop=True),
                                             reads=["Qt", "Sb"], writes=[("ps", b)])
                                for h in range(8):
                                    b = UB[h // 4]
                                    P.op("pe", lambda e, b=b, h=h, r0=r0, s=s: e.matmul(ps[:, b, (h % 4) * 128:(h % 4 + 1) * 128], Ktok[r0:r0 + 64, s, h * 128:(h + 1) * 128], Vt[r0:r0 + 64, s, h * 128:(h + 1) * 128], start=True, stop=True),
                                         reads=["Ktok", "Vt"], writes=[("ps", b)])
                                for hf in range(2):
                                    b = UB[hf]
                                    P.op("dve", lambda e, b=b, hf=hf: e.tensor_tensor(Sf[:, hf * 4:hf * 4 + 4, :], Sf[:, hf * 4:hf * 4 + 4, :], ps[:, b, :].rearrange("p (a n) -> p a n", a=4), ALU.add),
                                         reads=[("ps", b), "Sf"], writes=["Sf"])
                                P.op("dve", lambda e, ci=ci: e.tensor_tensor(Sf[:], Sf[:], Dd[:, :, ci % 8].unsqueeze(2).to_broadcast([128, 8, 128]), ALU.mult),
                                     reads=["Sf", "Dd"], writes=["Sf"])
                                if outputs:
                                    P.op("act", lambda e: e.copy(Sb[:], Sf[:]), reads=["Sf"], writes=["Sb"])
                            if outputs:
                                for h in range(8):
                                    b = R2[h // 4]
                                    P.op("pe", lambda e, b=b, h=h, s=s: e.matmul(ps[:, b, (h % 4) * 128:(h % 4 + 1) * 128], attb[:, h, :], Vt[:, s, h * 128:(h + 1) * 128], start=True, stop=True),
                                         reads=["attb", "Vt"], writes=[("ps", b)])
                                for hf in range(2):
                                    osl = oacc[:, s, hf * 512:(hf + 1) * 512]
                                    if d == 0:
                                        P.op("act", lambda e, hf=hf, osl=osl: e.copy(osl, ps[:, R1[hf], :]), reads=[("ps", R1[hf])], writes=[("oacc", s)])
                                    else:
                                        P.op("dve", lambda e, hf=hf, osl=osl: e.tensor_tensor(osl, osl, ps[:, R1[hf], :], ALU.add), reads=[("ps", R1[hf]), ("oacc", s)], writes=[("oacc", s)])
                                    P.op("dve", lambda e, hf=hf, osl=osl: e.tensor_tensor(osl, osl, ps[:, R2[hf], :], ALU.add), reads=[("ps", R2[hf]), ("oacc", s)], writes=[("oacc", s)])
                        if sample and not outputs:
                            P.op("dve", lambda e: e.tensor_reduce(Fl, Dd, AX.X, ALU.mult), reads=["Dd"], writes=["Fl"])
                            P.dma("sp", lambda e: e.dma_start(out=hgin[d].ap()[:, 0:1024], in_=Sf[:].rearrange("p h v -> p (h v)")), ("hgin", d), reads=["Sf"], writes=[("hgin", d)])
                            P.dma("sp", lambda e: e.dma_start(out=hgin[d].ap()[:, 1024:1032], in_=Fl), ("hginF", d), reads=["Fl"], writes=[("hgin", d)])
                            P.coll(lambda e: e.collective_compute("AllGather", ALU.bypass, replica_groups=[[0, 1, 2, 3], [4, 5, 6, 7]],
                                                                  ins=[hgin[d].ap().opt()], outs=[hgout[d].ap().opt()]),
                                   ("hgc", d), reads=[("hgin", d)], writes=[("hgout", d)])
                            gv = hgout[d].ap().rearrange("(r p) n -> p r n", p=128)
                            Fg = hgs2[:, 0:32].rearrange("p (r h) -> p r h", r=4)
                            P.dma("sp", lambda e: e.dma_start(out=Fg, in_=gv[:, :, 1024:1032]), ("hgF", d), reads=[("hgout", d)], writes=["Fg"])
                            P.dma("sp", lambda e: e.dma_start(out=Sf[:], in_=dr["hg_s0"][d].rearrange("h k v -> k h v")), ("hgs0", d), writes=["Sf"])
                            mo = K.RMASK + (0 if d == 0 else 8)
                            ranks = [0, 1, 2, 3] if d == 0 else [3, 2, 1, 0]
                            for r in ranks:
                                P.op("dve", lambda e, r=r: e.tensor_scalar(Fm[:, r, :], Fg[:, r, :], cst[:, mo + r:mo + r + 1], cst[:, mo + 4 + r:mo + 5 + r], ALU.mult, ALU.add),
                                     reads=["Fg", "cst"], writes=["Fm"])
                            for r in ranks:
                                sl = r % 2
                                P.dma("sp", lambda e, r=r, sl=sl: e.dma_start(out=stage[:, sl, :], in_=gv[:, r, 0:1024]), ("stage", sl), reads=[("hgout", d)], writes=[("stage", sl)])
                                P.op("dve", lambda e, r=r: e.tensor_tensor(Sf[:], Sf[:], Fm[:, r, :].unsqueeze(2).to_broadcast([128, 8, 128]), ALU.mult), reads=["Sf", "Fm"], writes=["Sf"])
                                P.op("dve", lambda e, r=r, sl=sl: e.scalar_tensor_tensor(Sf[:].rearrange("p h v -> p (h v)"), stage[:, sl, :], cst[:, mo + r:mo + r + 1], Sf[:].rearrange("p h v -> p (h v)"), ALU.mult, ALU.add),
                                     reads=["Sf", ("stage", sl), "cst"], writes=["Sf"])
                            P.op("act", lambda e: e.copy(Sb[:], Sf[:]), reads=["Sf"], writes=["Sb"])
                    if not sample:
                        sidx = (t * 512 + soff) // 256
                        P.dma("sp", lambda e, sidx=sidx: e.dma_start(out=dr["o_hg"][sidx, j, d].rearrange("h k v -> k h v"), in_=Sf[:]), ("ohg", d), reads=["Sf"], final=True)

            for t in [2, 0, 1]:
                tsl = slice(t * 512, (t + 1) * 512)
                sample = (t == 2)
                seqs = [(0, 512)] if sample else [(0, 256), (256, 256)]
                state["banks"] = list(range(8))
                wts = [wload(wi_d[:, :, pc * 512:(pc + 1) * 512], 8, 512) for pc in range(2)]
                for s in range(4):
                    for pc in range(2):
                        wt, wk = wts[pc]
                        b = bank()
                        for kc in range(8):
                            P.op("pe", lambda e, b=b, kc=kc, wt=wt, s=s, t=t: e.matmul(ps[:, b, :], hT[:, kc, t * 512 + s * 128:t * 512 + (s + 1) * 128], wt[:, kc, :], start=(kc == 0), stop=(kc == 7)),
                                 reads=[wk, ("h", t)], writes=[("ps", b)])
                        evac(Vt[:, s, pc * 512:(pc + 1) * 512], ps[:, b, :], [("ps", b)], ["Vt"])
                fm_proj(wq_d, lambda c, pa, pk: P.op("act", lambda e: e.activation(qT[:, c, :], pa, AF.Silu), reads=[pk], writes=["qT"]), tsl, t, "q")
                fm_proj(wg_d, lambda c, pa, pk: P.op("act", lambda e: e.activation(gT[:, c, :], pa, AF.Silu), reads=[pk], writes=["gT"]), tsl, t, "g")
                for d in range(2):
                    def prep(c, pa, pk, d=d):
                        lbc = lb[:, d * 8 + c:d * 8 + c + 1]
                        omc = oml[:, d * 8 + c:d * 8 + c + 1]
                        if c % 2 == 0:
                            tA_, tB_, tC_, tE_ = tA, tB, tC, tE
                            kA, kB, kC, kE = "tA", "tB", "tC", "tE"
                        else:
                            tA_, tB_, tC_, tE_ = stage[:, 0, 0:512], stage[:, 0, 512:1024], stage[:, 1, 0:512], stage[:, 1, 512:1024]
                            kA, kB, kC, kE = ("stage", 0), ("stage", 0), ("stage", 1), ("stage", 1)
                        P.op("act", lambda e: e.activation(tA_, pa, AF.Exp, scale=-1.0), reads=[pk], writes=[kA])
                        P.op("dve", lambda e: e.tensor_scalar(tA_, tA_, 1.0, None, ALU.add), reads=[kA], writes=[kA])
                        P.op("dve", lambda e: e.reciprocal(tA_, tA_), reads=[kA], writes=[kA])
                        P.op("dve", lambda e: e.tensor_scalar(tA_, tA_, omc, lbc, ALU.mult, ALU.add), reads=[kA, "lb"], writes=[kA])
                        P.op("act", lambda e: e.activation(tB_, tA_, AF.Identity, bias=1.0, scale=-1.0), reads=[kA], writes=[kB])
                        P.op("act", lambda e: e.activation(tA_, tA_, AF.Ln), reads=[kA, kB], writes=[kA])
                        for ck in range(8):
                            if d == 0:
                                o_, i_ = tC_[:, ck * 64:(ck + 1) * 64], tA_[:, ck * 64:(ck + 1) * 64]
                            else:
                                lo = ck * 64 - 1 if ck > 0 else None
                                o_, i_ = tC_[:, ck * 64 + 63:lo:-1], tA_[:, ck * 64 + 63:lo:-1]
                            P.op("dve", lambda e, o_=o_, i_=i_: e.tensor_tensor_scan(o_, ones64[:], i_, 0.0, ALU.mult, ALU.add), reads=[kA, "ones64"], writes=[kC])
                        P.op("act", lambda e: e.activation(tE_, tC_, AF.Exp), reads=[kC], writes=[kE])
                        P.op("dve", lambda e: e.tensor_tensor(Qt[:, c, :], qT[:, c, :], tE_, ALU.mult), reads=[kE, "qT"], writes=["Qt"])
                        dcol = tE_[:, 63::64] if d == 0 else tE_[:, 0::64]
                        P.op("dve", lambda e: e.tensor_copy(Dd[:, c, :], dcol), reads=[kE], writes=["Dd"])
                        P.op("act", lambda e: e.activation(tE_, tC_, AF.Exp, scale=-1.0), reads=[kC, kE, "Qt", "Dd"], writes=[kE])
                        P.op("dve", lambda e: e.tensor_tensor(Kt[:, c, :], tB_, tE_, ALU.mult), reads=[kE, kB], writes=["Kt"])

                    fm_proj(wf_d[d], prep, tsl, t, "f")
                    for s in range(4):
                        b = bank()
                        pv = ps[:, b, :].bitcast(BF16)
                        for h in range(8):
                            P.op("pe", lambda e, pv=pv, h=h, s=s: e.transpose(pv[:, h * 128:(h + 1) * 128], Kt[:, h, s * 128:(s + 1) * 128], identb),
                                 reads=["Kt", "cstb"], writes=[("ps", b)])
                        evac(Ktok[:, s, :], pv, [("ps", b)], ["Ktok"])
                    scan_dir(t, d, seqs, sample)
                state["banks"] = list(range(8))
                for s in range(4):
                    P.op("act", lambda e, s=s: e.activation(osq[:], oacc[:, s, :], AF.Square), reads=[("oacc", s)], writes=["osq"])
                    P.op("dve", lambda e: e.tensor_reduce(ssq, osq[:].rearrange("p (h v) -> p h v", h=8), AX.X, ALU.add), reads=["osq"], writes=["ssq"])
                    P.op("act", lambda e: e.activation(rsd, ssq, AF.Sqrt, bias=EPS, scale=1.0 / 128.0), reads=["ssq"], writes=["rsd"])
                    P.op("dve", lambda e: e.reciprocal(rsd, rsd), reads=["rsd"], writes=["rsd"])
                    P.op("dve", lambda e, s=s: e.tensor_tensor(onb[:].rearrange("p (h v) -> p h v", h=8), oacc[:, s, :].rearrange("p (h v) -> p h v", h=8), rsd.unsqueeze(2).to_broadcast([128, 8, 128]), ALU.mult),
                         reads=[("oacc", s), "rsd"], writes=["onb"])
                    b = bank()
                    pv = ps[:, b, :].bitcast(BF16)
                    for h in range(8):
                        P.op("pe", lambda e, pv=pv, h=h: e.transpose(pv[:, h * 128:(h + 1) * 128], onb[:, h * 128:(h + 1) * 128], identb),
                             reads=["onb", "cstb"], writes=[("ps", b)])
                    tc0 = t * 512 + s * 128
                    P.op("dve", lambda e, pv=pv, tc0=tc0, s=s: e.scalar_tensor_tensor(hT[:, :, tc0:tc0 + 128], pv.rearrange("p (a n) -> p a n", a=8), cst[:, K.ONORM:K.ONORM + 1], gT[:, :, s * 128:(s + 1) * 128], ALU.mult, ALU.mult),
                         reads=[("ps", b), "gT", "cst"], writes=[("h", t)])
                P.barrier(chans=[("ohg", 0), ("ohg", 1), ("hgin", 0), ("hgin", 1), ("hginF", 0), ("hginF", 1)])
            resid_proj(dr["hg_w_o"][j], lambda kc, t: hT[:, kc, t * 512:(t + 1) * 512], lambda t: [("h", t)], 16)

        def swa(layer):
            j = layer // 3
            SC = 64 ** -0.5
            P.barrier()
            QsT = carve(0, [128, 16, 512])
            KsT = carve(8192, [128, 4, 512])
            Vs = carve(10240, [128, 4, 256])
            Gt = carve(11264, [128, 4, 1536])
            HK = carve(17408, [128, 2, 4, 128])
            HV = carve(18432, [128, 2, 256])
            KcT = carve(18944, [128, 4, 256])
            Vc = carve(19968, [128, 2, 256])
            bmask = carve(20480, [128, 4, 2, 128])
            agst = carve(21504, [128, 1536])
            B1 = 23040
            QT = carve(B1, [128, 16, 512])
            qraw = carve(B1, [128, 512], F32)
            qt1 = carve(B1 + 1024, [128, 512], F32)
            KT = carve(B1 + 8192, [128, 4, 512])
            Vb = carve(B1 + 10240, [128, 4, 256])
            Pbp = carve(B1 + 11264, [128, 8, 256])
            Pbs = carve(B1 + 11264, [128, 4, 640])
            PTb = carve(B1 + 13824, [128, 20, 128])
            otok = carve(B1 + 16384, [128, 1024])
            st = carve(B1 + 17408, [128, 64], F32)

            wq_d, wk_d, wv_d = kmajor(dr["swa_w_q"][j]), kmajor(dr["swa_w_k"][j]), kmajor(dr["swa_w_v"][j])
            P.dma("pool", lambda e: e.dma_start(out=KcT[0:64, :, :], in_=dr["swa_kctxT"]), "swkc", writes=["KcT"])
            P.dma("pool", lambda e: e.dma_start(out=Vc[:], in_=dr["swa_vctx"].rearrange("(c p) f -> p c f", p=128)), "swvc", writes=["Vc"])
            P.dma("pool", lambda e: e.dma_start(out=bmask[:].rearrange("p a b n -> p (a b n)"), in_=dr["bandmask"]), "swbm", writes=["bmask"])

            def load_qkv_w():
                wq = [wload(wq_d[:, :, pc * 512:(pc + 1) * 512], 8, 512) for pc in range(2)]
                wkv = wload(wk_d, 8, 256)
                wvv = wload(wv_d, 8, 256)
                return wq, wkv, wvv

            def rope_evac(pb_, dst):
                evac(qraw[0:64, :], ps[0:64, pb_, :], [("ps", pb_)], ["qraw"], eng="act")
                b2 = bank()
                P.op("pe", lambda e, b2=b2: e.matmul(ps[0:64, b2, :], cst[0:64, K.PERM:K.PERM + 64], qraw[0:64, :], start=True, stop=True),
                     reads=["qraw", "cst"], writes=[("ps", b2)])
                P.op("dve", lambda e, b2=b2: e.tensor_tensor(qt1[0:64, :], ps[0:64, b2, :], cst[0:64, K.SIN:K.SIN + 512], ALU.mult), reads=[("ps", b2), "cst"], writes=["qt1"])
                P.op("dve", lambda e: e.tensor_tensor(qraw[0:64, :], qraw[0:64, :], cst[0:64, K.COS:K.COS + 512], ALU.mult), reads=["qraw", "cst"], writes=["qraw"])
                P.op("dve", lambda e: e.tensor_tensor(dst, qraw[0:64, :], qt1[0:64, :], ALU.add), reads=["qraw", "qt1"], writes=["ropeout"])

            t = 2
            tsl = slice(1024, 1536)
            state["banks"] = list(range(8))
            wq, (wkt, wkk), (wvt, wvk) = load_qkv_w()
            for hd in range(16):
                wt, wk = wq[hd // 8]
                c0 = (hd % 8) * 64
                b = bank()
                for kc in range(8):
                    P.op("pe", lambda e, b=b, kc=kc, wt=wt, c0=c0, tsl=tsl: e.matmul(ps[0:64, b, :], wt[:, kc, c0:c0 + 64], hT[:, kc, tsl], start=(kc == 0), stop=(kc == 7)),
                         reads=[wk, ("h", 2)], writes=[("ps", b)])
                rope_evac(b, QsT[0:64, hd, :])
            for kvh in range(4):
                b = bank()
                for kc in range(8):
                    P.op("pe", lambda e, b=b, kc=kc, kvh=kvh, tsl=tsl, wkt=wkt: e.matmul(ps[0:64, b, :], wkt[:, kc, kvh * 64:(kvh + 1) * 64], hT[:, kc, tsl], start=(kc == 0), stop=(kc == 7)),
                         reads=[wkk, ("h", 2)], writes=[("ps", b)])
                rope_evac(b, KsT[0:64, kvh, :])
            for s in range(4):
                b = bank()
                for kc in range(8):
                    P.op("pe", lambda e, b=b, kc=kc, s=s, wvt=wvt: e.matmul(ps[:, b, 0:256], hT[:, kc, 1024 + s * 128:1024 + (s + 1) * 128], wvt[:, kc, :], start=(kc == 0), stop=(kc == 7)),
                         reads=[wvk, ("h", 2)], writes=[("ps", b)])
                evac(Vs[:, s, :], ps[:, b, 0:256], [("ps", b)], ["Vs"])
            P.op("dve", lambda e: e.memset(agst[:], 0.0), writes=["agst"])
            P.op("dve", lambda e: e.tensor_copy(agst[0:64, 0:512].rearrange("p (a n) -> p a n", a=4), KsT[0:64, :, 0:128]), reads=["ropeout"], writes=["agst"])
            P.op("dve", lambda e: e.tensor_copy(agst[0:64, 512:1024].rearrange("p (a n) -> p a n", a=4), KsT[0:64, :, 384:512]), reads=["ropeout"], writes=["agst"])
            P.op("dve", lambda e: e.tensor_copy(agst[:, 1024:1280], Vs[:, 0, :]), reads=["Vs"], writes=["agst"])
            P.op("dve", lambda e: e.tensor_copy(agst[:, 1280:1536], Vs[:, 3, :]), reads=["Vs"], writes=["agst"])
            P.dma("sp", lambda e: e.dma_start(out=swin.ap(), in_=agst[:]), "swin", reads=["agst"], writes=["swin"])
            P.coll(lambda e: e.collective_compute("AllGather", ALU.bypass, replica_groups=[[0, 1, 2, 3], [4, 5, 6, 7]],
                                                  ins=[swin.ap().opt()], outs=[swout.ap().opt()]),
                   "swc", reads=["swin"], writes=["swout"])
            P.dma("sp", lambda e: e.dma_start(out=Gt[:], in_=swout.ap().rearrange("(r p) n -> p r n", p=128)), "swg", reads=["swout"], writes=["Gt"])
            P.barrier()

            cfgp = {"S": [0, 1, 2, 3], "O": (4, 0), "PT": [5]}
            for t in (range(2) if flags.get("swa_parts", 7) & 2 else []):
                tsl = slice(t * 512, (t + 1) * 512)
                state["banks"] = [6, 7]
                wq, (wkt, wkk), (wvt, wvk) = load_qkv_w()
                for hd in range(16):
                    wt, wk = wq[hd // 8]
                    c0 = (hd % 8) * 64
                    b = bank()
                    for kc in range(8):
                        P.op("pe", lambda e, b=b, kc=kc, wt=wt, c0=c0, tsl=tsl: e.matmul(ps[0:64, b, :], wt[:, kc, c0:c0 + 64], hT[:, kc, tsl], start=(kc == 0), stop=(kc == 7)),
                             reads=[wk, ("h", t)], writes=[("ps", b)])
                    evac(QT[0:64, hd, :], ps[0:64, b, :], [("ps", b)], ["QT"])
                for kvh in range(4):
                    b = bank()
                    for kc in range(8):
                        P.op("pe", lambda e, b=b, kc=kc, kvh=kvh, tsl=tsl, wkt=wkt: e.matmul(ps[0:64, b, :], wkt[:, kc, kvh * 64:(kvh + 1) * 64], hT[:, kc, tsl], start=(kc == 0), stop=(kc == 7)),
                             reads=[wkk, ("h", t)], writes=[("ps", b)])
                    evac(KT[0:64, kvh, :], ps[0:64, b, :], [("ps", b)], ["KT"])
                for s in (range(4) if not flags.get("nokvtok") else []):
                    b = bank()
                    b2 = bank()
                    tcs = slice(t * 512 + s * 128, t * 512 + (s + 1) * 128)
                    for kc in range(8):
                        P.op("pe", lambda e, b=b, kc=kc, tcs=tcs, wkt=wkt: e.matmul(ps[:, b, 0:256], hT[:, kc, tcs], wkt[:, kc, :], start=(kc == 0), stop=(kc == 7)),
                             reads=[wkk, ("h", t)], writes=[("ps", b)])
                    for kc in range(8):
                        P.op("pe", lambda e, b2=b2, kc=kc, tcs=tcs, wvt=wvt: e.matmul(ps[:, b2, 0:256], hT[:, kc, tcs], wvt[:, kc, :], start=(kc == 0), stop=(kc == 7)),
                             reads=[wvk, ("h", t)], writes=[("ps", b2)])
                    sl = s % 2
                    kvl = flags.get("kvlevel", 3)
                    if kvl >= 2:
                        evac(stage[:, sl, 0:256], ps[:, b, 0:256], [("ps", b)], [("stage", sl)], eng="act")
                        evac(stage[:, sl, 256:512], ps[:, b2, 0:256], [("ps", b2)], [("stage", sl)], eng="act")
                    if kvl >= 3:
                        evac(Vb[:, s, :], stage[:, sl, 256:512], [("stage", sl)], ["Vb"], eng="dve")
                    gtok = t * 512 + s * 128
                    sq_, r0 = gtok // 256, gtok % 256
                    if not flags.get("nokvout"):
                        P.dma("sp", lambda e, sl=sl, sq_=sq_, r0=r0: e.dma_start(out=dr["o_k"][sq_, j, r0:r0 + 128, :], in_=stage[:, sl, 0:256]),
                              ("ost", sl), reads=[("stage", sl)], final=True)
                        P.dma("sp", lambda e, sl=sl, sq_=sq_, r0=r0: e.dma_start(out=dr["o_v"][sq_, j, r0:r0 + 128, :], in_=stage[:, sl, 256:512]),
                              ("ost2", sl), reads=[("stage", sl)], final=True)
                for s_ in (range(2) if not flags.get("noattn") else []):
                    for qt in range(2):
                        q0 = s_ * 256 + qt * 128
                        for hg_ in range(2):
                            def score_ops(g, off, n, q0=q0, s_=s_, hg_=hg_):
                                hd = hg_ * 8 + g
                                return [(QT[0:64, hd, q0:q0 + 128], KT[0:64, hd // 4, s_ * 256 + off:s_ * 256 + off + n], ["QT", "KT"])]

                            def v_ops(g, kc, nk, s_=s_, hg_=hg_):
                                hd = hg_ * 8 + g
                                return Vb[0:nk, s_ * 2 + kc, (hd // 4) * 64:(hd // 4 + 1) * 64], ["Vb"]

                            p2_ = attn_core(8, 256, 256, [(0, 256)], score_ops, v_ops, 64, SC,
                                      lambda g, hg_=hg_: otok[:, (hg_ * 8 + g) * 64:(hg_ * 8 + g + 1) * 64], ["otok"], cfgp,
                                      lambda g: Pbp[:, g, :], lambda i0, n_: PTb[:, i0:i0 + n_, :], st,
                                      sinkv=(cst[:, K.SINK + hg_ * 8:K.SINK + hg_ * 8 + 8] if not flags.get("nosink") else None))
                            p2_()
                        otok_to_oT(otok, ["otok"], t * 512 + q0, cfgp)
            P.barrier()

            so = K.SEL
            for side, (koff, voff) in enumerate(((512, 1280), (0, 1024))):
                for r in range(4):
                    sc_ = cst[:, so + side * 4 + r:so + side * 4 + r + 1]
                    kin = Gt[0:64, r, koff:koff + 512].rearrange("p (a n) -> p a n", a=4)
                    vin = Gt[:, r, voff:voff + 256]
                    if r == 0:
                        P.op("dve", lambda e, sc_=sc_, kin=kin, side=side: e.tensor_scalar(HK[0:64, side, :, :], kin, sc_[0:64, :], None, ALU.mult), reads=["Gt", "cst"], writes=["HK"])
                        P.op("dve", lambda e, sc_=sc_, vin=vin, side=side: e.tensor_scalar(HV[:, side, :], vin, sc_, None, ALU.mult), reads=["Gt", "cst"], writes=["HV"])
                    else:
                        P.op("dve", lambda e, sc_=sc_, kin=kin, side=side: e.scalar_tensor_tensor(HK[0:64, side, :, :], kin, sc_[0:64, :], HK[0:64, side, :, :], ALU.mult, ALU.add), reads=["Gt", "cst", "HK"], writes=["HK"])
                        P.op("dve", lambda e, sc_=sc_, vin=vin, side=side: e.scalar_tensor_tensor(HV[:, side, :], vin, sc_, HV[:, side, :], ALU.mult, ALU.add), reads=["Gt", "cst", "HV"], writes=["HV"])
            cfgs = {"S": [0, 1, 2, 3, 4, 5], "O": (6, 0), "PT": [7]}
            hoffs = [0, 640, 1280, 2048]
            blocks = [(0, 256), (256, 128), (384, 128), (512, 128)]
            for jj in (range(4) if flags.get("swa_parts", 7) & 4 else []):
                q0 = jj * 128
                for kvh in range(4):
                    def kband(which, jj=jj, kvh=kvh):
                        bi = jj - 1 + which
                        if bi < 0:
                            return HK[0:64, 0, kvh, :], HV[:, 0, kvh * 64:(kvh + 1) * 64], ["HK", "HV"]
                        if bi > 3:
                            return HK[0:64, 1, kvh, :], HV[:, 1, kvh * 64:(kvh + 1) * 64], ["HK", "HV"]
                        return KsT[0:64, kvh, bi * 128:(bi + 1) * 128], Vs[:, bi, kvh * 64:(kvh + 1) * 64], ["ropeout", "Vs"]

                    def score_ops(g, off, n, jj=jj, kvh=kvh, q0=q0):
                        hd = kvh * 4 + g
                        qa = QsT[0:64, hd, q0:q0 + 128]
                        if off == 0:
                            return [(qa, KcT[0:64, kvh, :], ["ropeout", "KcT"])]
                        which = (off - 256) // 128
                        ka, _, rk = kband(which)
                        ops_ = [(qa, ka, ["ropeout"] + rk)]
                        if which != 1:
                            ops_.append((identb, bmask[:, jj, 0 if which == 0 else 1, :], ["cstb", "bmask"]))
                        return ops_

                    def v_ops(g, kc, nk, kvh=kvh):
                        if kc < 2:
                            return Vc[:, kc, kvh * 64:(kvh + 1) * 64], ["Vc"]
                        _, va, rk = kband(kc - 2)
                        return va, rk

                    p2_ = attn_core(4, 640, hoffs, blocks, score_ops, v_ops, 64, SC,
                              lambda g, kvh=kvh: otok[:, (kvh * 4 + g) * 64:(kvh * 4 + g + 1) * 64], ["otok"], cfgs,
                              lambda g: Pbs[:, g, :], lambda i0, n_: PTb[:, i0:i0 + n_, :], st,
                              sinkv=cst[:, K.SINK + kvh * 4:K.SINK + kvh * 4 + 4])
                    p2_()
                otok_to_oT(otok, ["otok"], 1024 + q0, cfgs)
            state["banks"] = list(range(8))
            P.barrier()
            resid_proj(dr["swa_w_o"][j], lambda kc, t: hT[:, kc, t * 512:(t + 1) * 512], lambda t: [("h", t)], 16)

        def final():
            for t in range(NT):
                yT = ytmp
                norm_tile(t,
                          lambda c: cst[:, K.NFIN + c:K.NFIN + c + 1],
                          lambda c: 0.0,
                          lambda c: yT[:, c, :],
                          lambda c: [("ytmp", c)])
                for q in range(4):
                    tt = t * 4 + q
                    sl = tt % 2
                    for half in range(2):
                        b = bank()
                        for j in range(4):
                            c = half * 4 + j
                            P.op("pe", lambda e, b=b, j=j, c=c, q=q: e.transpose(ps[:, b, j * 128:(j + 1) * 128], yT[:, c, q * 128:(q + 1) * 128], ident),
                                 reads=[("ytmp", c), "cst"], writes=[("ps", b)])
                        evac(stage[:, sl, half * 512:(half + 1) * 512], ps[:, b, :], [("ps", b)], [("stage", sl)])
                    P.dma("sp", lambda e, tt=tt, sl=sl: e.dma_start(out=dr["y"][tt * 128:(tt + 1) * 128, :], in_=stage[:, sl, :]),
                          ("ost", sl), reads=[("stage", sl)], final=True)

        for layer in range(depth):
            if layer == 0:
                for pc in range(12):
                    adaln_piece(0, pc)
            adaln_finish(layer)
            if mixers:
                kind = layer % 3
                if kind == 0:
                    modnorm(0)
                    mla(layer)
                elif kind == 1:
                    modnorm(0)
                    hgrn(layer)
                else:
                    modnorm(0)
                    swa(layer)
            modnorm(1)
            ffn(layer, ada_next=(layer + 1 if layer + 1 < depth else None))
        final()
        import os as _os
        if _os.environ.get("KDEBUG"):
            print("OPS", {e: len(v) for e, v in P.ops.items()}, "waits", {e: sum(len(w) for w, _, _ in v) for e, v in P.ops.items()}, "nsem", P.nsem)
        P.emit()
    return nc


def fm(v):
    v = np.asarray(v, np.float32)
    lead = v.shape[:-1]
    a = v.reshape(*lead, v.shape[-1] // 128, 128)
    a = np.moveaxis(a, -1, 0)
    return np.ascontiguousarray(a).reshape(128, -1)


def rope_tables(pos0):
    pos = np.arange(pos0, pos0 + NS_TOK)
    row = (pos // 64).astype(np.float32)
    col = (pos % 64).astype(np.float32)
    inv = (10000.0 ** (-np.arange(16, dtype=np.float32) / 16)).astype(np.float32)
    cos = np.zeros((64, NS_TOK), np.float32)
    sin = np.zeros((64, NS_TOK), np.float32)
    perm = np.zeros((64, 64), np.float32)
    for i in range(64):
        p = row if i < 32 else col
        ii = i % 32
        f = ii % 16
        first = ii < 16
        ang = (p * inv[f]).astype(np.float32)
        cos[i] = np.cos(ang)
        sin[i] = -np.sin(ang) if first else np.sin(ang)
        sw = i + 16 if first else i - 16
        perm[sw, i] = 1.0
    return cos, sin, perm


def make_consts(inp, core):
    g, q = core // 4, core % 4
    c = np.zeros((128, NCONST), np.float32)
    c[:, K.IDENT:K.IDENT + 128] = np.eye(128, dtype=np.float32)
    c[:, K.ONESN:K.ONESN + 128] = 1.0 / 1024.0
    cv = np.stack([inp["c_ctx"], inp["c"][g]], 0)
    c[:, K.CVEC:K.CVEC + 16] = np.ascontiguousarray(cv.reshape(2, 8, 128).transpose(2, 1, 0)).reshape(128, 16)
    ab = inp["ada_b"].reshape(DEPTH, 48, 128).transpose(2, 0, 1).reshape(128, DEPTH * 48)
    c[:, K.ADAB:K.ADAB + DEPTH * 48] = ab
    c[:, K.NMIX:K.NMIX + 32] = fm(inp["norm_mix"])
    c[:, K.NFFN:K.NFFN + 32] = fm(inp["norm_ffn"])
    c[:, K.NFIN:K.NFIN + 8] = fm(inp["final_norm"])
    c[:, K.QNORM:K.QNORM + 8] = fm(inp["mla_q_norm"])
    c[:, K.KVNORM:K.KVNORM + 4] = fm(inp["mla_kv_norm"])
    cos, sin, perm = rope_tables(q * NS_TOK)
    c[0:64, K.PERM:K.PERM + 64] = perm
    c[0:64, K.COS:K.COS + 512] = cos
    c[0:64, K.SIN:K.SIN + 512] = sin
    s_ = np.arange(128)[:, None]
    t_ = np.arange(128)[None, :]
    same = (s_ // 64) == (t_ // 64)
    c[:, K.MASKF:K.MASKF + 128] = (same & (s_ <= t_)).astype(np.float32)
    c[:, K.MASKB:K.MASKB + 128] = (same & (s_ >= t_)).astype(np.float32)
    lbl = inp["hg_lb_logits"].reshape(2, DEPTH, 8, 128).transpose(3, 0, 2, 1)
    c[:, K.LBL:K.LBL + 64] = np.ascontiguousarray(lbl).reshape(128, 64)
    c[:, K.ONORM] = inp["hg_o_norm"][0]
    r = np.arange(4)
    c[:, K.RMASK + 0:K.RMASK + 4] = (r < q).astype(np.float32)[None, :]
    c[:, K.RMASK + 4:K.RMASK + 8] = 1.0 - (r < q).astype(np.float32)[None, :]
    c[:, K.RMASK + 8:K.RMASK + 12] = (r > q).astype(np.float32)[None, :]
    c[:, K.RMASK + 12:K.RMASK + 16] = 1.0 - (r > q).astype(np.float32)[None, :]
    c[:, K.SINK:K.SINK + 16] = inp["swa_sink"][0][None, :]
    c[:, K.SEL + 0:K.SEL + 4] = (r == q - 1).astype(np.float32)[None, :]
    c[:, K.SEL + 4:K.SEL + 8] = (r == q + 1).astype(np.float32)[None, :]
    return c


def make_bandmask(core):
    q = core % 4
    qq = np.arange(128)[:, None]
    kk = np.arange(128)[None, :]
    m = np.zeros((128, 4, 2, 128), np.float32)
    for jj in range(4):
        bq = 4 * q + jj
        prev_ok = (kk >= qq) & (bq >= 1)
        next_ok = (kk <= qq) & (bq <= 14)
        m[:, jj, 0, :] = np.where(prev_ok, 0.0, -1e30)
        m[:, jj, 1, :] = np.where(next_ok, 0.0, -1e30)
    return m.reshape(128, 1024)


_CACHE = {}


def run(inputs, flags):
    inp = {k: np.asarray(v) for k, v in inputs.items()}
    key = tuple(sorted(flags.items()))
    if key not in _CACHE:
        _CACHE[key] = build(flags)
    nc = _CACHE[key]
    in_maps = []
    for core in range(NCORES):
        g, q = core // 4, core % 4
        xin = np.concatenate([inp["x_prompt"][4 * core:4 * core + 4].reshape(NP_TOK, D),
                              inp["x_sample"][g, q * NS_TOK:(q + 1) * NS_TOK]], 0)
        m = {"xin": np.ascontiguousarray(xin, dtype=np.float32), "consts": make_consts(inp, core)}
        ck = inp["cache_mla_ckv"][g]
        m["ckv_ctxT"] = np.ascontiguousarray(ck.reshape(2, 256, 2, 128).transpose(0, 3, 2, 1), dtype=np.float32)
        m["krope_ctxT"] = np.ascontiguousarray(inp["cache_mla_krope"][g].transpose(0, 2, 1), dtype=np.float32)
        m["hg_s0"] = np.ascontiguousarray(inp["state_hgrn"][g, 0], dtype=np.float32)
        m["swa_kctxT"] = np.ascontiguousarray(inp["cache_swa_k"][g, 0].transpose(2, 1, 0), dtype=np.float32)
        m["swa_vctx"] = np.ascontiguousarray(inp["cache_swa_v"][g, 0].reshape(256, 256), dtype=np.float32)
        m["bandmask"] = make_bandmask(core)
        for k in W_SPECS:
            m[k] = np.ascontiguousarray(inp[k], dtype=np.float32)
        in_maps.append(m)
    res = run_bass_kernel_spmd(nc, in_maps, core_ids=list(range(NCORES)))
    return res.results


def assemble(res):
    y_p = np.zeros((32, 256, D), np.float32)
    y_s = np.zeros((2, 2048, D), np.float32)
    ckv = np.zeros((32, 2, 256, 256), np.float32)
    krope = np.zeros((32, 2, 256, 64), np.float32)
    hg = np.zeros((32, 1, 2, 8, 128, 128), np.float32)
    ok = np.zeros((32, 1, 256, 4, 64), np.float32)
    ov = np.zeros((32, 1, 256, 4, 64), np.float32)
    for core in range(NCORES):
        g, q = core // 4, core % 4
        y = res[core]["y"]
        y_p[4 * core:4 * core + 4] = y[:NP_TOK].reshape(4, 256, D)
        y_s[g, q * NS_TOK:(q + 1) * NS_TOK] = y[NP_TOK:]
        ckv[4 * core:4 * core + 4] = res[core]["o_ckv"]
        krope[4 * core:4 * core + 4] = res[core]["o_krope"]
        hg[4 * core:4 * core + 4] = res[core]["o_hg"]
        ok[4 * core:4 * core + 4] = res[core]["o_k"].reshape(4, 1, 256, 4, 64)
        ov[4 * core:4 * core + 4] = res[core]["o_v"].reshape(4, 1, 256, 4, 64)
    return y_p, y_s, ckv, krope, hg, ok, ov


def kernel(**inputs):
    res = run(inputs, {})
    return assemble(res)
```

```python
import numpy as np
from contextlib import ExitStack
import concourse.bass as bass
import concourse.mybir as mybir
from concourse.bass_utils import run_bass_kernel_spmd

F32 = mybir.dt.float32
BF16 = mybir.dt.bfloat16
AF = mybir.ActivationFunctionType
ALU = mybir.AluOpType
AX = mybir.AxisListType

D = 1024
DFF = 2816
DEPTH = 4
NP_TOK = 1024
NS_TOK = 512
T = NP_TOK + NS_TOK
NT = T // 512
EPS = 1e-6
NCORES = 8


class Tok:
    __slots__ = ("sem", "val", "eng")

    def __init__(self, sem, val, eng):
        self.sem, self.val, self.eng = sem, val, eng


class _Rec:
    def __init__(self):
        self.call = None

    def __getattr__(self, name):
        def f(*a, **k):
            assert self.call is None
            self.call = (name, a, k)
            return None
        return f


def _freeze(fn):
    r = _Rec()
    fn(r)
    name, a, k = r.call
    return lambda e: getattr(e, name)(*a, **k)


class Prog:
    ENG = ("pe", "act", "dve", "pool", "sp")
    EPOCH = 6000

    def __init__(self, nc, stack):
        self.nc = nc
        self.stack = stack
        self.ops = {e: [] for e in self.ENG}
        self.cur_sem = {}
        self.cur_cnt = {}
        for e in self.ENG:
            self._new_epoch(e)
        self.known = {e: {} for e in self.ENG}
        self.last_w = {}
        self.readers = {}
        self.dma_sem = {}
        self.dma_cnt = {}
        self.nsem = 0
        self.final_toks = []

    def _sem(self, name):
        self.nsem = getattr(self, "nsem", 0) + 1
        return self.stack.enter_context(self.nc.semaphore(name))

    def _new_epoch(self, e):
        self._ep = getattr(self, "_ep", 0) + 1
        self.cur_sem[e] = self._sem(f"c_{e}_{self._ep}")
        self.cur_cnt[e] = 0

    def _waits_for(self, eng, reads, writes):
        toks = []
        for k in reads:
            t = self.last_w.get(k)
            if t is not None:
                toks.append(t)
        for k in writes:
            t = self.last_w.get(k)
            if t is not None:
                toks.append(t)
            for t in self.readers.get(k, {}).values():
                toks.append(t)
        need = {}
        for t in toks:
            if t.eng == eng:
                if eng == "pe":
                    continue
                if t.sem is self.cur_sem[eng] and t.val < self.cur_cnt[eng] - 1:
                    continue
                if t.sem is not self.cur_sem[eng]:
                    continue
            kn = self.known[eng].get(id(t.sem), 0)
            if kn >= t.val:
                continue
            if need.get(id(t.sem), (None, 0))[1] < t.val:
                need[id(t.sem)] = (t.sem, t.val)
        out = []
        for sid, (s, v) in need.items():
            self.known[eng][sid] = v
            out.append((s, v))
        return out

    def _record(self, tok, reads, writes, rkey):
        for k in writes:
            self.last_w[k] = tok
            self.readers[k] = {}
        for k in reads:
            self.readers.setdefault(k, {})[rkey] = tok

    def op(self, eng, fn, reads=(), writes=()):
        if self.cur_cnt[eng] >= self.EPOCH:
            self._new_epoch(eng)
        waits = self._waits_for(eng, reads, writes)
        self.cur_cnt[eng] += 1
        tok = Tok(self.cur_sem[eng], self.cur_cnt[eng], eng)
        self.ops[eng].append((waits, _freeze(fn), (tok.sem, 1)))
        self._record(tok, reads, writes, eng)
        return tok

    def dma(self, q, fn, chan, reads=(), writes=(), final=False):
        if chan not in self.dma_sem:
            self.dma_sem[chan] = self._sem("d_" + str(len(self.dma_sem)))
            self.dma_cnt[chan] = 0
        waits = self._waits_for(q, reads, writes)
        self.dma_cnt[chan] += 16
        tok = Tok(self.dma_sem[chan], self.dma_cnt[chan], "dma")
        self.ops[q].append((waits, _freeze(fn), (tok.sem, 16)))
        self._record(tok, reads, writes, ("dma", chan))
        if final:
            self.final_toks.append(tok)
        return tok

    def coll(self, fn, chan, reads=(), writes=()):
        if chan not in self.dma_sem:
            self.dma_sem[chan] = self._sem("cc_" + str(len(self.dma_sem)))
            self.dma_cnt[chan] = 0
        waits = self._waits_for("pool", reads, writes)
        self.dma_cnt[chan] += 1
        tok = Tok(self.dma_sem[chan], self.dma_cnt[chan], "dma")
        self.ops["pool"].append((waits, _freeze(fn), (tok.sem, None)))
        self._record(tok, reads, writes, ("dma", chan))
        return tok

    def barrier(self, chans=()):
        toks = [Tok(self.cur_sem[e], self.cur_cnt[e], e) for e in self.ENG if self.cur_cnt[e] > 0]
        for c in chans:
            if c in self.dma_sem:
                toks.append(Tok(self.dma_sem[c], self.dma_cnt[c], "dma"))
        for e in self.ENG:
            waits = []
            for t in toks:
                if t.eng == e:
                    continue
                if self.known[e].get(id(t.sem), 0) >= t.val:
                    continue
                self.known[e][id(t.sem)] = t.val
                waits.append((t.sem, t.val))
            if waits:
                self.ops[e].append((waits, None, None))

    def emit(self):
        nc = self.nc
        fin = {}
        for t in self.final_toks:
            if fin.get(id(t.sem), (None, 0))[1] < t.val:
                fin[id(t.sem)] = (t.sem, t.val)
        self.ops["sp"].append((list(fin.values()), None, None))
        with nc.Block() as block:
            def run(eng_obj, lst):
                for waits, fn, inc in lst:
                    for s, v in waits:
                        eng_obj.wait_ge(s, v)
                    if fn is not None:
                        ins = fn(eng_obj)
                        if inc is not None:
                            if inc[1] is None:
                                ins.then_inc(inc[0])
                            else:
                                ins.then_inc(inc[0], inc[1])

            @block.tensor
            def _(e):
                run(e, self.ops["pe"])

            @block.scalar
            def _(e):
                run(e, self.ops["act"])

            @block.vector
            def _(e):
                run(e, self.ops["dve"])

            @block.gpsimd
            def _(e):
                run(e, self.ops["pool"])

            @block.sync
            def _(e):
                run(e, self.ops["sp"])


W_SPECS = {
    "ada_w": [DEPTH, D, 6 * D], "ffn_w_gate": [DEPTH, D, DFF], "ffn_w_up": [DEPTH, D, DFF],
    "ffn_w_down": [DEPTH, DFF, D],
    "mla_w_dq": [2, D, 512], "mla_w_uq": [2, 512, 1536], "mla_w_dkv": [2, D, 320],
    "mla_w_uk": [2, 256, 1024], "mla_w_uv": [2, 256, 1024], "mla_w_o": [2, 1024, D],
    "swa_w_q": [1, D, D], "swa_w_k": [1, D, 256], "swa_w_v": [1, D, 256], "swa_w_o": [1, D, D],
    "hg_w_q": [1, D, D], "hg_w_f": [1, 2, D, D], "hg_w_i": [1, D, D], "hg_w_g": [1, D, D], "hg_w_o": [1, D, D],
}
NCONST = 2048
AR = 40960


class K:
    IDENT = 0
    ONESN = 128
    MASKF = 256
    MASKB = 384
    CVEC = 512
    ADAB = 528
    NMIX = 720
    NFFN = 752
    NFIN = 784
    QNORM = 792
    KVNORM = 800
    PERM = 804
    COS = 868
    SIN = 1380
    LBL = 1892
    ONORM = 1956
    RMASK = 1957
    SINK = 1973
    SEL = 1989
    END = 1997


def build(flags):
    nc = bass.Bass("TRN2", target_bir_lowering=False)
    stack = ExitStack()
    depth = flags.get("depth", DEPTH)
    mixers = flags.get("mixers", 1)
    with stack:
        P = Prog(nc, stack)
        dr = {}
        dr["xin"] = nc.dram_tensor("xin", [T, D], F32, kind="ExternalInput").ap()
        dr["consts"] = nc.dram_tensor("consts", [128, NCONST], F32, kind="ExternalInput").ap()
        dr["ckv_ctxT"] = nc.dram_tensor("ckv_ctxT", [2, 128, 2, 256], F32, kind="ExternalInput").ap()
        dr["krope_ctxT"] = nc.dram_tensor("krope_ctxT", [2, 64, 256], F32, kind="ExternalInput").ap()
        for k, shp in W_SPECS.items():
            dr[k] = nc.dram_tensor(k, shp, F32, kind="ExternalInput").ap()
        dr["y"] = nc.dram_tensor("y", [T, D], F32, kind="ExternalOutput").ap()
        dr["o_ckv"] = nc.dram_tensor("o_ckv", [4, 2, 256, 256], F32, kind="ExternalOutput").ap()
        dr["o_krope"] = nc.dram_tensor("o_krope", [4, 2, 256, 64], F32, kind="ExternalOutput").ap()
        dr["swa_kctxT"] = nc.dram_tensor("swa_kctxT", [64, 4, 256], F32, kind="ExternalInput").ap()
        dr["swa_vctx"] = nc.dram_tensor("swa_vctx", [256, 256], F32, kind="ExternalInput").ap()
        dr["bandmask"] = nc.dram_tensor("bandmask", [128, 1024], F32, kind="ExternalInput").ap()
        dr["o_k"] = nc.dram_tensor("o_k", [4, 1, 256, 256], F32, kind="ExternalOutput").ap()
        dr["o_v"] = nc.dram_tensor("o_v", [4, 1, 256, 256], F32, kind="ExternalOutput").ap()
        swin = nc.dram_tensor("swin", [128, 1536], BF16)
        swout = nc.dram_tensor("swout", [4 * 128, 1536], BF16)
        dr["hg_s0"] = nc.dram_tensor("hg_s0", [2, 8, 128, 128], F32, kind="ExternalInput").ap()
        dr["o_hg"] = nc.dram_tensor("o_hg", [4, 1, 2, 8, 128, 128], F32, kind="ExternalOutput").ap()
        hgin = [nc.dram_tensor(f"hgin{d_}", [128, 1032], F32) for d_ in range(2)]
        hgout = [nc.dram_tensor(f"hgout{d_}", [4 * 128, 1032], F32) for d_ in range(2)]
        agin = [nc.dram_tensor(f"agin{j}", [128, 1536], BF16) for j in range(2)]
        agout = [nc.dram_tensor(f"agout{j}", [4 * 128, 1536], BF16) for j in range(2)]

        def sb(name, shape, dt):
            return stack.enter_context(nc.sbuf_tensor(name, shape, dt))

        xT = sb("xT", [128, 8, T], F32)
        hT = sb("hT", [128, 8, T], BF16)
        cst = sb("cst", [128, NCONST], F32)
        cstb = sb("cstb", [128, 512], BF16)
        NSLOT = 4
        wring = sb("wring", [128, NSLOT, 4096], BF16)
        mod = sb("mod", [128, 2, 48], F32)
        gm = sb("gm", [128, 2, 2, 8], F32)
        silc = sb("silc", [128, 16], BF16)
        rstd = sb("rstd", [128, 512], F32)
        stage = sb("stage", [128, 2, 1024], F32)
        hgs = sb("hgs", [128, 128], F32)
        hgs2 = sb("hgs2", [128, 32], F32)
        lbw = sb("lbw", [128, 144], F32)
        ones64 = sb("ones64", [128, 64], F32)
        A = sb("arena", [128, AR], BF16)
        ps = stack.enter_context(nc.psum_tensor("ps", [128, 8, 512], F32))

        def carve(off, shape, dt=BF16):
            n = 1
            for s_ in shape[1:]:
                n *= s_
            if dt == F32:
                v = A[:, off:off + 2 * n].bitcast(F32)
            else:
                v = A[:, off:off + n]
            if len(shape) == 3:
                v = v.rearrange("p (a b) -> p a b", a=shape[1])
            elif len(shape) == 4:
                v = v.rearrange("p (a b c) -> p a b c", a=shape[1], b=shape[2])
            return v

        sq_default = carve(0, [128, 8, 512])
        ytmp = carve(4096, [128, 8, 512], F32)
        hid = carve(12288, [128, 2, 8, T])
        sgt = carve(36864, [128, 2, 512], F32)

        ident = cst[:, K.IDENT:K.IDENT + 128]
        identb = cstb[:, 0:128]
        onesb = cstb[:, 128:256]

        state = {"bank": 0, "slot": 0, "ev": 0, "banks": list(range(8))}

        def bank():
            bl = state["banks"]
            b = bl[state["bank"] % len(bl)]
            state["bank"] += 1
            return b

        def evac(dst, src, reads, writes, eng=None):
            if eng is None:
                eng = "act" if state["ev"] % 2 == 0 else "dve"
                state["ev"] += 1
            if eng == "act":
                return P.op("act", lambda e: e.copy(dst, src), reads=reads, writes=writes)
            return P.op("dve", lambda e: e.tensor_copy(dst, src), reads=reads, writes=writes)

        P.dma("sp", lambda e: e.dma_start(out=cst[:], in_=dr["consts"]), "cst", writes=["cst"])
        P.dma("pool", lambda e: e.dma_start(out=cstb[:], in_=dr["consts"][:, 0:512]), "cstb", writes=["cstb"])

        def wload(src, a, b):
            s = state["slot"]
            state["slot"] = (s + 1) % NSLOT
            assert a * b <= 4096
            dst = wring[:, s, 0:a * b].rearrange("p (a b) -> p a b", a=a)
            P.dma("pool", lambda e: e.dma_start(out=dst, in_=src), ("w", s), writes=[("w", s)])
            return dst, ("w", s)

        def kmajor(w2d):
            return w2d.rearrange("(kc p) n -> p kc n", p=128)

        for tt in range(T // 128):
            sl = tt % 2
            P.dma("sp", lambda e, tt=tt, sl=sl: e.dma_start(out=stage[:, sl, :], in_=dr["xin"][tt * 128:(tt + 1) * 128, :]),
                  ("stage", sl), writes=[("stage", sl)])
            for half in range(2):
                b = bank()
                for j in range(4):
                    c = half * 4 + j
                    P.op("pe", lambda e, b=b, j=j, c=c, sl=sl: e.transpose(ps[:, b, j * 128:(j + 1) * 128], stage[:, sl, c * 128:(c + 1) * 128], ident),
                         reads=[("stage", sl), "cst"], writes=[("ps", b)])
                evac(xT[:, half * 4:half * 4 + 4, tt * 128:(tt + 1) * 128], ps[:, b, :].rearrange("p (j n) -> p j n", j=4),
                     [("ps", b)], [("x", tt // 4)])

        def cond_of(t):
            return 0 if t < 2 else 1

        ADA_BANK = 7

        def adaln_piece(layer, pc):
            if layer == 0 and pc == 0:
                P.op("act", lambda e: e.activation(silc[:], cst[:, K.CVEC:K.CVEC + 16], AF.Silu), reads=["cst"], writes=["silc"])
            b = ADA_BANK
            wv = kmajor(dr["ada_w"][layer])
            wt, wk = wload(wv[:, :, pc * 512:(pc + 1) * 512], 8, 512)
            for jc in range(4):
                j = pc * 4 + jc
                for kc in range(8):
                    P.op("pe", lambda e, wt=wt, jc=jc, kc=kc, j=j, b=b: e.matmul(ps[:, b, 2 * j:2 * j + 2], wt[:, kc, jc * 128:(jc + 1) * 128], silc[:, 2 * kc:2 * kc + 2], start=(kc == 0), stop=(kc == 7)),
                         reads=[wk, "silc"], writes=[("ps", b)])

        def adaln_finish(layer):
            b = ADA_BANK
            for c in range(2):
                P.op("dve", lambda e, c=c, b=b: e.tensor_tensor(mod[:, c, :], ps[:, b, 0:96].rearrange("p (j c) -> p j c", c=2)[:, :, c], cst[:, K.ADAB + layer * 48:K.ADAB + (layer + 1) * 48], ALU.add),
                     reads=[("ps", b), "cst"], writes=["mod"])
            for n, (goff, so) in enumerate(((K.NMIX, 8), (K.NFFN, 32))):
                for c in range(2):
                    P.op("dve", lambda e, n=n, c=c, goff=goff, so=so: e.scalar_tensor_tensor(gm[:, n, c, :], mod[:, c, so:so + 8], 1.0, cst[:, goff + layer * 8:goff + layer * 8 + 8], ALU.add, ALU.mult),
                         reads=["mod", "cst"], writes=["gm"])

        def rms_rstd(src_fn, nch, srckeys, mscale, sq=None):
            if sq is None:
                sq = sq_default
            P.op("act", lambda e: e.activation(sq[:, 0:nch, :], src_fn(), AF.Square), reads=srckeys, writes=["sq"])
            b = bank()
            for c in range(nch):
                P.op("pe", lambda e, c=c, b=b: e.matmul(ps[:, b, :], onesb, sq[:, c, :], start=(c == 0), stop=(c == nch - 1)),
                     reads=["sq", "cstb"], writes=[("ps", b)])
            P.op("act", lambda e, b=b: e.activation(rstd[:], ps[:, b, :], AF.Ln, bias=EPS, scale=mscale), reads=[("ps", b)], writes=["rstd"])
            P.op("act", lambda e: e.activation(rstd[:], rstd[:], AF.Exp, scale=-0.5), reads=["rstd"], writes=["rstd"])

        def norm_tile(t, gain_fn, shift_fn, out_fn, out_keys):
            tok = slice(t * 512, (t + 1) * 512)
            rms_rstd(lambda: xT[:, :, tok], 8, [("x", t)], 1.0)
            for c in range(8):
                P.op("dve", lambda e, c=c: e.tensor_tensor(ytmp[:, c, :], xT[:, c, tok], rstd[:], ALU.mult),
                     reads=[("x", t), "rstd"], writes=[("ytmp", c)])
                P.op("act", lambda e, c=c: e.activation(out_fn(c), ytmp[:, c, :], AF.Identity, bias=shift_fn(c), scale=gain_fn(c)),
                     reads=[("ytmp", c), "gm", "mod", "cst"], writes=out_keys(c))

        def modnorm(n):
            so = 0 if n == 0 else 24
            for t in range(NT):
                cd = cond_of(t)
                norm_tile(t,
                          lambda c, cd=cd: gm[:, n, cd, c:c + 1],
                          lambda c, cd=cd: mod[:, cd, so + c:so + c + 1],
                          lambda c, t=t: hT[:, c, t * 512:(t + 1) * 512],
                          lambda c, t=t: [("h", t)])

        def resid_proj(wsrc, in_fn, in_keys, goff):
            wv = kmajor(wsrc)
            for pc in range(2):
                wt, wk = wload(wv[:, :, pc * 512:(pc + 1) * 512], 8, 512)
                for mc in range(4):
                    m = pc * 4 + mc
                    for t in range(NT):
                        tok = slice(t * 512, (t + 1) * 512)
                        cd = cond_of(t)
                        b = bank()
                        for kc in range(8):
                            P.op("pe", lambda e, kc=kc, b=b, wt=wt, mc=mc, t=t: e.matmul(ps[:, b, :], wt[:, kc, mc * 128:(mc + 1) * 128], in_fn(kc, t), start=(kc == 0), stop=(kc == 7)),
                                 reads=[wk] + in_keys(t), writes=[("ps", b)])
                        P.op("dve", lambda e, b=b, m=m, tok=tok, cd=cd: e.scalar_tensor_tensor(xT[:, m, tok], ps[:, b, :], mod[:, cd, goff + m:goff + m + 1], xT[:, m, tok], ALU.mult, ALU.add),
                             reads=[("ps", b), "mod", ("x", t)], writes=[("x", t)])

        def ffn(layer, ada_next=None):
            groups = [(0, 8), (8, 8), (16, 6)]
            state["banks"] = [0, 1, 2, 3, 4, 5, 6]
            ada_todo = list(range(12)) if ada_next is not None else []

            def ada_step():
                if ada_todo:
                    adaln_piece(ada_next, ada_todo.pop(0))

            wg = kmajor(dr["ffn_w_gate"][layer])
            wu = kmajor(dr["ffn_w_up"][layer])
            wd = dr["ffn_w_down"][layer].rearrange("(f p) n -> p f n", p=128)
            go = 40
            for gi, (f0, nf) in enumerate(groups):
                hb = gi % 2
                npc = (nf + 3) // 4
                for pc in range(npc):
                    nfc = min(4, nf - pc * 4)
                    c0 = (f0 + pc * 4) * 128
                    wgt, wgk = wload(wg[:, :, c0:c0 + nfc * 128], 8, nfc * 128)
                    wut, wuk = wload(wu[:, :, c0:c0 + nfc * 128], 8, nfc * 128)
                    for fc in range(nfc):
                        fl = pc * 4 + fc
                        for t in range(NT):
                            tok = slice(t * 512, (t + 1) * 512)
                            bg, bu = bank(), bank()
                            for kc in range(8):
                                P.op("pe", lambda e, kc=kc, bg=bg, wgt=wgt, fc=fc, tok=tok: e.matmul(ps[:, bg, :], wgt[:, kc, fc * 128:(fc + 1) * 128], hT[:, kc, tok], start=(kc == 0), stop=(kc == 7)),
                                     reads=[wgk, ("h", t)], writes=[("ps", bg)])
                            for kc in range(8):
                                P.op("pe", lambda e, kc=kc, bu=bu, wut=wut, fc=fc, tok=tok: e.matmul(ps[:, bu, :], wut[:, kc, fc * 128:(fc + 1) * 128], hT[:, kc, tok], start=(kc == 0), stop=(kc == 7)),
                                     reads=[wuk, ("h", t)], writes=[("ps", bu)])
                            sl = (fl * NT + t) % 2
                            P.op("act", lambda e, bg=bg, sl=sl: e.activation(sgt[:, sl, :], ps[:, bg, :], AF.Silu),
                                 reads=[("ps", bg)], writes=[("sgt", sl)])
                            P.op("dve", lambda e, bu=bu, sl=sl, hb=hb, fl=fl, tok=tok: e.tensor_tensor(hid[:, hb, fl, tok], sgt[:, sl, :], ps[:, bu, :], ALU.mult),
                                 reads=[("ps", bu), ("sgt", sl)], writes=[("hid", hb, t)])
                    ada_step()
                for pc in range(2):
                    wdt, wdk = wload(wd[:, f0:f0 + nf, pc * 512:(pc + 1) * 512], nf, 512)
                    for mc in range(4):
                        m = pc * 4 + mc
                        for t in range(NT):
                            tok = slice(t * 512, (t + 1) * 512)
                            cd = cond_of(t)
                            b = bank()
                            for fl in range(nf):
                                P.op("pe", lambda e, fl=fl, b=b, wdt=wdt, mc=mc, hb=hb, tok=tok: e.matmul(ps[:, b, :], wdt[:, fl, mc * 128:(mc + 1) * 128], hid[:, hb, fl, tok], start=(fl == 0), stop=(fl == nf - 1)),
                                     reads=[wdk, ("hid", hb, t)], writes=[("ps", b)])
                            P.op("dve", lambda e, b=b, m=m, tok=tok, cd=cd: e.scalar_tensor_tensor(xT[:, m, tok], ps[:, b, :], mod[:, cd, go + m:go + m + 1], xT[:, m, tok], ALU.mult, ALU.add),
                                 reads=[("ps", b), "mod", ("x", t)], writes=[("x", t)])
                    ada_step()
            while ada_todo:
                ada_step()
            state["banks"] = list(range(8))

        def attn_core(G, Lk, hoffs, blocks, score_ops, v_ops, dv, scale, out_fn, out_keys, cfg, Pb, PT, st, sinkv=None, tag=0):
            Sb = cfg["S"]
            nkc = (Lk + 127) // 128

            if isinstance(hoffs, int):
                hoffs = [g * hoffs for g in range(G)]

            def scol(g, off):
                col = hoffs[g] + off
                return Sb[col // 512], col % 512

            def K_(n):
                return (n, tag)

            skeys = [("S", b) for b in Sb]
            mx, negm, rs, rinv = st[:, 0:G], st[:, G:2 * G], st[:, 2 * G:3 * G], st[:, 3 * G:4 * G]
            tmpv = st[:, 4 * G:5 * G]
            bm = st[:, 5 * G:5 * G + 8]
            blockwise = (G == 1 and len(blocks) > 1)
            for g in range(G):
                for bi, (off, n) in enumerate(blocks):
                    b, c0 = scol(g, off)
                    assert c0 + n <= 512
                    ops_ = score_ops(g, off, n)
                    for i, (lt, rh, rk) in enumerate(ops_):
                        P.op("pe", lambda e, b=b, c0=c0, n=n, lt=lt, rh=rh, i=i, last=len(ops_) - 1: e.matmul(ps[:, b, c0:c0 + n], lt, rh, start=(i == 0), stop=(i == last)),
                             reads=rk, writes=[("S", b)])
                    if blockwise:
                        P.op("dve", lambda e, b=b, c0=c0, n=n, bi=bi: e.tensor_reduce(bm[:, bi:bi + 1], ps[:, b, c0:c0 + n], AX.X, ALU.max), reads=[("S", b)], writes=[K_("st_bm")])
            if blockwise:
                P.op("dve", lambda e: e.tensor_reduce(mx, bm[:, 0:len(blocks)], AX.X, ALU.max), reads=[K_("st_bm")], writes=[K_("st_mx")])
            elif all(hoffs[g] == g * Lk for g in range(G)):
                sview = ps[:, Sb[0]:Sb[0] + len(Sb), :].rearrange("p b n -> p (b n)")[:, 0:G * Lk].rearrange("p (g k) -> p g k", g=G)
                P.op("dve", lambda e: e.tensor_reduce(mx, sview, AX.X, ALU.max), reads=skeys, writes=[K_("st_mx")])
            else:
                for g in range(G):
                    col = hoffs[g]
                    sv = ps[:, Sb[0]:Sb[0] + len(Sb), :].rearrange("p b n -> p (b n)")[:, col:col + Lk]
                    P.op("dve", lambda e, g=g, sv=sv: e.tensor_reduce(mx[:, g:g + 1], sv, AX.X, ALU.max), reads=skeys, writes=[K_("st_mx")])
            if sinkv is not None:
                P.op("dve", lambda e: e.scalar_tensor_tensor(mx, mx, scale, sinkv, ALU.mult, ALU.max), reads=[K_("st_mx"), "cst"], writes=[K_("st_mx")])
                P.op("dve", lambda e: e.tensor_scalar(negm, mx, -1.0, None, ALU.mult), reads=[K_("st_mx")], writes=[K_("st_negm")])
            else:
                P.op("dve", lambda e: e.tensor_scalar(negm, mx, -scale, None, ALU.mult), reads=[K_("st_mx")], writes=[K_("st_negm")])
            for g in range(G):
                col = hoffs[g]
                sv = ps[:, Sb[0]:Sb[0] + len(Sb), :].rearrange("p b n -> p (b n)")[:, col:col + Lk]
                P.op("act", lambda e, g=g, sv=sv: e.activation(Pb(g), sv, AF.Exp, bias=negm[:, g:g + 1], scale=scale, accum_out=rs[:, g:g + 1]),
                     reads=skeys + [K_("st_negm")], writes=[("Pb", g, tag), K_("st_rs")])
            if sinkv is not None:
                P.op("dve", lambda e: e.tensor_tensor(tmpv, sinkv, negm, ALU.add), reads=[K_("st_negm"), "cst"], writes=[K_("st_tmp")])
                P.op("act", lambda e: e.activation(tmpv, tmpv, AF.Exp), reads=[K_("st_tmp")], writes=[K_("st_tmp")])
                P.op("dve", lambda e: e.tensor_tensor(rs, rs, tmpv, ALU.add), reads=[K_("st_tmp"), K_("st_rs")], writes=[K_("st_rs")])
            P.op("dve", lambda e: e.reciprocal(rinv, rs), reads=[K_("st_rs")], writes=[K_("st_rinv")])

            def phase2():
                ptb = cfg["PT"]
                idx = 0
                pend = []
                total = G * nkc
                for g in range(G):
                    for kc in range(nkc):
                        nk = min(128, Lk - kc * 128)
                        slot = idx % 8
                        pb_ = ptb[(idx // 8) % len(ptb)]
                        pv = ps[:, pb_, :].bitcast(BF16)
                        P.op("pe", lambda e, g=g, kc=kc, nk=nk, slot=slot, pv=pv: e.transpose(pv[0:nk, slot * 128:(slot + 1) * 128], Pb(g)[:, kc * 128:kc * 128 + nk], identb),
                             reads=[("Pb", g, tag), "cstb"], writes=[("ps", pb_)])
                        pend.append(idx)
                        idx += 1
                        if len(pend) == 8 or idx == total:
                            i0 = pend[0]
                            n_ = len(pend)
                            evac(PT(i0, n_), pv[:, 0:n_ * 128].rearrange("p (a b) -> p a b", a=n_), [("ps", pb_)], ["PT"])
                            pend = []
                ob, oc0 = cfg["O"]
                for g in range(G):
                    col = oc0 + g * dv
                    b = ob + col // 512
                    c0 = col % 512
                    for kc in range(nkc):
                        nk = min(128, Lk - kc * 128)
                        rh, rk = v_ops(g, kc, nk)
                        ii = g * nkc + kc
                        P.op("pe", lambda e, b=b, c0=c0, ii=ii, nk=nk, rh=rh, kc=kc: e.matmul(ps[:, b, c0:c0 + dv], PT(ii, 1)[0:nk, 0, :], rh, start=(kc == 0), stop=(kc == nkc - 1)),
                             reads=["PT"] + rk, writes=["Oacc"])
                    P.op("act", lambda e, b=b, c0=c0, g=g: e.activation(out_fn(g), ps[:, b, c0:c0 + dv], AF.Identity, scale=rinv[:, g:g + 1]),
                         reads=["Oacc", K_("st_rinv")], writes=out_keys)

            return phase2

        def otok_to_oT(otok_v, okeys, tokcol, cfg):
            pb_ = cfg["PT"][0]
            pv = ps[:, pb_, :].bitcast(BF16)
            for c in range(8):
                P.op("pe", lambda e, c=c, pv=pv: e.transpose(pv[:, c * 128:(c + 1) * 128], otok_v[:, c * 128:(c + 1) * 128], identb),
                     reads=okeys + ["cstb"], writes=[("ps", pb_)])
            evac(hT[:, :, tokcol:tokcol + 128], pv.rearrange("p (a b) -> p a b", a=8), [("ps", pb_)], [("h", tokcol // 512)])

        def mla(layer):
            j = layer // 3
            SC = 192 ** -0.5
            P.barrier()
            qn = carve(0, [128, 4, T])
            ckb = carve(6144, [128, 2, 3328])
            krb = carve(12800, [128, 3328])
            B0 = 16128
            qlf = carve(B0, [128, 4, 512], F32)
            sqm = carve(B0 + 4096, [128, 4, 512])
            ckf = carve(B0 + 6144, [128, 2, 512], F32)
            krf = carve(B0 + 8192, [128, 512], F32)
            krt = carve(B0 + 9216, [128, 512], F32)
            agst = carve(B0 + 10240, [128, 1536])
            P.dma("pool", lambda e: e.dma_start(out=ckb[:, :, 1024:1280], in_=dr["ckv_ctxT"][j]), "ckctx", writes=["ckb_ctx"])
            P.dma("pool", lambda e: e.dma_start(out=krb[0:64, 1024:1280], in_=dr["krope_ctxT"][j]), "krctx", writes=["krb_ctx"])
            wdq, wdqk = wload(kmajor(dr["mla_w_dq"][j]), 8, 512)
            wdkv, wdkvk = wload(kmajor(dr["mla_w_dkv"][j]), 8, 320)
            state["banks"] = list(range(8))
            for t in range(NT):
                tok = slice(t * 512, (t + 1) * 512)
                for oc in range(4):
                    b = bank()
                    for kc in range(8):
                        P.op("pe", lambda e, b=b, kc=kc, oc=oc, tok=tok: e.matmul(ps[:, b, :], wdq[:, kc, oc * 128:(oc + 1) * 128], hT[:, kc, tok], start=(kc == 0), stop=(kc == 7)),
                             reads=[wdqk, ("h", t)], writes=[("ps", b)])
                    evac(qlf[:, oc, :], ps[:, b, :], [("ps", b)], ["qlf"])
                rms_rstd(lambda: qlf[:], 4, ["qlf"], 2.0, sqm)
                for oc in range(4):
                    P.op("dve", lambda e, oc=oc: e.tensor_tensor(qlf[:, oc, :], qlf[:, oc, :], rstd[:], ALU.mult), reads=["qlf", "rstd"], writes=["qlf"])
                    P.op("act", lambda e, oc=oc, tok=tok: e.activation(qn[:, oc, tok], qlf[:, oc, :], AF.Identity, scale=cst[:, K.QNORM + j * 4 + oc:K.QNORM + j * 4 + oc + 1]),
                         reads=["qlf", "cst"], writes=[("qn", t)])
                for oc in range(2):
                    b = bank()
                    for kc in range(8):
                        P.op("pe", lambda e, b=b, kc=kc, oc=oc, tok=tok: e.matmul(ps[:, b, :], wdkv[:, kc, oc * 128:(oc + 1) * 128], hT[:, kc, tok], start=(kc == 0), stop=(kc == 7)),
                             reads=[wdkvk, ("h", t)], writes=[("ps", b)])
                    evac(ckf[:, oc, :], ps[:, b, :], [("ps", b)], ["ckf"])
                b = bank()
                for kc in range(8):
                    P.op("pe", lambda e, b=b, kc=kc, tok=tok: e.matmul(ps[0:64, b, :], wdkv[:, kc, 256:320], hT[:, kc, tok], start=(kc == 0), stop=(kc == 7)),
                         reads=[wdkvk, ("h", t)], writes=[("ps", b)])
                evac(krf[0:64, :], ps[0:64, b, :], [("ps", b)], ["krf"])
                rms_rstd(lambda: ckf[:], 2, ["ckf"], 4.0, sqm)
                kcol = t * 512 if t < 2 else None
                for oc in range(2):
                    P.op("dve", lambda e, oc=oc: e.tensor_tensor(ckf[:, oc, :], ckf[:, oc, :], rstd[:], ALU.mult), reads=["ckf", "rstd"], writes=["ckf"])
                    P.op("act", lambda e, oc=oc: e.activation(ckf[:, oc, :], ckf[:, oc, :], AF.Identity, scale=cst[:, K.KVNORM + j * 2 + oc:K.KVNORM + j * 2 + oc + 1]),
                         reads=["ckf", "cst"], writes=["ckf"])
                    if t < 2:
                        evac(ckb[:, oc, kcol:kcol + 512], ckf[:, oc, :], ["ckf"], [("ckb", t)])
                    else:
                        evac(agst[:, oc * 512:(oc + 1) * 512], ckf[:, oc, :], ["ckf"], ["agst"])
                if t < 2:
                    evac(krb[0:64, kcol:kcol + 512], krf[0:64, :], ["krf"], [("krb", t)])
                    for q in range(4):
                        b = bank()
                        for oc in range(2):
                            P.op("pe", lambda e, b=b, oc=oc, q=q: e.transpose(ps[:, b, oc * 128:(oc + 1) * 128], ckf[:, oc, q * 128:(q + 1) * 128], ident),
                                 reads=["ckf", "cst"], writes=[("ps", b)])
                        P.op("pe", lambda e, b=b, q=q: e.transpose(ps[:, b, 256:320], krf[0:64, q * 128:(q + 1) * 128], ident[0:64, 0:64]),
                             reads=["krf", "cst"], writes=[("ps", b)])
                        sl = q % 2
                        evac(stage[:, sl, 0:320], ps[:, b, 0:320], [("ps", b)], [("stage", sl)])
                        gtok = t * 512 + q * 128
                        sq_, r0 = gtok // 256, gtok % 256
                        P.dma("sp", lambda e, sl=sl, sq_=sq_, r0=r0: e.dma_start(out=dr["o_ckv"][sq_, j, r0:r0 + 128, :], in_=stage[:, sl, 0:256]),
                              ("ost", sl), reads=[("stage", sl)], final=True)
                        P.dma("sp", lambda e, sl=sl, sq_=sq_, r0=r0: e.dma_start(out=dr["o_krope"][sq_, j, r0:r0 + 128, :], in_=stage[:, sl, 256:320]),
                              ("ost2", sl), reads=[("stage", sl)], final=True)
                else:
                    b = bank()
                    P.op("pe", lambda e, b=b: e.matmul(ps[0:64, b, :], cst[0:64, K.PERM:K.PERM + 64], krf[0:64, :], start=True, stop=True),
                         reads=["krf", "cst"], writes=[("ps", b)])
                    P.op("dve", lambda e, b=b: e.tensor_tensor(krt[0:64, :], ps[0:64, b, :], cst[0:64, K.SIN:K.SIN + 512], ALU.mult), reads=[("ps", b), "cst"], writes=["krt"])
                    P.op("dve", lambda e: e.tensor_tensor(krf[0:64, :], krf[0:64, :], cst[0:64, K.COS:K.COS + 512], ALU.mult), reads=["krf", "cst"], writes=["krf"])
                    P.op("dve", lambda e: e.tensor_tensor(agst[0:64, 1024:1536], krf[0:64, :], krt[0:64, :], ALU.add), reads=["krf", "krt"], writes=["agst"])
                    P.op("dve", lambda e: e.memset(agst[64:128, 1024:1536], 0.0), reads=[], writes=["agst"])
                    P.dma("sp", lambda e: e.dma_start(out=agin[j].ap(), in_=agst[:]), ("agin", j), reads=["agst"], writes=[("agin", j)])
                    P.coll(lambda e: e.collective_compute("AllGather", ALU.bypass, replica_groups=[[0, 1, 2, 3], [4, 5, 6, 7]],
                                                          ins=[agin[j].ap().opt()], outs=[agout[j].ap().opt()]),
                           ("agc", j), reads=[("agin", j)], writes=[("agout", j)])
                    agv = agout[j].ap().rearrange("(r p) n -> p r n", p=128)
                    for oc in range(2):
                        P.dma("sp", lambda e, oc=oc: e.dma_start(out=ckb[:, oc, 1280:3328].rearrange("p (r n) -> p r n", r=4), in_=agv[:, :, oc * 512:(oc + 1) * 512]),
                              ("agld", oc), reads=[("agout", j)], writes=["ckb_lat"])
                    P.dma("sp", lambda e: e.dma_start(out=krb[0:64, 1280:3328].rearrange("p (r n) -> p r n", r=4), in_=agv[0:64, :, 1024:1536]),
                          ("agld", 2), reads=[("agout", j)], writes=["krb_lat"])
            P.barrier(chans=[("agin", j)])
            wuq0, wuq0k = wload(kmajor(dr["mla_w_uq"][j])[:, :, 0:768], 4, 768)
            wuq1, wuq1k = wload(kmajor(dr["mla_w_uq"][j])[:, :, 768:1536], 4, 768)
            wuk, wukk = wload(kmajor(dr["mla_w_uk"][j]), 2, 1024)
            wuv, wuvk = wload(kmajor(dr["mla_w_uv"][j]), 2, 1024)

            def uq(h):
                w_, k_ = (wuq0, wuq0k) if h < 4 else (wuq1, wuq1k)
                return w_, k_, (h % 4) * 192

            qno = carve(B0, [128, 8, 512])
            qro = carve(B0 + 4096, [128, 8, 512])
            kn = carve(B0 + 8192, [128, 8, 512])
            V = carve(B0 + 12288, [128, 4, 1024])
            Pbp2 = [carve(B0 + 16384, [128, 8, 256]), carve(B0 + 22784, [128, 8, 256])]
            PTp = carve(B0 + 18432, [128, 16, 128])
            otok = carve(B0 + 20480, [128, 2, 1024])
            st = carve(B0 + 22528, [128, 128], F32)
            cfgp = {"S": [0, 1, 2, 3], "O": (4, 0), "PT": [6]}
            state["banks"] = [7]
            for t in range(2):
                tok = slice(t * 512, (t + 1) * 512)
                for h in range(8):
                    w_, k_, c0 = uq(h)
                    b = bank()
                    for kc in range(4):
                        P.op("pe", lambda e, b=b, kc=kc, w_=w_, c0=c0, tok=tok: e.matmul(ps[:, b, :], w_[:, kc, c0:c0 + 128], qn[:, kc, tok], start=(kc == 0), stop=(kc == 3)),
                             reads=[k_, ("qn", t)], writes=[("ps", b)])
                    evac(qno[:, h, :], ps[:, b, :], [("ps", b)], ["qno"])
                    b = bank()
                    for kc in range(4):
                        P.op("pe", lambda e, b=b, kc=kc, w_=w_, c0=c0, tok=tok: e.matmul(ps[0:64, b, :], w_[:, kc, c0 + 128:c0 + 192], qn[:, kc, tok], start=(kc == 0), stop=(kc == 3)),
                             reads=[k_, ("qn", t)], writes=[("ps", b)])
                    evac(qro[0:64, h, :], ps[0:64, b, :], [("ps", b)], ["qro"])
                    b = bank()
                    for oc in range(2):
                        P.op("pe", lambda e, b=b, oc=oc, h=h, tok=tok: e.matmul(ps[:, b, :], wuk[:, oc, h * 128:(h + 1) * 128], ckb[:, oc, tok], start=(oc == 0), stop=(oc == 1)),
                             reads=[wukk, ("ckb", t)], writes=[("ps", b)])
                    evac(kn[:, h, :], ps[:, b, :], [("ps", b)], ["kn"])
                for c in range(4):
                    for hf in range(2):
                        b = bank()
                        for oc in range(2):
                            P.op("pe", lambda e, b=b, oc=oc, c=c, hf=hf, t=t: e.matmul(ps[:, b, :], ckb[:, oc, t * 512 + c * 128:t * 512 + (c + 1) * 128], wuv[:, oc, hf * 512:(hf + 1) * 512], start=(oc == 0), stop=(oc == 1)),
                                 reads=[wuvk, ("ckb", t)], writes=[("ps", b)])
                        evac(V[:, c, hf * 512:(hf + 1) * 512], ps[:, b, :], [("ps", b)], ["V"])
                pend2 = None
                for s_ in range(2):
                    for qt in range(2):
                        q0 = s_ * 256 + qt * 128
                        ob = (s_ * 2 + qt) % 2

                        def score_ops(g, off, n, q0=q0, s_=s_):
                            return [(qno[:, g, q0:q0 + 128], kn[:, g, s_ * 256 + off:s_ * 256 + off + n], ["qno", "kn"]),
                                    (qro[0:64, g, q0:q0 + 128], krb[0:64, t * 512 + s_ * 256 + off:t * 512 + s_ * 256 + off + n], ["qro", ("krb", t)])]

                        def v_ops(g, kc, nk, s_=s_):
                            return V[0:nk, s_ * 2 + kc, g * 128:(g + 1) * 128], ["V"]

                        p2 = attn_core(8, 256, 256, [(0, 256)], score_ops, v_ops, 128, SC,
                                       lambda g, ob=ob: otok[:, ob, g * 128:(g + 1) * 128], [("otok", ob)], cfgp,
                                       lambda g, ob=ob: Pbp2[ob][:, g, :], lambda i0, n_: PTp[:, i0:i0 + n_, :], st[:, ob * 64:(ob + 1) * 64], tag=ob)

                        def fin(p2=p2, ob=ob, tc=t * 512 + q0):
                            p2()
                            otok_to_oT(otok[:, ob, :], [("otok", ob)], tc, cfgp)

                        if pend2 is not None:
                            pend2()
                        pend2 = fin
                pend2()
            P.barrier()
            qh = carve(B0, [128, 2, 512])
            qr = carve(B0 + 1024, [128, 2, 512])
            qraw = carve(B0 + 2048, [128, 512], F32)
            qt1 = carve(B0 + 3072, [128, 512], F32)
            kns = carve(B0 + 4096, [128, 2, 2304])
            vh = carve(B0 + 8704, [128, 2, 18, 128])
            Pbs = carve(B0 + 13312, [128, 2, 2304])
            PTs = carve(B0 + 17920, [128, 18, 128])
            otoks = carve(B0 + 20224, [128, 4, 1024])
            sts = carve(B0 + 24320, [128, 128], F32)
            pend_s = [None]
            cfgs = {"S": [0, 1, 2, 3, 4], "O": (5, 0), "PT": [6]}
            t = 2
            tok = slice(1024, 1536)
            kblocks = [(0, 512), (512, 512), (1024, 512), (1536, 512), (2048, 256)]
            it = 0
            for h in range(8):
                hb = h % 2
                w_, k_, c0 = uq(h)
                b = bank()
                for kc in range(4):
                    P.op("pe", lambda e, b=b, kc=kc, w_=w_, c0=c0, tok=tok: e.matmul(ps[:, b, :], w_[:, kc, c0:c0 + 128], qn[:, kc, tok], start=(kc == 0), stop=(kc == 3)),
                         reads=[k_, ("qn", t)], writes=[("ps", b)])
                evac(qh[:, hb, :], ps[:, b, :], [("ps", b)], [("qh", hb)])
                b = bank()
                for kc in range(4):
                    P.op("pe", lambda e, b=b, kc=kc, w_=w_, c0=c0, tok=tok: e.matmul(ps[0:64, b, :], w_[:, kc, c0 + 128:c0 + 192], qn[:, kc, tok], start=(kc == 0), stop=(kc == 3)),
                         reads=[k_, ("qn", t)], writes=[("ps", b)])
                evac(qraw[0:64, :], ps[0:64, b, :], [("ps", b)], ["qraw"], eng="act")
                b = bank()
                P.op("pe", lambda e, b=b: e.matmul(ps[0:64, b, :], cst[0:64, K.PERM:K.PERM + 64], qraw[0:64, :], start=True, stop=True),
                     reads=["qraw", "cst"], writes=[("ps", b)])
                P.op("dve", lambda e, b=b: e.tensor_tensor(qt1[0:64, :], ps[0:64, b, :], cst[0:64, K.SIN:K.SIN + 512], ALU.mult), reads=[("ps", b), "cst"], writes=["qt1"])
                P.op("dve", lambda e: e.tensor_tensor(qraw[0:64, :], qraw[0:64, :], cst[0:64, K.COS:K.COS + 512], ALU.mult), reads=["qraw", "cst"], writes=["qraw"])
                P.op("dve", lambda e, hb=hb: e.tensor_tensor(qr[0:64, hb, :], qraw[0:64, :], qt1[0:64, :], ALU.add), reads=["qraw", "qt1"], writes=[("qr", hb)])
                for (off, n) in kblocks:
                    b = bank()
                    for oc in range(2):
                        P.op("pe", lambda e, b=b, oc=oc, h=h, off=off, n=n: e.matmul(ps[:, b, 0:n], wuk[:, oc, h * 128:(h + 1) * 128], ckb[:, oc, 1024 + off:1024 + off + n], start=(oc == 0), stop=(oc == 1)),
                             reads=[wukk, "ckb_ctx", "ckb_lat"], writes=[("ps", b)])
                    evac(kns[:, hb, off:off + n], ps[:, b, 0:n], [("ps", b)], [("kns", hb)])
                for k4 in range(5):
                    nk4 = min(4, 18 - k4 * 4)
                    b = bank()
                    for kk in range(nk4):
                        kc = k4 * 4 + kk
                        for oc in range(2):
                            P.op("pe", lambda e, b=b, oc=oc, kk=kk, kc=kc, h=h: e.matmul(ps[:, b, kk * 128:(kk + 1) * 128], ckb[:, oc, 1024 + kc * 128:1024 + (kc + 1) * 128], wuv[:, oc, h * 128:(h + 1) * 128], start=(oc == 0), stop=(oc == 1)),
                                 reads=[wuvk, "ckb_ctx", "ckb_lat"], writes=[("ps", b)])
                    evac(vh[:, hb, k4 * 4:k4 * 4 + nk4, :], ps[:, b, 0:nk4 * 128].rearrange("p (a b) -> p a b", a=nk4), [("ps", b)], [("vh", hb)])
                for qt in range(4):
                    q0 = qt * 128
                    pbuf = it % 2
                    it += 1

                    def score_ops(g, off, n, q0=q0, hb=hb):
                        return [(qh[:, hb, q0:q0 + 128], kns[:, hb, off:off + n], [("qh", hb), ("kns", hb)]),
                                (qr[0:64, hb, q0:q0 + 128], krb[0:64, 1024 + off:1024 + off + n], [("qr", hb), "krb_ctx", "krb_lat"])]

                    def v_ops(g, kc, nk, hb=hb):
                        return vh[0:nk, hb, kc, :], [("vh", hb)]

                    p2 = attn_core(1, 2304, 2304, kblocks, score_ops, v_ops, 128, SC,
                                   lambda g, qt=qt, h=h: otoks[:, qt, h * 128:(h + 1) * 128], [("otoks", qt)], cfgs,
                                   lambda g, pbuf=pbuf: Pbs[:, pbuf, :], lambda i0, n_: PTs[:, i0:i0 + n_, :], sts[:, pbuf * 64:(pbuf + 1) * 64], tag=pbuf)
                    if pend_s[0] is not None:
                        pend_s[0]()
                    pend_s[0] = p2
            pend_s[0]()
            for qt in range(4):
                otok_to_oT(otoks[:, qt, :], [("otoks", qt)], 1024 + qt * 128, cfgs)
            state["banks"] = list(range(8))
            P.barrier(chans=[("agin", j), "ckctx", "krctx", ("agld", 0), ("agld", 1), ("agld", 2)])
            resid_proj(dr["mla_w_o"][j], lambda kc, t: hT[:, kc, t * 512:(t + 1) * 512], lambda t: [("h", t)], 16)

        def hgrn(layer):
            j = layer // 3
            P.barrier()
            Vt = carve(0, [128, 4, 1024])
            qT = carve(4096, [128, 8, 512])
            gT = carve(8192, [128, 8, 512])
            Qt = carve(12288, [128, 8, 512])
            Kt = carve(16384, [128, 8, 512])
            Ktok = carve(20480, [128, 4, 1024])
            oacc = carve(24576, [128, 4, 1024], F32)
            tA = carve(32768, [128, 512], F32)
            tB = carve(32768 + 1024, [128, 512], F32)
            tC = carve(32768 + 2048, [128, 512], F32)
            tE = carve(32768 + 3072, [128, 512], F32)
            osq = carve(32768, [128, 1024], F32)
            onb = carve(32768 + 2048, [128, 1024])
            Sf = carve(36864, [128, 8, 128], F32)
            Sb = carve(38912, [128, 8, 128])
            attb = carve(39936, [128, 8, 128])
            Dd = hgs[:, 0:64].rearrange("p (h c) -> p h c", h=8)
            Fl = hgs[:, 64:72]
            ssq = hgs[:, 72:80]
            rsd = hgs[:, 80:88]
            Fm = hgs[:, 88:120].rearrange("p (r h) -> p r h", r=4)
            lg = cst[:, K.LBL:K.LBL + 64].rearrange("p (a l) -> p a l", l=4)
            le = lbw[:, 0:64].rearrange("p (a l) -> p a l", l=4)
            lmx, lsum, lnum = lbw[:, 64:80], lbw[:, 80:96], lbw[:, 96:112]
            lb, oml = lbw[:, 112:128], lbw[:, 128:144]
            P.op("dve", lambda e: e.tensor_reduce(lmx, lg, AX.X, ALU.max), reads=["cst"], writes=["lmx"])
            P.op("dve", lambda e: e.tensor_tensor(le, lg, lmx.unsqueeze(2).to_broadcast([128, 16, 4]), ALU.subtract), reads=["cst", "lmx"], writes=["le"])
            P.op("act", lambda e: e.activation(le, le, AF.Exp), reads=["le"], writes=["le"])
            P.op("dve", lambda e: e.tensor_reduce(lsum, le, AX.X, ALU.add), reads=["le"], writes=["lsum"])
            P.op("dve", lambda e: e.tensor_reduce(lnum, le[:, :, 1:layer + 1], AX.X, ALU.add), reads=["le"], writes=["lnum"])
            P.op("dve", lambda e: e.reciprocal(lsum, lsum), reads=["lsum"], writes=["lsum"])
            P.op("dve", lambda e: e.tensor_tensor(lb, lnum, lsum, ALU.mult), reads=["lsum", "lnum"], writes=["lb"])
            P.op("dve", lambda e: e.tensor_scalar(oml, lb, -1.0, 1.0, ALU.mult, ALU.add), reads=["lb"], writes=["lb"])
            P.op("dve", lambda e: e.memset(ones64[:], 1.0), writes=["ones64"])

            wq_d, wi_d, wg_d = kmajor(dr["hg_w_q"][j]), kmajor(dr["hg_w_i"][j]), kmajor(dr["hg_w_g"][j])
            wf_d = [kmajor(dr["hg_w_f"][j, 0]), kmajor(dr["hg_w_f"][j, 1])]
            maskb = [cstb[:, 256:384], cstb[:, 384:512]]
            AB, R1, R2, UB = [0, 1], [2, 3], [4, 5], [6, 7]

            def fm_proj(wd, out_fn, tsl, t, keyw):
                for pc in range(2):
                    wt, wk = wload(wd[:, :, pc * 512:(pc + 1) * 512], 8, 512)
                    for cc in range(4):
                        c = pc * 4 + cc
                        b = bank()
                        for kc in range(8):
                            P.op("pe", lambda e, b=b, kc=kc, wt=wt, cc=cc, tsl=tsl: e.matmul(ps[:, b, :], wt[:, kc, cc * 128:(cc + 1) * 128], hT[:, kc, tsl], start=(kc == 0), stop=(kc == 7)),
                                 reads=[wk, ("h", t)], writes=[("ps", b)])
                        out_fn(c, ps[:, b, :], ("ps", b))

            def scan_dir(t, d, seqs, sample):
                for (soff, slen) in seqs:
                    subs = list(range(soff // 128, (soff + slen) // 128))
                    if d == 1:
                        subs = subs[::-1]
                    chs = [0, 1] if d == 0 else [1, 0]
                    for outputs in ([False, True] if sample else [True]):
                        if not sample:
                            P.op("dve", lambda e: e.memset(Sf[:], 0.0), writes=["Sf"])
                            P.op("dve", lambda e: e.memset(Sb[:], 0.0), writes=["Sb"])
                        elif not outputs:
                            P.op("dve", lambda e: e.memset(Sf[:], 0.0), writes=["Sf"])
                        for s in subs:
                            cols = slice(s * 128, (s + 1) * 128)
                            if outputs:
                                for h in range(8):
                                    b = AB[h // 4]
                                    P.op("pe", lambda e, b=b, h=h, cols=cols: e.matmul(ps[:, b, (h % 4) * 128:(h % 4 + 1) * 128], Kt[:, h, cols], Qt[:, h, cols], start=True, stop=True),
                                         reads=["Kt", "Qt"], writes=[("ps", b)])
                                for hf in range(2):
                                    b = AB[hf]
                                    P.op("dve", lambda e, b=b, hf=hf: e.tensor_tensor(attb[:, hf * 4:hf * 4 + 4, :], ps[:, b, :].rearrange("p (a n) -> p a n", a=4), maskb[d].unsqueeze(1).to_broadcast([128, 4, 128]), ALU.mult),
                                         reads=[("ps", b), "cstb"], writes=["attb"])
                            for ch in chs:
                                r0 = ch * 64
                                ci = (s * 2 + ch)
                                tcols = slice(s * 128 + r0, s * 128 + r0 + 64)
                                if outputs:
                                    for h in range(8):
                                        b = R1[h // 4]
                                        P.op("pe", lambda e, b=b, h=h, r0=r0, tcols=tcols: e.matmul(ps[r0:r0 + 64, b, (h % 4) * 128:(h % 4 + 1) * 128], Qt[:, h, tcols], Sb[:, h, :], start=True, stop=True),
                                             reads=["Qt", "Sb"], writes=[("ps", b)])
                                for h in range(8):
                                    b = UB[h // 4]
                                    P.op("pe", lambda e, b=b, h=h, r0=r0, s=s: e.matmul(ps[:, b, (h % 4) * 128:(h % 4 + 1) * 128], Ktok[r0:r0 + 64, s, h * 128:(h + 1) * 128], Vt[r0:r0 + 64, s, h * 128:(h + 1) * 128], start=True, stop=True),
                                         reads=["Ktok", "Vt"], writes=[("ps", b)])
                                for hf in range(2):
                                    b = UB[hf]
                                    P.op("dve", lambda e, b=b, hf=hf: e.tensor_tensor(Sf[:, hf * 4:hf * 4 + 4, :], Sf[:, hf * 4:hf * 4 + 4, :], ps[:, b, :].rearrange("p (a n) -> p a n", a=4), ALU.add),
                                         reads=[("ps", b), "Sf"], writes=["Sf"])
                                P.op("dve", lambda e, ci=ci: e.tensor_tensor(Sf[:], Sf[:], Dd[:, :, ci % 8].unsqueeze(2).to_broadcast([128, 8, 128]), ALU.mult),
                                     reads=["Sf", "Dd"], writes=["Sf"])
                                if outputs:
                                    P.op("act", lambda e: e.copy(Sb[:], Sf[:]), reads=["Sf"], writes=["Sb"])
                            if outputs:
                                for h in range(8):
                                    b = R2[h // 4]
                                    P.op("pe", lambda e, b=b, h=h, s=s: e.matmul(ps[:, b, (h % 4) * 128:(h % 4 + 1) * 128], attb[:, h, :], Vt[:, s, h * 128:(h + 1) * 128], start=True, stop=True),
                                         reads=["attb", "Vt"], writes=[("ps", b)])
                                for hf in range(2):
                                    osl = oacc[:, s, hf * 512:(hf + 1) * 512]
                                    if d == 0:
                                        P.op("act", lambda e, hf=hf, osl=osl: e.copy(osl, ps[:, R1[hf], :]), reads=[("ps", R1[hf])], writes=[("oacc", s)])
                                    else:
                                        P.op("dve", lambda e, hf=hf, osl=osl: e.tensor_tensor(osl, osl, ps[:, R1[hf], :], ALU.add), reads=[("ps", R1[hf]), ("oacc", s)], writes=[("oacc", s)])
                                    P.op("dve", lambda e, hf=hf, osl=osl: e.tensor_tensor(osl, osl, ps[:, R2[hf], :], ALU.add), reads=[("ps", R2[hf]), ("oacc", s)], writes=[("oacc", s)])
                        if sample and not outputs:
                            P.op("dve", lambda e: e.tensor_reduce(Fl, Dd, AX.X, ALU.mult), reads=["Dd"], writes=["Fl"])
                            P.dma("sp", lambda e: e.dma_start(out=hgin[d].ap()[:, 0:1024], in_=Sf[:].rearrange("p h v -> p (h v)")), ("hgin", d), reads=["Sf"], writes=[("hgin", d)])
                            P.dma("sp", lambda e: e.dma_start(out=hgin[d].ap()[:, 1024:1032], in_=Fl), ("hginF", d), reads=["Fl"], writes=[("hgin", d)])
                            P.coll(lambda e: e.collective_compute("AllGather", ALU.bypass, replica_groups=[[0, 1, 2, 3], [4, 5, 6, 7]],
                                                                  ins=[hgin[d].ap().opt()], outs=[hgout[d].ap().opt()]),
                                   ("hgc", d), reads=[("hgin", d)], writes=[("hgout", d)])
                            gv = hgout[d].ap().rearrange("(r p) n -> p r n", p=128)
                            Fg = hgs2[:, 0:32].rearrange("p (r h) -> p r h", r=4)
                            P.dma("sp", lambda e: e.dma_start(out=Fg, in_=gv[:, :, 1024:1032]), ("hgF", d), reads=[("hgout", d)], writes=["Fg"])
                            P.dma("sp", lambda e: e.dma_start(out=Sf[:], in_=dr["hg_s0"][d].rearrange("h k v -> k h v")), ("hgs0", d), writes=["Sf"])
                            mo = K.RMASK + (0 if d == 0 else 8)
                            ranks = [0, 1, 2, 3] if d == 0 else [3, 2, 1, 0]
                            for r in ranks:
                                P.op("dve", lambda e, r=r: e.tensor_scalar(Fm[:, r, :], Fg[:, r, :], cst[:, mo + r:mo + r + 1], cst[:, mo + 4 + r:mo + 5 + r], ALU.mult, ALU.add),
                                     reads=["Fg", "cst"], writes=["Fm"])
                            for r in ranks:
                                sl = r % 2
                                P.dma("sp", lambda e, r=r, sl=sl: e.dma_start(out=stage[:, sl, :], in_=gv[:, r, 0:1024]), ("stage", sl), reads=[("hgout", d)], writes=[("stage", sl)])
                                P.op("dve", lambda e, r=r: e.tensor_tensor(Sf[:], Sf[:], Fm[:, r, :].unsqueeze(2).to_broadcast([128, 8, 128]), ALU.mult), reads=["Sf", "Fm"], writes=["Sf"])
                                P.op("dve", lambda e, r=r, sl=sl: e.scalar_tensor_tensor(Sf[:].rearrange("p h v -> p (h v)"), stage[:, sl, :], cst[:, mo + r:mo + r + 1], Sf[:].rearrange("p h v -> p (h v)"), ALU.mult, ALU.add),
                                     reads=["Sf", ("stage", sl), "cst"], writes=["Sf"])
                            P.op("act", lambda e: e.copy(Sb[:], Sf[:]), reads=["Sf"], writes=["Sb"])
                    if not sample:
                        sidx = (t * 512 + soff) // 256
                        P.dma("sp", lambda e, sidx=sidx: e.dma_start(out=dr["o_hg"][sidx, j, d].rearrange("h k v -> k h v"), in_=Sf[:]), ("ohg", d), reads=["Sf"], final=True)

            for t in [2, 0, 1]:
                tsl = slice(t * 512, (t + 1) * 512)
                sample = (t == 2)
                seqs = [(0, 512)] if sample else [(0, 256), (256, 256)]
                state["banks"] = list(range(8))
                wts = [wload(wi_d[:, :, pc * 512:(pc + 1) * 512], 8, 512) for pc in range(2)]
                for s in range(4):
                    for pc in range(2):
                        wt, wk = wts[pc]
                        b = bank()
                        for kc in range(8):
                            P.op("pe", lambda e, b=b, kc=kc, wt=wt, s=s, t=t: e.matmul(ps[:, b, :], hT[:, kc, t * 512 + s * 128:t * 512 + (s + 1) * 128], wt[:, kc, :], start=(kc == 0), stop=(kc == 7)),
                                 reads=[wk, ("h", t)], writes=[("ps", b)])
                        evac(Vt[:, s, pc * 512:(pc + 1) * 512], ps[:, b, :], [("ps", b)], ["Vt"])
                fm_proj(wq_d, lambda c, pa, pk: P.op("act", lambda e: e.activation(qT[:, c, :], pa, AF.Silu), reads=[pk], writes=["qT"]), tsl, t, "q")
                fm_proj(wg_d, lambda c, pa, pk: P.op("act", lambda e: e.activation(gT[:, c, :], pa, AF.Silu), reads=[pk], writes=["gT"]), tsl, t, "g")
                for d in range(2):
                    def prep(c, pa, pk, d=d):
                        lbc = lb[:, d * 8 + c:d * 8 + c + 1]
                        omc = oml[:, d * 8 + c:d * 8 + c + 1]
                        if c % 2 == 0:
                            tA_, tB_, tC_, tE_ = tA, tB, tC, tE
                            kA, kB, kC, kE = "tA", "tB", "tC", "tE"
                        else:
                            tA_, tB_, tC_, tE_ = stage[:, 0, 0:512], stage[:, 0, 512:1024], stage[:, 1, 0:512], stage[:, 1, 512:1024]
                            kA, kB, kC, kE = ("stage", 0), ("stage", 0), ("stage", 1), ("stage", 1)
                        P.op("act", lambda e: e.activation(tA_, pa, AF.Exp, scale=-1.0), reads=[pk], writes=[kA])
                        P.op("act", lambda e: e.activation(tA_, tA_, AF.Ln, bias=1.0), reads=[kA], writes=[kA])
                        P.op("act", lambda e: e.activation(tA_, tA_, AF.Exp, scale=-1.0), reads=[kA], writes=[kA])
                        P.op("dve", lambda e: e.tensor_scalar(tA_, tA_, omc, lbc, ALU.mult, ALU.add), reads=[kA, "lb"], writes=[kA])
                        P.op("act", lambda e: e.activation(tB_, tA_, AF.Identity, bias=1.0, scale=-1.0), reads=[kA], writes=[kB])
                        P.op("act", lambda e: e.activation(tA_, tA_, AF.Ln), reads=[kA, kB], writes=[kA])
                        for ck in range(8):
                            if d == 0:
                                o_, i_ = tC_[:, ck * 64:(ck + 1) * 64], tA_[:, ck * 64:(ck + 1) * 64]
                            else:
                                lo = ck * 64 - 1 if ck > 0 else None
                                o_, i_ = tC_[:, ck * 64 + 63:lo:-1], tA_[:, ck * 64 + 63:lo:-1]
                            P.op("dve", lambda e, o_=o_, i_=i_: e.tensor_tensor_scan(o_, ones64[:], i_, 0.0, ALU.mult, ALU.add), reads=[kA, "ones64"], writes=[kC])
                        P.op("act", lambda e: e.activation(tE_, tC_, AF.Exp), reads=[kC], writes=[kE])
                        P.op("dve", lambda e: e.tensor_tensor(Qt[:, c, :], qT[:, c, :], tE_, ALU.mult), reads=[kE, "qT"], writes=["Qt"])
                        dcol = tE_[:, 63::64] if d == 0 else tE_[:, 0::64]
                        P.op("dve", lambda e: e.tensor_copy(Dd[:, c, :], dcol), reads=[kE], writes=["Dd"])
                        P.op("act", lambda e: e.activation(tE_, tC_, AF.Exp, scale=-1.0), reads=[kC, kE, "Qt", "Dd"], writes=[kE])
                        P.op("dve", lambda e: e.tensor_tensor(Kt[:, c, :], tB_, tE_, ALU.mult), reads=[kE, kB], writes=["Kt"])

                    fm_proj(wf_d[d], prep, tsl, t, "f")
                    for s in range(4):
                        b = bank()
                        pv = ps[:, b, :].bitcast(BF16)
                        for h in range(8):
                            P.op("pe", lambda e, pv=pv, h=h, s=s: e.transpose(pv[:, h * 128:(h + 1) * 128], Kt[:, h, s * 128:(s + 1) * 128], identb),
                                 reads=["Kt", "cstb"], writes=[("ps", b)])
                        evac(Ktok[:, s, :], pv, [("ps", b)], ["Ktok"])
                    scan_dir(t, d, seqs, sample)
                state["banks"] = list(range(8))
                for s in range(4):
                    P.op("act", lambda e, s=s: e.activation(osq[:], oacc[:, s, :], AF.Square), reads=[("oacc", s)], writes=["osq"])
                    P.op("dve", lambda e: e.tensor_reduce(ssq, osq[:].rearrange("p (h v) -> p h v", h=8), AX.X, ALU.add), reads=["osq"], writes=["ssq"])
                    P.op("act", lambda e: e.activation(rsd, ssq, AF.Sqrt, bias=EPS, scale=1.0 / 128.0), reads=["ssq"], writes=["rsd"])
                    P.op("dve", lambda e: e.reciprocal(rsd, rsd), reads=["rsd"], writes=["rsd"])
                    P.op("dve", lambda e, s=s: e.tensor_tensor(onb[:].rearrange("p (h v) -> p h v", h=8), oacc[:, s, :].rearrange("p (h v) -> p h v", h=8), rsd.unsqueeze(2).to_broadcast([128, 8, 128]), ALU.mult),
                         reads=[("oacc", s), "rsd"], writes=["onb"])
                    b = bank()
                    pv = ps[:, b, :].bitcast(BF16)
                    for h in range(8):
                        P.op("pe", lambda e, pv=pv, h=h: e.transpose(pv[:, h * 128:(h + 1) * 128], onb[:, h * 128:(h + 1) * 128], identb),
                             reads=["onb", "cstb"], writes=[("ps", b)])
                    tc0 = t * 512 + s * 128
                    P.op("dve", lambda e, pv=pv, tc0=tc0, s=s: e.scalar_tensor_tensor(hT[:, :, tc0:tc0 + 128], pv.rearrange("p (a n) -> p a n", a=8), cst[:, K.ONORM:K.ONORM + 1], gT[:, :, s * 128:(s + 1) * 128], ALU.mult, ALU.mult),
                         reads=[("ps", b), "gT", "cst"], writes=[("h", t)])
                P.barrier(chans=[("ohg", 0), ("ohg", 1), ("hgin", 0), ("hgin", 1), ("hginF", 0), ("hginF", 1)])
            resid_proj(dr["hg_w_o"][j], lambda kc, t: hT[:, kc, t * 512:(t + 1) * 512], lambda t: [("h", t)], 16)

        def swa(layer):
            j = layer // 3
            SC = 64 ** -0.5
            P.barrier()
            QsT = carve(0, [128, 16, 512])
            KsT = carve(8192, [128, 4, 512])
            Vs = carve(10240, [128, 4, 256])
            Gt = carve(11264, [128, 4, 1536])
            HK = carve(17408, [128, 2, 4, 128])
            HV = carve(18432, [128, 2, 256])
            KcT = carve(18944, [128, 4, 256])
            Vc = carve(19968, [128, 2, 256])
            bmask = carve(20480, [128, 4, 2, 128])
            agst = carve(21504, [128, 1536])
            B1 = 23040
            QT = carve(B1, [128, 16, 512])
            qraw = carve(B1, [128, 512], F32)
            qt1 = carve(B1 + 1024, [128, 512], F32)
            KT = carve(B1 + 8192, [128, 4, 512])
            Vb = carve(B1 + 10240, [128, 4, 256])
            Pbp = carve(B1 + 11264, [128, 8, 256])
            Pbs = carve(B1 + 11264, [128, 4, 640])
            PTb = carve(B1 + 13824, [128, 20, 128])
            otok = carve(B1 + 16384, [128, 1024])
            st = carve(B1 + 17408, [128, 64], F32)

            wq_d, wk_d, wv_d = kmajor(dr["swa_w_q"][j]), kmajor(dr["swa_w_k"][j]), kmajor(dr["swa_w_v"][j])
            P.dma("pool", lambda e: e.dma_start(out=KcT[0:64, :, :], in_=dr["swa_kctxT"]), "swkc", writes=["KcT"])
            P.dma("pool", lambda e: e.dma_start(out=Vc[:], in_=dr["swa_vctx"].rearrange("(c p) f -> p c f", p=128)), "swvc", writes=["Vc"])
            P.dma("pool", lambda e: e.dma_start(out=bmask[:].rearrange("p a b n -> p (a b n)"), in_=dr["bandmask"]), "swbm", writes=["bmask"])

            def load_qkv_w():
                wq = [wload(wq_d[:, :, pc * 512:(pc + 1) * 512], 8, 512) for pc in range(2)]
                wkv = wload(wk_d, 8, 256)
                wvv = wload(wv_d, 8, 256)
                return wq, wkv, wvv

            def rope_evac(pb_, dst):
                evac(qraw[0:64, :], ps[0:64, pb_, :], [("ps", pb_)], ["qraw"], eng="act")
                b2 = bank()
                P.op("pe", lambda e, b2=b2: e.matmul(ps[0:64, b2, :], cst[0:64, K.PERM:K.PERM + 64], qraw[0:64, :], start=True, stop=True),
                     reads=["qraw", "cst"], writes=[("ps", b2)])
                P.op("dve", lambda e, b2=b2: e.tensor_tensor(qt1[0:64, :], ps[0:64, b2, :], cst[0:64, K.SIN:K.SIN + 512], ALU.mult), reads=[("ps", b2), "cst"], writes=["qt1"])
                P.op("dve", lambda e: e.tensor_tensor(qraw[0:64, :], qraw[0:64, :], cst[0:64, K.COS:K.COS + 512], ALU.mult), reads=["qraw", "cst"], writes=["qraw"])
                P.op("dve", lambda e: e.tensor_tensor(dst, qraw[0:64, :], qt1[0:64, :], ALU.add), reads=["qraw", "qt1"], writes=["ropeout"])

            t = 2
            tsl = slice(1024, 1536)
            state["banks"] = list(range(8))
            wq, (wkt, wkk), (wvt, wvk) = load_qkv_w()
            for hd in range(16):
                wt, wk = wq[hd // 8]
                c0 = (hd % 8) * 64
                b = bank()
                for kc in range(8):
                    P.op("pe", lambda e, b=b, kc=kc, wt=wt, c0=c0, tsl=tsl: e.matmul(ps[0:64, b, :], wt[:, kc, c0:c0 + 64], hT[:, kc, tsl], start=(kc == 0), stop=(kc == 7)),
                         reads=[wk, ("h", 2)], writes=[("ps", b)])
                rope_evac(b, QsT[0:64, hd, :])
            for kvh in range(4):
                b = bank()
                for kc in range(8):
                    P.op("pe", lambda e, b=b, kc=kc, kvh=kvh, tsl=tsl, wkt=wkt: e.matmul(ps[0:64, b, :], wkt[:, kc, kvh * 64:(kvh + 1) * 64], hT[:, kc, tsl], start=(kc == 0), stop=(kc == 7)),
                         reads=[wkk, ("h", 2)], writes=[("ps", b)])
                rope_evac(b, KsT[0:64, kvh, :])
            for s in range(4):
                b = bank()
                for kc in range(8):
                    P.op("pe", lambda e, b=b, kc=kc, s=s, wvt=wvt: e.matmul(ps[:, b, 0:256], hT[:, kc, 1024 + s * 128:1024 + (s + 1) * 128], wvt[:, kc, :], start=(kc == 0), stop=(kc == 7)),
                         reads=[wvk, ("h", 2)], writes=[("ps", b)])
                evac(Vs[:, s, :], ps[:, b, 0:256], [("ps", b)], ["Vs"])
            P.op("dve", lambda e: e.memset(agst[:], 0.0), writes=["agst"])
            P.op("dve", lambda e: e.tensor_copy(agst[0:64, 0:512].rearrange("p (a n) -> p a n", a=4), KsT[0:64, :, 0:128]), reads=["ropeout"], writes=["agst"])
            P.op("dve", lambda e: e.tensor_copy(agst[0:64, 512:1024].rearrange("p (a n) -> p a n", a=4), KsT[0:64, :, 384:512]), reads=["ropeout"], writes=["agst"])
            P.op("dve", lambda e: e.tensor_copy(agst[:, 1024:1280], Vs[:, 0, :]), reads=["Vs"], writes=["agst"])
            P.op("dve", lambda e: e.tensor_copy(agst[:, 1280:1536], Vs[:, 3, :]), reads=["Vs"], writes=["agst"])
            P.dma("sp", lambda e: e.dma_start(out=swin.ap(), in_=agst[:]), "swin", reads=["agst"], writes=["swin"])
            P.coll(lambda e: e.collective_compute("AllGather", ALU.bypass, replica_groups=[[0, 1, 2, 3], [4, 5, 6, 7]],
                                                  ins=[swin.ap().opt()], outs=[swout.ap().opt()]),
                   "swc", reads=["swin"], writes=["swout"])
            P.dma("sp", lambda e: e.dma_start(out=Gt[:], in_=swout.ap().rearrange("(r p) n -> p r n", p=128)), "swg", reads=["swout"], writes=["Gt"])
            P.barrier()

            cfgp = {"S": [0, 1, 2, 3], "O": (4, 0), "PT": [5]}
            for t in (range(2) if flags.get("swa_parts", 7) & 2 else []):
                tsl = slice(t * 512, (t + 1) * 512)
                state["banks"] = [6, 7]
                wq, (wkt, wkk), (wvt, wvk) = load_qkv_w()
                for hd in range(16):
                    wt, wk = wq[hd // 8]
                    c0 = (hd % 8) * 64
                    b = bank()
                    for kc in range(8):
                        P.op("pe", lambda e, b=b, kc=kc, wt=wt, c0=c0, tsl=tsl: e.matmul(ps[0:64, b, :], wt[:, kc, c0:c0 + 64], hT[:, kc, tsl], start=(kc == 0), stop=(kc == 7)),
                             reads=[wk, ("h", t)], writes=[("ps", b)])
                    evac(QT[0:64, hd, :], ps[0:64, b, :], [("ps", b)], ["QT"])
                for kvh in range(4):
                    b = bank()
                    for kc in range(8):
                        P.op("pe", lambda e, b=b, kc=kc, kvh=kvh, tsl=tsl, wkt=wkt: e.matmul(ps[0:64, b, :], wkt[:, kc, kvh * 64:(kvh + 1) * 64], hT[:, kc, tsl], start=(kc == 0), stop=(kc == 7)),
                             reads=[wkk, ("h", t)], writes=[("ps", b)])
                    evac(KT[0:64, kvh, :], ps[0:64, b, :], [("ps", b)], ["KT"])
                for s in (range(4) if not flags.get("nokvtok") else []):
                    b = bank()
                    b2 = bank()
                    tcs = slice(t * 512 + s * 128, t * 512 + (s + 1) * 128)
                    for kc in range(8):
                        P.op("pe", lambda e, b=b, kc=kc, tcs=tcs, wkt=wkt: e.matmul(ps[:, b, 0:256], hT[:, kc, tcs], wkt[:, kc, :], start=(kc == 0), stop=(kc == 7)),
                             reads=[wkk, ("h", t)], writes=[("ps", b)])
                    for kc in range(8):
                        P.op("pe", lambda e, b2=b2, kc=kc, tcs=tcs, wvt=wvt: e.matmul(ps[:, b2, 0:256], hT[:, kc, tcs], wvt[:, kc, :], start=(kc == 0), stop=(kc == 7)),
                             reads=[wvk, ("h", t)], writes=[("ps", b2)])
                    sl = s % 2
                    kvl = flags.get("kvlevel", 3)
                    if kvl >= 2:
                        evac(stage[:, sl, 0:256], ps[:, b, 0:256], [("ps", b)], [("stage", sl)], eng="act")
                        evac(stage[:, sl, 256:512], ps[:, b2, 0:256], [("ps", b2)], [("stage", sl)], eng="act")
                    if kvl >= 3:
                        evac(Vb[:, s, :], stage[:, sl, 256:512], [("stage", sl)], ["Vb"], eng="dve")
                    gtok = t * 512 + s * 128
                    sq_, r0 = gtok // 256, gtok % 256
                    if not flags.get("nokvout"):
                        P.dma("sp", lambda e, sl=sl, sq_=sq_, r0=r0: e.dma_start(out=dr["o_k"][sq_, j, r0:r0 + 128, :], in_=stage[:, sl, 0:256]),
                              ("ost", sl), reads=[("stage", sl)], final=True)
                        P.dma("sp", lambda e, sl=sl, sq_=sq_, r0=r0: e.dma_start(out=dr["o_v"][sq_, j, r0:r0 + 128, :], in_=stage[:, sl, 256:512]),
                              ("ost2", sl), reads=[("stage", sl)], final=True)
                for s_ in (range(2) if not flags.get("noattn") else []):
                    for qt in range(2):
                        q0 = s_ * 256 + qt * 128
                        for hg_ in range(2):
                            def score_ops(g, off, n, q0=q0, s_=s_, hg_=hg_):
                                hd = hg_ * 8 + g
                                return [(QT[0:64, hd, q0:q0 + 128], KT[0:64, hd // 4, s_ * 256 + off:s_ * 256 + off + n], ["QT", "KT"])]

                            def v_ops(g, kc, nk, s_=s_, hg_=hg_):
                                hd = hg_ * 8 + g
                                return Vb[0:nk, s_ * 2 + kc, (hd // 4) * 64:(hd // 4 + 1) * 64], ["Vb"]

                            p2_ = attn_core(8, 256, 256, [(0, 256)], score_ops, v_ops, 64, SC,
                                      lambda g, hg_=hg_: otok[:, (hg_ * 8 + g) * 64:(hg_ * 8 + g + 1) * 64], ["otok"], cfgp,
                                      lambda g: Pbp[:, g, :], lambda i0, n_: PTb[:, i0:i0 + n_, :], st,
                                      sinkv=(cst[:, K.SINK + hg_ * 8:K.SINK + hg_ * 8 + 8] if not flags.get("nosink") else None))
                            p2_()
                        otok_to_oT(otok, ["otok"], t * 512 + q0, cfgp)
            P.barrier()

            so = K.SEL
            for side, (koff, voff) in enumerate(((512, 1280), (0, 1024))):
                for r in range(4):
                    sc_ = cst[:, so + side * 4 + r:so + side * 4 + r + 1]
                    kin = Gt[0:64, r, koff:koff + 512].rearrange("p (a n) -> p a n", a=4)
                    vin = Gt[:, r, voff:voff + 256]
                    if r == 0:
                        P.op("dve", lambda e, sc_=sc_, kin=kin, side=side: e.tensor_scalar(HK[0:64, side, :, :], kin, sc_[0:64, :], None, ALU.mult), reads=["Gt", "cst"], writes=["HK"])
                        P.op("dve", lambda e, sc_=sc_, vin=vin, side=side: e.tensor_scalar(HV[:, side, :], vin, sc_, None, ALU.mult), reads=["Gt", "cst"], writes=["HV"])
                    else:
                        P.op("dve", lambda e, sc_=sc_, kin=kin, side=side: e.scalar_tensor_tensor(HK[0:64, side, :, :], kin, sc_[0:64, :], HK[0:64, side, :, :], ALU.mult, ALU.add), reads=["Gt", "cst", "HK"], writes=["HK"])
                        P.op("dve", lambda e, sc_=sc_, vin=vin, side=side: e.scalar_tensor_tensor(HV[:, side, :], vin, sc_, HV[:, side, :], ALU.mult, ALU.add), reads=["Gt", "cst", "HV"], writes=["HV"])
            cfgs = {"S": [0, 1, 2, 3, 4, 5], "O": (6, 0), "PT": [7]}
            hoffs = [0, 640, 1280, 2048]
            blocks = [(0, 256), (256, 128), (384, 128), (512, 128)]
            for jj in (range(4) if flags.get("swa_parts", 7) & 4 else []):
                q0 = jj * 128
                for kvh in range(4):
                    def kband(which, jj=jj, kvh=kvh):
                        bi = jj - 1 + which
                        if bi < 0:
                            return HK[0:64, 0, kvh, :], HV[:, 0, kvh * 64:(kvh + 1) * 64], ["HK", "HV"]
                        if bi > 3:
                            return HK[0:64, 1, kvh, :], HV[:, 1, kvh * 64:(kvh + 1) * 64], ["HK", "HV"]
                        return KsT[0:64, kvh, bi * 128:(bi + 1) * 128], Vs[:, bi, kvh * 64:(kvh + 1) * 64], ["ropeout", "Vs"]

                    def score_ops(g, off, n, jj=jj, kvh=kvh, q0=q0):
                        hd = kvh * 4 + g
                        qa = QsT[0:64, hd, q0:q0 + 128]
                        if off == 0:
                            return [(qa, KcT[0:64, kvh, :], ["ropeout", "KcT"])]
                        which = (off - 256) // 128
                        ka, _, rk = kband(which)
                        ops_ = [(qa, ka, ["ropeout"] + rk)]
                        if which != 1:
                            ops_.append((identb, bmask[:, jj, 0 if which == 0 else 1, :], ["cstb", "bmask"]))
                        return ops_

                    def v_ops(g, kc, nk, kvh=kvh):
                        if kc < 2:
                            return Vc[:, kc, kvh * 64:(kvh + 1) * 64], ["Vc"]
                        _, va, rk = kband(kc - 2)
                        return va, rk

                    p2_ = attn_core(4, 640, hoffs, blocks, score_ops, v_ops, 64, SC,
                              lambda g, kvh=kvh: otok[:, (kvh * 4 + g) * 64:(kvh * 4 + g + 1) * 64], ["otok"], cfgs,
                              lambda g: Pbs[:, g, :], lambda i0, n_: PTb[:, i0:i0 + n_, :], st,
                              sinkv=cst[:, K.SINK + kvh * 4:K.SINK + kvh * 4 + 4])
                    p2_()
                otok_to_oT(otok, ["otok"], 1024 + q0, cfgs)
            state["banks"] = list(range(8))
            P.barrier(chans=["swin", "swkc", "swvc", "swbm", "swg"])
            resid_proj(dr["swa_w_o"][j], lambda kc, t: hT[:, kc, t * 512:(t + 1) * 512], lambda t: [("h", t)], 16)

        def final():
            for t in range(NT):
                yT = ytmp
                norm_tile(t,
                          lambda c: cst[:, K.NFIN + c:K.NFIN + c + 1],
                          lambda c: 0.0,
                          lambda c: yT[:, c, :],
                          lambda c: [("ytmp", c)])
                for q in range(4):
                    tt = t * 4 + q
                    sl = tt % 2
                    for half in range(2):
                        b = bank()
                        for j in range(4):
                            c = half * 4 + j
                            P.op("pe", lambda e, b=b, j=j, c=c, q=q: e.transpose(ps[:, b, j * 128:(j + 1) * 128], yT[:, c, q * 128:(q + 1) * 128], ident),
                                 reads=[("ytmp", c), "cst"], writes=[("ps", b)])
                        evac(stage[:, sl, half * 512:(half + 1) * 512], ps[:, b, :], [("ps", b)], [("stage", sl)])
                    P.dma("sp", lambda e, tt=tt, sl=sl: e.dma_start(out=dr["y"][tt * 128:(tt + 1) * 128, :], in_=stage[:, sl, :]),
                          ("ost", sl), reads=[("stage", sl)], final=True)

        for layer in range(depth):
            if layer == 0:
                for pc in range(12):
                    adaln_piece(0, pc)
            adaln_finish(layer)
            if mixers:
                kind = layer % 3
                if kind == 0:
                    modnorm(0)
                    mla(layer)
                elif kind == 1:
                    modnorm(0)
                    hgrn(layer)
                else:
                    modnorm(0)
                    swa(layer)
            modnorm(1)
            ffn(layer, ada_next=(layer + 1 if layer + 1 < depth else None))
        final()
        import os as _os
        if _os.environ.get("KDEBUG"):
            print("OPS", {e: len(v) for e, v in P.ops.items()}, "waits", {e: sum(len(w) for w, _, _ in v) for e, v in P.ops.items()}, "nsem", P.nsem)
        P.emit()
    return nc


def fm(v):
    v = np.asarray(v, np.float32)
    lead = v.shape[:-1]
    a = v.reshape(*lead, v.shape[-1] // 128, 128)
    a = np.moveaxis(a, -1, 0)
    return np.ascontiguousarray(a).reshape(128, -1)


def rope_tables(pos0):
    pos = np.arange(pos0, pos0 + NS_TOK)
    row = (pos // 64).astype(np.float32)
    col = (pos % 64).astype(np.float32)
    inv = (10000.0 ** (-np.arange(16, dtype=np.float32) / 16)).astype(np.float32)
    cos = np.zeros((64, NS_TOK), np.float32)
    sin = np.zeros((64, NS_TOK), np.float32)
    perm = np.zeros((64, 64), np.float32)
    for i in range(64):
        p = row if i < 32 else col
        ii = i % 32
        f = ii % 16
        first = ii < 16
        ang = (p * inv[f]).astype(np.float32)
        cos[i] = np.cos(ang)
        sin[i] = -np.sin(ang) if first else np.sin(ang)
        sw = i + 16 if first else i - 16
        perm[sw, i] = 1.0
    return cos, sin, perm


def make_consts(inp, core):
    g, q = core // 4, core % 4
    c = np.zeros((128, NCONST), np.float32)
    c[:, K.IDENT:K.IDENT + 128] = np.eye(128, dtype=np.float32)
    c[:, K.ONESN:K.ONESN + 128] = 1.0 / 1024.0
    cv = np.stack([inp["c_ctx"], inp["c"][g]], 0)
    c[:, K.CVEC:K.CVEC + 16] = np.ascontiguousarray(cv.reshape(2, 8, 128).transpose(2, 1, 0)).reshape(128, 16)
    ab = inp["ada_b"].reshape(DEPTH, 48, 128).transpose(2, 0, 1).reshape(128, DEPTH * 48)
    c[:, K.ADAB:K.ADAB + DEPTH * 48] = ab
    c[:, K.NMIX:K.NMIX + 32] = fm(inp["norm_mix"])
    c[:, K.NFFN:K.NFFN + 32] = fm(inp["norm_ffn"])
    c[:, K.NFIN:K.NFIN + 8] = fm(inp["final_norm"])
    c[:, K.QNORM:K.QNORM + 8] = fm(inp["mla_q_norm"])
    c[:, K.KVNORM:K.KVNORM + 4] = fm(inp["mla_kv_norm"])
    cos, sin, perm = rope_tables(q * NS_TOK)
    c[0:64, K.PERM:K.PERM + 64] = perm
    c[0:64, K.COS:K.COS + 512] = cos
    c[0:64, K.SIN:K.SIN + 512] = sin
    s_ = np.arange(128)[:, None]
    t_ = np.arange(128)[None, :]
    same = (s_ // 64) == (t_ // 64)
    c[:, K.MASKF:K.MASKF + 128] = (same & (s_ <= t_)).astype(np.float32)
    c[:, K.MASKB:K.MASKB + 128] = (same & (s_ >= t_)).astype(np.float32)
    lbl = inp["hg_lb_logits"].reshape(2, DEPTH, 8, 128).transpose(3, 0, 2, 1)
    c[:, K.LBL:K.LBL + 64] = np.ascontiguousarray(lbl).reshape(128, 64)
    c[:, K.ONORM] = inp["hg_o_norm"][0]
    r = np.arange(4)
    c[:, K.RMASK + 0:K.RMASK + 4] = (r < q).astype(np.float32)[None, :]
    c[:, K.RMASK + 4:K.RMASK + 8] = 1.0 - (r < q).astype(np.float32)[None, :]
    c[:, K.RMASK + 8:K.RMASK + 12] = (r > q).astype(np.float32)[None, :]
    c[:, K.RMASK + 12:K.RMASK + 16] = 1.0 - (r > q).astype(np.float32)[None, :]
    c[:, K.SINK:K.SINK + 16] = inp["swa_sink"][0][None, :]
    c[:, K.SEL + 0:K.SEL + 4] = (r == q - 1).astype(np.float32)[None, :]
    c[:, K.SEL + 4:K.SEL + 8] = (r == q + 1).astype(np.float32)[None, :]
    return c


def make_bandmask(core):
    q = core % 4
    qq = np.arange(128)[:, None]
    kk = np.arange(128)[None, :]
    m = np.zeros((128, 4, 2, 128), np.float32)
    for jj in range(4):
        bq = 4 * q + jj
        prev_ok = (kk >= qq) & (bq >= 1)
        next_ok = (kk <= qq) & (bq <= 14)
        m[:, jj, 0, :] = np.where(prev_ok, 0.0, -1e30)
        m[:, jj, 1, :] = np.where(next_ok, 0.0, -1e30)
    return m.reshape(128, 1024)


_CACHE = {}


def run(inputs, flags):
    inp = {k: np.asarray(v) for k, v in inputs.items()}
    key = tuple(sorted(flags.items()))
    if key not in _CACHE:
        _CACHE[key] = build(flags)
    nc = _CACHE[key]
    in_maps = []
    for core in range(NCORES):
        g, q = core // 4, core % 4
        xin = np.concatenate([inp["x_prompt"][4 * core:4 * core + 4].reshape(NP_TOK, D),
                              inp["x_sample"][g, q * NS_TOK:(q + 1) * NS_TOK]], 0)
        m = {"xin": np.ascontiguousarray(xin, dtype=np.float32), "consts": make_consts(inp, core)}
        ck = inp["cache_mla_ckv"][g]
        m["ckv_ctxT"] = np.ascontiguousarray(ck.reshape(2, 256, 2, 128).transpose(0, 3, 2, 1), dtype=np.float32)
        m["krope_ctxT"] = np.ascontiguousarray(inp["cache_mla_krope"][g].transpose(0, 2, 1), dtype=np.float32)
        m["hg_s0"] = np.ascontiguousarray(inp["state_hgrn"][g, 0], dtype=np.float32)
        m["swa_kctxT"] = np.ascontiguousarray(inp["cache_swa_k"][g, 0].transpose(2, 1, 0), dtype=np.float32)
        m["swa_vctx"] = np.ascontiguousarray(inp["cache_swa_v"][g, 0].reshape(256, 256), dtype=np.float32)
        m["bandmask"] = make_bandmask(core)
        for k in W_SPECS:
            m[k] = np.ascontiguousarray(inp[k], dtype=np.float32)
        in_maps.append(m)
    res = run_bass_kernel_spmd(nc, in_maps, core_ids=list(range(NCORES)))
    return res.results


def assemble(res):
    y_p = np.zeros((32, 256, D), np.float32)
    y_s = np.zeros((2, 2048, D), np.float32)
    ckv = np.zeros((32, 2, 256, 256), np.float32)
    krope = np.zeros((32, 2, 256, 64), np.float32)
    hg = np.zeros((32, 1, 2, 8, 128, 128), np.float32)
    ok = np.zeros((32, 1, 256, 4, 64), np.float32)
    ov = np.zeros((32, 1, 256, 4, 64), np.float32)
    for core in range(NCORES):
        g, q = core // 4, core % 4
        y = res[core]["y"]
        y_p[4 * core:4 * core + 4] = y[:NP_TOK].reshape(4, 256, D)
        y_s[g, q * NS_TOK:(q + 1) * NS_TOK] = y[NP_TOK:]
        ckv[4 * core:4 * core + 4] = res[core]["o_ckv"]
        krope[4 * core:4 * core + 4] = res[core]["o_krope"]
        hg[4 * core:4 * core + 4] = res[core]["o_hg"]
        ok[4 * core:4 * core + 4] = res[core]["o_k"].reshape(4, 1, 256, 4, 64)
        ov[4 * core:4 * core + 4] = res[core]["o_v"].reshape(4, 1, 256, 4, 64)
    return y_p, y_s, ckv, krope, hg, ok, ov


def kernel(**inputs):
    res = run(inputs, {})
    return assemble(res)
```
